# Optimizing a Trainium2 kernel written in Bass

```python
import math
import jax
import jax.numpy as jnp
from jax import lax
import numpy as np

D_MODEL = 1024
BATCH = 4
SEQ = 4096
DEPTH = 4
DEC_BATCH = 128
DEC_SEQ = 1
PAST_LEN = 8192
PAGE_SIZE = 128

HEAD_DIM = 64
N_HEADS = 8
N_KV_HEADS = 2
Q_PER_KV = N_HEADS // N_KV_HEADS
ATTN_WIDTH = N_HEADS * HEAD_DIM
KV_WIDTH = N_KV_HEADS * HEAD_DIM
WINDOW = 128
BLOCK = 128
SSM_WIDTH = D_MODEL // 2
SSM_GROUP = 16
SSM_GROUPS = SSM_WIDTH // SSM_GROUP
SSM_STATE = 64
DT_MIN = 1e-3
DT_MAX = 1e-1
N_MEM = 256
X_HEADS = 4
X_HEAD_DIM = 128
X_WIDTH = X_HEADS * X_HEAD_DIM
N_BRANCH = 3
IN_SPLITS = (ATTN_WIDTH, KV_WIDTH, KV_WIDTH, SSM_WIDTH, X_WIDTH, N_BRANCH * D_MODEL)
IN_COLS = ATTN_WIDTH + 2 * KV_WIDTH + SSM_WIDTH + X_WIDTH + N_BRANCH * D_MODEL
D_FF = 2816
CONV_W = 3
EPS = 1e-6

kernel_name = 'hybrid_swa_s5_memxattn_convffn_step'


def rmsnorm(x, g):
    xf = x.astype(jnp.float32)
    y = xf * lax.rsqrt(jnp.mean(xf * xf, axis=-1, keepdims=True) + EPS)
    return (y * g.astype(jnp.float32)).astype(x.dtype)


def alibi_slopes():
    return jnp.exp2(-8.0 * (jnp.arange(N_HEADS, dtype=jnp.float32) + 1.0) / N_HEADS)


def split_cols(p):
    idx, acc = [], 0
    for s in IN_SPLITS[:-1]:
        acc += s
        idx.append(acc)
    return jnp.split(p, idx, axis=-1)


def sink_attention(q, k, v, dist, valid, sinks):
    slopes = alibi_slopes().reshape(N_KV_HEADS, Q_PER_KV, 1, 1)
    s = jnp.einsum('...qkgd,...skd->...kgqs', q, k).astype(jnp.float32) * (HEAD_DIM ** -0.5)
    s = jnp.where(valid, s - slopes * dist, -jnp.inf)
    sink = jnp.broadcast_to(sinks.astype(jnp.float32).reshape(N_KV_HEADS, Q_PER_KV, 1, 1), s.shape[:-1] + (1,))
    p = jax.nn.softmax(jnp.concatenate([s, sink], axis=-1), axis=-1)[..., :-1]
    return jnp.einsum('...kgqs,...skd->...qkgd', p.astype(v.dtype), v)


def swa_prompt(q, k, v, sinks):
    B, L = q.shape[:2]
    nb = L // BLOCK
    qb = q.reshape(B, nb, BLOCK, N_KV_HEADS, Q_PER_KV, HEAD_DIM)

    def band(t):
        tb = t.reshape(B, nb, BLOCK, N_KV_HEADS, HEAD_DIM)
        tp = jnp.pad(tb, ((0, 0), (1, 0), (0, 0), (0, 0), (0, 0)))
        return jnp.concatenate([tp[:, :-1], tp[:, 1:]], axis=2)

    qi = jnp.arange(BLOCK)[:, None]
    kj = jnp.arange(2 * BLOCK)[None, :]
    dist = qi + BLOCK - kj
    in_win = (dist >= 0) & (dist <= WINDOW)
    real_key = (jnp.arange(nb)[:, None, None] > 0) | (kj >= BLOCK)[None]
    valid = (in_win[None] & real_key)[:, None, None]
    o = sink_attention(qb, band(k), band(v), dist.astype(jnp.float32), valid, sinks)
    return o.reshape(B, L, ATTN_WIDTH)


def swa_sample(q, k, v, k_buf, v_buf, sinks):
    B, T = q.shape[:2]
    kc = jnp.concatenate([k_buf.astype(k.dtype), k], axis=1)
    vc = jnp.concatenate([v_buf.astype(v.dtype), v], axis=1)
    nbuf = k_buf.shape[1]
    dist = (nbuf + jnp.arange(T)[:, None]) - jnp.arange(nbuf + T)[None, :]
    valid = (dist >= 0) & (dist <= WINDOW)
    o = sink_attention(q.reshape(B, T, N_KV_HEADS, Q_PER_KV, HEAD_DIM), kc, vc,
                       dist.astype(jnp.float32), valid, sinks)
    return o.reshape(B, T, ATTN_WIDTH), kc[:, -WINDOW:], vc[:, -WINDOW:]


def complex_affine_combine(e1, e2):
    a1r, a1i, b1r, b1i = e1
    a2r, a2i, b2r, b2i = e2
    return (a2r * a1r - a2i * a1i, a2r * a1i + a2i * a1r,
            a2r * b1r - a2i * b1i + b2r, a2r * b1i + a2i * b1r + b2i)


def s5_branch(u, prev, lp):
    B, T, _ = u.shape
    f32 = jnp.float32
    uf = u.astype(f32)
    ug = uf.reshape(B, T, SSM_GROUPS, SSM_GROUP)
    lr, li = lp['lam_re'].astype(f32), lp['lam_im'].astype(f32)
    dt = jnp.exp(lp['log_dt'].astype(f32))[:, None]
    mag = jnp.exp(lr * dt)
    ar, ai = mag * jnp.cos(li * dt), mag * jnp.sin(li * dt)
    den = lr * lr + li * li
    fr = ((ar - 1.0) * lr + ai * li) / den
    fi = (ai * lr - (ar - 1.0) * li) / den
    br, bi = lp['b_re'].astype(f32), lp['b_im'].astype(f32)
    bbr = fr[..., None] * br - fi[..., None] * bi
    bbi = fr[..., None] * bi + fi[..., None] * br
    xr = jnp.einsum('btgk,gpk->btgp', ug, bbr)
    xi = jnp.einsum('btgk,gpk->btgp', ug, bbi)
    if prev is not None:
        pr, pim = prev[0].astype(f32), prev[1].astype(f32)
        xr = xr.at[:, 0].add(ar * pr - ai * pim)
        xi = xi.at[:, 0].add(ar * pim + ai * pr)
    shp = xr.shape
    elems = (jnp.broadcast_to(ar, shp), jnp.broadcast_to(ai, shp), xr, xi)
    _, _, hr, hi = lax.associative_scan(complex_affine_combine, elems, axis=1)
    y = (jnp.einsum('btgp,gkp->btgk', hr, lp['c_re'].astype(f32))
         - jnp.einsum('btgp,gkp->btgk', hi, lp['c_im'].astype(f32)))
    y = y.reshape(B, T, SSM_WIDTH) + lp['d_skip'].astype(f32) * uf
    z = jax.nn.gelu(y).astype(u.dtype)
    out = (z @ lp['w_glu_a']) * jax.nn.sigmoid(z @ lp['w_glu_b'])
    return out, hr[:, -1], hi[:, -1]


def memory_kv(mem, g, w):
    B, M, _ = mem.shape
    mk, mv = jnp.split(rmsnorm(mem, g) @ w, 2, axis=-1)
    return mk.reshape(B, M, X_HEADS, X_HEAD_DIM), mv.reshape(B, M, X_HEADS, X_HEAD_DIM)


def memory_attention(q, mk, mv):
    s = jnp.einsum('bthd,bmhd->bhtm', q, mk.astype(q.dtype)).astype(jnp.float32) * (X_HEAD_DIM ** -0.5)
    p = jax.nn.softmax(s, axis=-1)
    return jnp.einsum('bhtm,bmhd->bthd', p.astype(mv.dtype), mv)


def mixer_sublayer(h, lp, mem_k, mem_v, swa_buf, ssm_prev):
    B, T, _ = h.shape
    q, k, v, u, xq, gates = split_cols(h @ lp['w_in'])
    q = q.reshape(B, T, N_HEADS, HEAD_DIM)
    k = k.reshape(B, T, N_KV_HEADS, HEAD_DIM)
    v = v.reshape(B, T, N_KV_HEADS, HEAD_DIM)
    if swa_buf is None:
        o_a = swa_prompt(q, k, v, lp['sinks'])
        k_new, v_new = k[:, -WINDOW:], v[:, -WINDOW:]
    else:
        o_a, k_new, v_new = swa_sample(q, k, v, swa_buf[0], swa_buf[1], lp['sinks'])
    o_b, s_re, s_im = s5_branch(u, ssm_prev, lp)
    o_c = memory_attention(xq.reshape(B, T, X_HEADS, X_HEAD_DIM), mem_k, mem_v).reshape(B, T, X_WIDTH)
    g = jax.nn.sigmoid(gates.astype(jnp.float32)).astype(h.dtype).reshape(B, T, N_BRANCH, D_MODEL)
    merged = (g[:, :, 0] * (o_a @ lp['w_br_attn']) + g[:, :, 1] * o_b
              + g[:, :, 2] * (o_c @ lp['w_br_mem']))
    return merged @ lp['w_out'], k_new, v_new, s_re, s_im


def conv_ffn(h, lp, prev):
    B, T, _ = h.shape
    a = h @ lp['w_ffn_gate']
    up = h @ lp['w_ffn_up']
    if prev is None:
        prev = jnp.zeros((B, CONV_W - 1, D_FF), a.dtype)
    ext = jnp.concatenate([prev.astype(a.dtype), a], axis=1)
    w = lp['conv_w']
    c = lp['conv_b'] + sum(w[j] * ext[:, j:j + T] for j in range(CONV_W))
    y = (jax.nn.gelu(c) * up) @ lp['w_ffn_down']
    return y, ext[:, T:]


def setup_inputs(seed: int = 0) -> dict:
    key = jax.random.key(seed)
    ks = iter(jax.random.split(key, 48))

    def nrm(shape, scale):
        return scale * jax.random.normal(next(ks), shape, jnp.float32)

    def gain(shape):
        return 1.0 + nrm(shape, 0.05)

    lam_im = (jnp.pi * jnp.arange(SSM_STATE, dtype=jnp.float32))[None, None, :] + nrm((DEPTH, SSM_GROUPS, SSM_STATE), 0.01)
    return {
        'x_prompt': nrm((BATCH, SEQ, D_MODEL), 1.0),
        'x_sample': nrm((DEC_BATCH, DEC_SEQ, D_MODEL), 1.0),
        'mem_prompt': nrm((BATCH, N_MEM, D_MODEL), 1.0),
        'cache_swa_k': nrm((DEPTH, DEC_BATCH, WINDOW, N_KV_HEADS, HEAD_DIM), 1.0),
        'cache_swa_v': nrm((DEPTH, DEC_BATCH, WINDOW, N_KV_HEADS, HEAD_DIM), 1.0),
        'state_ssm_re': nrm((DEPTH, DEC_BATCH, SSM_GROUPS, SSM_STATE), 0.1),
        'state_ssm_im': nrm((DEPTH, DEC_BATCH, SSM_GROUPS, SSM_STATE), 0.1),
        'cache_ffn_conv': nrm((DEPTH, DEC_BATCH, CONV_W - 1, D_FF), 1.0),
        'cache_mem_k': nrm((DEPTH, DEC_BATCH, N_MEM, X_HEADS, X_HEAD_DIM), 1.0),
        'cache_mem_v': nrm((DEPTH, DEC_BATCH, N_MEM, X_HEADS, X_HEAD_DIM), 1.0),
        'norm_mix_g': gain((DEPTH, D_MODEL)),
        'norm_ffn_g': gain((DEPTH, D_MODEL)),
        'norm_mem_g': gain((DEPTH, D_MODEL)),
        'norm_final_g': gain((D_MODEL,)),
        'w_in': nrm((DEPTH, D_MODEL, IN_COLS), D_MODEL ** -0.5),
        'sinks': nrm((DEPTH, N_HEADS), 1.0),
        'w_br_attn': nrm((DEPTH, ATTN_WIDTH, D_MODEL), ATTN_WIDTH ** -0.5),
        'lam_re': -0.5 + nrm((DEPTH, SSM_GROUPS, SSM_STATE), 0.01),
        'lam_im': lam_im,
        'log_dt': jax.random.uniform(next(ks), (DEPTH, SSM_GROUPS), jnp.float32,
                                     minval=math.log(DT_MIN), maxval=math.log(DT_MAX)),
        'b_re': nrm((DEPTH, SSM_GROUPS, SSM_STATE, SSM_GROUP), (2 * SSM_GROUP) ** -0.5),
        'b_im': nrm((DEPTH, SSM_GROUPS, SSM_STATE, SSM_GROUP), (2 * SSM_GROUP) ** -0.5),
        'c_re': nrm((DEPTH, SSM_GROUPS, SSM_GROUP, SSM_STATE), 2.0 * SSM_STATE ** -0.5),
        'c_im': nrm((DEPTH, SSM_GROUPS, SSM_GROUP, SSM_STATE), 2.0 * SSM_STATE ** -0.5),
        'd_skip': nrm((DEPTH, SSM_WIDTH), 1.0),
        'w_glu_a': nrm((DEPTH, SSM_WIDTH, D_MODEL), SSM_WIDTH ** -0.5),
        'w_glu_b': nrm((DEPTH, SSM_WIDTH, D_MODEL), SSM_WIDTH ** -0.5),
        'w_mem_kv': nrm((DEPTH, D_MODEL, 2 * X_WIDTH), D_MODEL ** -0.5),
        'w_br_mem': nrm((DEPTH, X_WIDTH, D_MODEL), X_WIDTH ** -0.5),
        'w_out': nrm((DEPTH, D_MODEL, D_MODEL), D_MODEL ** -0.5),
        'w_ffn_gate': nrm((DEPTH, D_MODEL, D_FF), D_MODEL ** -0.5),
        'w_ffn_up': nrm((DEPTH, D_MODEL, D_FF), D_MODEL ** -0.5),
        'conv_w': nrm((DEPTH, CONV_W, D_FF), CONV_W ** -0.5),
        'conv_b': nrm((DEPTH, D_FF), 0.02),
        'w_ffn_down': nrm((DEPTH, D_FF, D_MODEL), D_FF ** -0.5),
    }


def reference(x_prompt, x_sample, mem_prompt, cache_swa_k, cache_swa_v, state_ssm_re, state_ssm_im,
              cache_ffn_conv, cache_mem_k, cache_mem_v, norm_mix_g, norm_ffn_g, norm_mem_g, norm_final_g,
              w_in, sinks, w_br_attn, lam_re, lam_im, log_dt, b_re, b_im, c_re, c_im, d_skip, w_glu_a, w_glu_b,
              w_mem_kv, w_br_mem, w_out, w_ffn_gate, w_ffn_up, conv_w, conv_b, w_ffn_down):
    xp, xs = x_prompt, x_sample
    pk, pv, psr, psi, pcv, pmk, pmv = [], [], [], [], [], [], []
    sk, sv, ssr, ssi, scv = [], [], [], [], []
    for l in range(DEPTH):
        lp = {'w_in': w_in[l], 'sinks': sinks[l], 'w_br_attn': w_br_attn[l], 'lam_re': lam_re[l],
              'lam_im': lam_im[l], 'log_dt': log_dt[l], 'b_re': b_re[l], 'b_im': b_im[l], 'c_re': c_re[l],
              'c_im': c_im[l], 'd_skip': d_skip[l], 'w_glu_a': w_glu_a[l], 'w_glu_b': w_glu_b[l],
              'w_br_mem': w_br_mem[l], 'w_out': w_out[l], 'w_ffn_gate': w_ffn_gate[l],
              'w_ffn_up': w_ffn_up[l], 'conv_w': conv_w[l], 'conv_b': conv_b[l], 'w_ffn_down': w_ffn_down[l]}
        mk, mv = memory_kv(mem_prompt, norm_mem_g[l], w_mem_kv[l])
        o, kn, vn, sr, si = mixer_sublayer(rmsnorm(xp, norm_mix_g[l]), lp, mk, mv, None, None)
        xp = xp + o
        f, cv = conv_ffn(rmsnorm(xp, norm_ffn_g[l]), lp, None)
        xp = xp + f
        pk.append(kn); pv.append(vn); psr.append(sr); psi.append(si); pcv.append(cv)
        pmk.append(mk); pmv.append(mv)
        o, kn, vn, sr, si = mixer_sublayer(rmsnorm(xs, norm_mix_g[l]), lp, cache_mem_k[l], cache_mem_v[l],
                                           (cache_swa_k[l], cache_swa_v[l]), (state_ssm_re[l], state_ssm_im[l]))
        xs = xs + o
        f, cv = conv_ffn(rmsnorm(xs, norm_ffn_g[l]), lp, cache_ffn_conv[l])
        xs = xs + f
        sk.append(kn); sv.append(vn); ssr.append(sr); ssi.append(si); scv.append(cv)
    y_prompt = rmsnorm(xp, norm_final_g)
    y_sample = rmsnorm(xs, norm_final_g)
    return (y_prompt, y_sample,
            jnp.stack(pk), jnp.stack(pv), jnp.stack(psr), jnp.stack(psi), jnp.stack(pcv),
            jnp.stack(pmk), jnp.stack(pmv),
            jnp.stack(sk), jnp.stack(sv), jnp.stack(ssr), jnp.stack(ssi), jnp.stack(scv))
```

```python
import contextlib
import math
import os
import numpy as np
import concourse.bass as bass
import concourse.mybir as mybir
from concourse.bass_utils import run_bass_kernel_spmd

F32 = mybir.dt.float32
BF = mybir.dt.bfloat16
I32 = mybir.dt.int32
AF = mybir.ActivationFunctionType
ALU = mybir.AluOpType
AX = mybir.AxisListType

D = 1024
DEPTH = 4
NS = 16
DFF = 2816
NFT = 22
INC = 4864
EPS = 1e-6
TWO_PI = 2.0 * math.pi


class Buf:
    def __init__(self, t, psum=False):
        self.t = t
        self.w = {}
        self.r = {}
        self.psum = psum


class KB:
    def __init__(self, nc, es):
        self.nc = nc
        self.es = es
        self.E = {'pe': nc.tensor, 'act': nc.scalar, 'dve': nc.vector, 'pool': nc.gpsimd, 'sp': nc.sync}
        self.cur = {}
        self.cnt = {}
        self.nsem = 0
        self.waited = {e: {} for e in self.E}
        for e in self.E:
            self._newesem(e)
        self.dq = {q: [[self._sem(), 0] for _ in range(8)] for q in ('sp', 'pool', 'act')}
        self.dqi = {q: 0 for q in self.dq}

    def _sem(self):
        self.nsem += 1
        return self.es.enter_context(self.nc.semaphore(f"sm{self.nsem}"))

    def _newesem(self, e):
        self.cur[e] = self._sem()
        self.cnt[e] = 0

    def wait(self, e, sem, val):
        k = id(sem)
        if self.waited[e].get(k, 0) >= val:
            return
        self.E[e].wait_ge(sem, val)
        self.waited[e][k] = val

    def deps(self, e, r, w, acc=False):
        own = self.cur[e]
        for b in r:
            for sem, val in b.w.values():
                if e == 'pe' and sem is own:
                    continue
                self.wait(e, sem, val)
            if b.psum:
                for sem, val in b.r.values():
                    if sem is not own:
                        self.wait(e, sem, val)
        if acc:
            return
        for b in w:
            for sem, val in list(b.w.values()) + list(b.r.values()):
                if e == 'pe' and sem is own:
                    continue
                self.wait(e, sem, val)

    def mark(self, sem, val, r, w, acc=False):
        for b in r:
            b.r[id(sem)] = (sem, val)
        for b in w:
            if acc:
                b.w[id(sem)] = (sem, val)
            else:
                b.w = {id(sem): (sem, val)}
                b.r = {}

    def op(self, e, fn, r=(), w=(), acc=False):
        self.deps(e, r, w, acc)
        ins = fn()
        self.cnt[e] += 1
        ins.then_inc(self.cur[e], 1)
        self.mark(self.cur[e], self.cnt[e], r, w, acc)

    def dma(self, q, out, in_, r=(), w=(), acc=False):
        slot = self.dq[q][self.dqi[q] % 8]
        self.dqi[q] += 1
        self.wait(q, slot[0], slot[1])
        self.deps(q, r, w, acc)
        self.E[q].dma_start(out=out, in_=in_).then_inc(slot[0], 16)
        slot[1] += 16
        self.mark(slot[0], slot[1], r, w, acc)

    def barrier(self):
        for e in self.E:
            for e2 in self.E:
                if e2 != e and self.cnt[e2] > 0:
                    self.wait(e, self.cur[e2], self.cnt[e2])
            for q in self.dq:
                for sem, val in self.dq[q]:
                    if val > 0:
                        self.wait(e, sem, val)
        for e in self.E:
            if self.cnt[e] > 20000:
                self._newesem(e)


def build(T):
    assert T % 512 == 0
    NTL = T // 512
    TA = T + NS
    nc = bass.Bass("TRN2", target_bir_lowering=False)

    def din(name, shape, dt=F32):
        return nc.dram_tensor(name, list(shape), dt, kind="ExternalInput").ap()

    def dout(name, shape):
        return nc.dram_tensor(name, list(shape), F32, kind="ExternalOutput").ap()

    def dscr(name, shape, dt):
        return nc.dram_tensor(name, list(shape), dt, kind="Internal").ap()

    xp = din("xp", [T, D]); xs = din("xs", [NS, D]); mem = din("mem", [256, D])
    c_swk = din("c_swk", [DEPTH, NS, 128, 128]); c_swv = din("c_swv", [DEPTH, NS, 128, 128])
    c_sre = din("c_sre", [DEPTH, NS, 2048]); c_sim = din("c_sim", [DEPTH, NS, 2048])
    c_cv = din("c_cv", [DEPTH, NS, 2, DFF])
    c_mk = din("c_mk", [DEPTH, NS, 256, 512]); c_mv = din("c_mv", [DEPTH, NS, 256, 512])
    g_mix = din("norm_mix_g", [DEPTH, D]); g_ffn = din("norm_ffn_g", [DEPTH, D])
    g_mem = din("norm_mem_g", [DEPTH, D]); g_fin = din("norm_final_g", [1, D])
    w_in = din("w_in", [DEPTH, D, INC]); sinks = din("sinks", [DEPTH, 8])
    w_bra = din("w_br_attn", [DEPTH, 512, D])
    lam_re = din("lam_re", [DEPTH, 32, 64]); lam_im = din("lam_im", [DEPTH, 32, 64])
    log_dt = din("log_dt", [DEPTH, 32])
    b_re = din("b_re", [DEPTH, 32, 64, 16]); b_im = din("b_im", [DEPTH, 32, 64, 16])
    c_re = din("c_re", [DEPTH, 32, 16, 64]); c_im = din("c_im", [DEPTH, 32, 16, 64])
    d_skip = din("d_skip", [DEPTH, 512])
    w_ga = din("w_glu_a", [DEPTH, 512, D]); w_gb = din("w_glu_b", [DEPTH, 512, D])
    w_mkv = din("w_mem_kv", [DEPTH, D, 1024]); w_brm = din("w_br_mem", [DEPTH, 512, D])
    w_out = din("w_out", [DEPTH, D, D])
    w_fg = din("w_ffn_gate", [DEPTH, D, DFF]); w_fu = din("w_ffn_up", [DEPTH, D, DFF])
    conv_w = din("conv_w", [DEPTH, 3, DFF]); conv_b = din("conv_b", [DEPTH, DFF])
    w_fd = din("w_ffn_down", [DEPTH, DFF, D])
    o_yp = dout("o_yp", [T, D]); o_ys = dout("o_ys", [NS, D])
    o_pk = dout("o_pk", [DEPTH, 128, 128]); o_pv = dout("o_pv", [DEPTH, 128, 128])
    o_pre = dout("o_pre", [DEPTH, 2048]); o_pim = dout("o_pim", [DEPTH, 2048])
    o_pcv = dout("o_pcv", [DEPTH, 2, DFF])
    o_pmk = dout("o_pmk", [DEPTH, 256, 512]); o_pmv = dout("o_pmv", [DEPTH, 256, 512])
    o_sk = dout("o_sk", [DEPTH, NS, 128, 128]); o_sv = dout("o_sv", [DEPTH, NS, 128, 128])
    o_sre = dout("o_sre", [DEPTH, NS, 2048]); o_sim = dout("o_sim", [DEPTH, NS, 2048])
    o_scv = dout("o_scv", [DEPTH, NS, 2, DFF])
    XT = dscr("XT", [D, TA], F32)
    Qs = dscr("Qs", [8, 64, TA], BF)
    Ks = dscr("Ks", [2, 64, 128 + TA], BF)
    Vs = dscr("Vs", [128 + T, 128], BF)
    Us = dscr("Us", [512, TA], BF)
    XQs = dscr("XQs", [512, TA], BF)
    SGs = dscr("SGs", [3072, TA], BF)
    OAs = dscr("OAs", [512, TA], BF)
    Zs = dscr("Zs", [512, TA], BF)
    OCs = dscr("OCs", [512, TA], BF)
    Gs = dscr("Gs", [DFF, TA], BF)
    VNs = dscr("VNs", [NS, 128], F32)

    tiles = [(i * 512, 512) for i in range(NTL)] + [(T, NS)]

    with contextlib.ExitStack() as es:
        es.enter_context(nc.allow_non_contiguous_dma(reason="small parameter tables"))
        kb = KB(nc, es)

        uid = [0]

        def sb(st, name, shape, dt):
            uid[0] += 1
            return Buf(st.enter_context(nc.sbuf_tensor(f"{name}_{uid[0]}", list(shape), dt)))

        def sbn(st, name, shape, dt, n):
            t = st.enter_context(nc.sbuf_tensor(name, list(shape), dt))
            return t, [Buf(t) for _ in range(n)]

        def psb(st, name, shape, dt):
            uid[0] += 1
            nb = int(np.prod(shape[1:])) * (2 if dt == BF else 4)
            assert nb % 2048 == 0, (name, shape)
            return Buf(st.enter_context(nc.psum_tensor(f"{name}_{uid[0]}", list(shape), dt)), psum=True)

        V, S_, G_, PE = 'dve', 'act', 'pool', 'pe'

        def TT(e, out, in0, in1, op, r, w):
            kb.op(e, lambda: kb.E[e].tensor_tensor(out=out, in0=in0, in1=in1, op=op), r, w)

        def TS(e, out, in0, s1, s2, op0, op1, r, w):
            if op1 is None:
                kb.op(e, lambda: kb.E[e].tensor_scalar(out=out, in0=in0, scalar1=s1, scalar2=None, op0=op0), r, w)
            else:
                kb.op(e, lambda: kb.E[e].tensor_scalar(out=out, in0=in0, scalar1=s1, scalar2=s2, op0=op0, op1=op1), r, w)

        def STT(out, in0, sc, in1, op0, op1, r, w):
            kb.op(V, lambda: nc.vector.scalar_tensor_tensor(out=out, in0=in0, scalar=sc, in1=in1, op0=op0, op1=op1), r, w)

        def ACTF(out, in_, func, r, w, bias=None, scale=None, accum=None):
            kw = {}
            if bias is not None:
                kw['bias'] = bias
            if scale is not None:
                kw['scale'] = scale
            if accum is not None:
                kw['accum_out'] = accum
            kb.op(S_, lambda: nc.scalar.activation(out=out, in_=in_, func=func, **kw), r, w)

        def CP(e, out, in_, r, w, acc=False):
            if e == S_:
                kb.op(e, lambda: nc.scalar.copy(out=out, in_=in_), r, w, acc)
            else:
                kb.op(e, lambda: kb.E[e].tensor_copy(out=out, in_=in_), r, w, acc)

        def MM(out, lhsT, rhs, start, stop, r, w):
            kb.op(PE, lambda: nc.tensor.matmul(out, lhsT, rhs, start=start, stop=stop), r, w)

        def TR(out, in_, ident, r, w):
            kb.op(PE, lambda: nc.tensor.transpose(out, in_, ident), r, w)

        def MEMSET(e, ap, val, w):
            kb.op(e, lambda: kb.E[e].memset(ap, val), (), w)

        wst = {"bufs": None, "i": 0}

        def load_w_begin(st):
            wst["bufs"] = [sb(st, f"wstg{i}", [128, 2048], F32) for i in range(2)]

        def load_w(dst, dram_rows_fn, nkt, ncols, q='pool'):
            for kt in range(nkt):
                src_ = dram_rows_fn(kt)
                for c0 in range(0, ncols, 2048):
                    c1 = min(ncols, c0 + 2048)
                    if wst["bufs"] is None:
                        kb.dma('pool', dst.t[:, kt, c0:c1], src_[:, c0:c1], (), [dst], acc=True)
                        continue
                    sg_ = wst["bufs"][wst["i"] % 2]
                    e = (S_, V)[wst["i"] % 2]
                    wst["i"] += 1
                    kb.dma('sp', sg_.t[:, 0:c1 - c0], src_[:, c0:c1], (), [sg_])
                    CP(e, dst.t[:, kt, c0:c1], sg_.t[:, 0:c1 - c0], [sg_], [dst], acc=True)

        def load_w_end():
            wst["bufs"] = None

        evac_rr = [0]

        def evac(out, in_, r, w):
            e = (S_, V)[evac_rr[0] % 2]
            evac_rr[0] += 1
            CP(e, out, in_, r, w)

        cst = es
        ident_i = sb(cst, "ident_i", [128, 128], I32)
        identf = sb(cst, "identf", [128, 128], F32)
        identb = sb(cst, "identb", [128, 128], BF)
        onesf = sb(cst, "onesf", [128, 128], F32)
        kb.op(G_, lambda: nc.gpsimd.iota(ident_i.t[:], pattern=[[1, 128]], base=0, channel_multiplier=-1), (), [ident_i])
        TS(V, identf.t[:], ident_i.t[:], 0, None, ALU.is_equal, None, [ident_i], [identf])
        CP(V, identb.t[:], identf.t[:], [identf], [identb])
        MEMSET(V, onesf.t[:], 1.0, [onesf])
        onesb = sb(cst, "onesb", [128, 128], BF)
        MEMSET(V, onesb.t[:], 1.0, [onesb])
        dist_i = sb(cst, "dist_i", [128, 256], I32)
        distf = sb(cst, "distf", [128, 256], F32)
        mskf = sb(cst, "mskf", [128, 256], F32)
        msk2 = sb(cst, "msk2", [128, 256], F32)
        biasA = sb(cst, "biasA", [128, 8, 256], F32)
        biasA0 = sb(cst, "biasA0", [128, 8, 256], F32)
        kb.op(G_, lambda: nc.gpsimd.iota(dist_i.t[:], pattern=[[-1, 256]], base=128, channel_multiplier=1), (), [dist_i])
        CP(V, distf.t[:], dist_i.t[:], [dist_i], [distf])
        TS(V, mskf.t[:], distf.t[:], 0.0, None, ALU.is_ge, None, [distf], [mskf])
        TS(V, msk2.t[:], distf.t[:], 128.0, None, ALU.is_le, None, [distf], [msk2])
        TT(V, mskf.t[:], mskf.t[:], msk2.t[:], ALU.mult, [mskf, msk2], [mskf])
        TS(V, msk2.t[:], mskf.t[:], -1.0, 30000.0, ALU.add, ALU.mult, [mskf], [msk2])
        TT(V, distf.t[:], distf.t[:], mskf.t[:], ALU.mult, [distf, mskf], [distf])
        for h in range(8):
            slope = 2.0 ** (-(h + 1))
            STT(biasA.t[:, h, :], distf.t[:], -slope, msk2.t[:], ALU.mult, ALU.add, [distf, msk2], [biasA])
        CP(V, biasA0.t[:], biasA.t[:], [biasA], [biasA0])
        MEMSET(V, biasA0.t[:, :, 0:128], -30000.0, [biasA0])
        bs_i = sb(cst, "bs_i", [4, 129], I32)
        bs_f = sb(cst, "bs_f", [4, 129], F32)
        biasS = sb(cst, "biasS", [4, 2, 129], F32)
        slp = sb(cst, "slp", [4, 2], F32)
        slp_i = sb(cst, "slp_i", [4, 2], I32)
        kb.op(G_, lambda: nc.gpsimd.iota(bs_i.t[:], pattern=[[-1, 129]], base=128, channel_multiplier=0), (), [bs_i])
        CP(V, bs_f.t[:], bs_i.t[:], [bs_i], [bs_f])
        kb.op(G_, lambda: nc.gpsimd.iota(slp_i.t[:], pattern=[[4, 2]], base=1, channel_multiplier=1), (), [slp_i])
        CP(V, slp.t[:], slp_i.t[:], [slp_i], [slp])
        ACTF(slp.t[:], slp.t[:], AF.Exp, [slp], [slp], scale=-math.log(2.0))
        for k2 in range(2):
            TS(V, biasS.t[:, k2, :], bs_f.t[:], slp.t[:, k2:k2 + 1], -1.0, ALU.mult, ALU.mult, [bs_f, slp], [biasS])
        jidx_i = sb(cst, "jidx_i", [128, 512], I32)
        jidx = sb(cst, "jidx", [128, 512], F32)
        kb.op(G_, lambda: nc.gpsimd.iota(jidx_i.t[:], pattern=[[1, 512]], base=0, channel_multiplier=0), (), [jidx_i])
        CP(V, jidx.t[:], jidx_i.t[:], [jidx_i], [jidx])
        xqT_s = sb(cst, "xqT_s", [128, 4, NS], BF)
        qT_s = sb(cst, "qT_s", [64, 8, NS], BF)
        kT_s = sb(cst, "kT_s", [64, 2, NS], BF)
        uT_s = sb(cst, "uT_s", [128, 4, NS], BF)
        mkT = sb(cst, "mkT", [128, 4, 256], BF)
        mvT = sb(cst, "mvT", [128, 2, 512], BF)
        kb.barrier()

        def stage_p0():
            with contextlib.ExitStack() as st:
                xin = [sb(st, f"xin{i}", [128, D], F32) for i in range(2)]
                xo = [sb(st, f"xo{i}", [128, 8, 128], F32) for i in range(2)]
                pt = [psb(st, f"p0t{i}", [128, 1024], F32) for i in range(2)]
                blocks = [(j * 128, 128, xp[j * 128:(j + 1) * 128, :]) for j in range(T // 128)] + [(T, NS, xs[:, :])]
                for bi, (t0, n, src) in enumerate(blocks):
                    a = bi % 2
                    kb.dma('sp', xin[a].t[0:n, :], src, (), [xin[a]])
                    for k in range(8):
                        TR(pt[a].t[:, k * 128:k * 128 + n], xin[a].t[0:n, k * 128:(k + 1) * 128], identf.t[0:n, 0:n], [xin[a], identf], [pt[a]])
                    evac(xo[a].t[:, :, 0:n], pt[a].t[:].rearrange("p (k t) -> p k t", k=8)[:, :, 0:n], [pt[a]], [xo[a]])
                    kb.dma('sp', XT.rearrange("(k p) t -> p k t", p=128)[:, :, t0:t0 + n], xo[a].t[:, :, 0:n], [xo[a]], ())
            kb.barrier()

        def rmsnorm_tile(xt, n, gcol, hout, sq, rt, pss, r_extra=()):
            ACTF(sq.t[:, :, 0:n], xt.t[:, :, 0:n], AF.Square, [xt], [sq])
            for k in range(8):
                MM(pss.t[:, 0:n], onesb.t[:], sq.t[:, k, 0:n], k == 0, k == 7, [onesb, sq], [pss])
            ACTF(rt.t[:, 0:n], pss.t[:, 0:n], AF.Sqrt, [pss], [rt], bias=EPS, scale=1.0 / D)
            kb.op(V, lambda: nc.vector.reciprocal(out=rt.t[:, 0:n], in_=rt.t[:, 0:n]), [rt], [rt])
            for k in range(8):
                STT(hout.t[:, k, 0:n], xt.t[:, k, 0:n], gcol.t[:, k:k + 1], rt.t[:, 0:n], ALU.mult, ALU.mult, [xt, gcol, rt], [hout])

        def stage_mem(l):
            with contextlib.ExitStack() as st:
                wm = sb(st, "wm", [128, 8, 1024], BF)
                load_w_begin(st)
                load_w(wm, lambda kt: w_mkv[l, kt * 128:(kt + 1) * 128, :], 8, 1024)
                load_w_end()
                gm = sb(st, "gm", [128, 8], F32)
                kb.dma('sp', gm.t[:], g_mem[l].rearrange("(k p) -> p k", p=128), (), [gm])
                min_ = sb(st, "min_", [128, 2, D], F32)
                kb.dma('sp', min_.t[:], mem.rearrange("(j p) d -> p j d", p=128), (), [min_])
                mx = sb(st, "mx", [128, 8, 512], F32)
                mh = sb(st, "mh", [128, 8, 512], BF)
                sq = sb(st, "msq", [128, 8, 512], BF)
                rt = sb(st, "mrt", [128, 512], F32)
                pss = psb(st, "mpss", [128, 512], F32)
                pt = psb(st, "mpt", [128, 1024], F32)
                po = [psb(st, f"mpo{i}", [128, 512], F32) for i in range(2)]
                mo = [sb(st, f"mo{i}", [128, 512], F32) for i in range(2)]
                for j in range(2):
                    for k in range(8):
                        TR(pt.t[:, k * 128:(k + 1) * 128], min_.t[:, j, k * 128:(k + 1) * 128], identf.t[:], [min_, identf], [pt])
                    evac(mx.t[:, :, j * 128:(j + 1) * 128], pt.t[:].rearrange("p (k t) -> p k t", k=8), [pt], [mx])
                rmsnorm_tile(mx, 256, gm, mh, sq, rt, pss)
                cnt = 0
                for j in range(2):
                    for half in range(2):
                        a = cnt % 2
                        cnt += 1
                        for k in range(8):
                            MM(po[a].t[:, :], mh.t[:, k, j * 128:(j + 1) * 128], wm.t[:, k, half * 512:(half + 1) * 512], k == 0, k == 7, [mh, wm], [po[a]])
                        evac(mo[a].t[:], po[a].t[:], [po[a]], [mo[a]])
                        dst = (o_pmk, o_pmv)[half]
                        kb.dma('sp', dst[l, j * 128:(j + 1) * 128, :], mo[a].t[:], [mo[a]], ())
                        if half == 1:
                            CP(V, mvT.t[:, j, :], mo[a].t[:], [mo[a]], [mvT])
                for h in range(4):
                    a = h % 2
                    for k in range(8):
                        MM(po[a].t[:, 0:256], wm.t[:, k, h * 128:(h + 1) * 128], mh.t[:, k, 0:256], k == 0, k == 7, [mh, wm], [po[a]])
                    evac(mkT.t[:, h, :], po[a].t[:, 0:256], [po[a]], [mkT])
            kb.barrier()

        def stage_in(l):
            with contextlib.ExitStack() as st:
                win = sb(st, "win", [128, 8, INC], BF)
                load_w_begin(st)
                load_w(win, lambda kt: w_in[l, kt * 128:(kt + 1) * 128, :], 8, INC)
                load_w_end()
                gm = sb(st, "gmix", [128, 8], F32)
                kb.dma('sp', gm.t[:], g_mix[l].rearrange("(k p) -> p k", p=128), (), [gm])
                xt = [sb(st, "xt0", [128, 8, 512], F32)] * 2
                hT = [sb(st, f"hT{i}", [128, 8, 512], BF) for i in range(2)]
                sq = sb(st, "sq", [128, 8, 512], BF)
                rt = sb(st, "rt", [128, 512], F32)
                pss = psb(st, "pss", [128, 512], F32)
                pp = [psb(st, f"pp{i}", [128, 512], F32) for i in range(5)]
                stq = sb(st, "stq", [64, 8, 512], BF)
                stk = sb(st, "stk", [64, 2, 512], BF)
                stv = sb(st, "stv", [128, 4, 128], BF)
                stvf = sb(st, "stvf", [128, 128], F32)
                stkf = sb(st, "stkf", [128, 128], F32)
                stu = sb(st, "stu", [128, 4, 512], BF)
                stx = sb(st, "stx", [128, 4, 512], BF)
                sg = [sb(st, "sg0", [128, 8, 512], BF)] * 2
                ppi = [0]

                def nextp():
                    p = pp[ppi[0] % 5]
                    ppi[0] += 1
                    return p
                XTv = XT.rearrange("(k p) t -> p k t", p=128)
                def prol(ti):
                    t0_, n_ = tiles[ti]
                    kb.dma('sp', xt[ti % 2].t[:, :, 0:n_], XTv[:, :, t0_:t0_ + n_], (), [xt[ti % 2]])
                    rmsnorm_tile(xt[ti % 2], n_, gm, hT[ti % 2], sq, rt, pss)
                prol(0)
                for ti, (t0, n) in enumerate(tiles):
                    a = ti % 2
                    smp = (n == NS)
                    if ti + 1 < len(tiles):
                        prol(ti + 1)
                    h_ = hT[a]

                    def proj(c0, M, outap, r_w, sig=False):
                        p = nextp()
                        for k in range(8):
                            MM(p.t[0:M, 0:n], win.t[:, k, c0:c0 + M], h_.t[:, k, 0:n], k == 0, k == 7, [win, h_], [p])
                        if sig:
                            ACTF(outap, p.t[0:M, 0:n], AF.Sigmoid, [p], r_w)
                        else:
                            evac(outap, p.t[0:M, 0:n], [p], r_w)
                    kin = os.environ.get("KIN", "q,k,v,u,x,g").split(",")
                    for h in range(8 if "q" in kin else 0):
                        proj(64 * h, 64, (qT_s.t[:, h, :] if smp else stq.t[:, h, 0:n]), [qT_s if smp else stq])
                    for h in range(2 if "k" in kin else 0):
                        proj(512 + 64 * h, 64, (kT_s.t[:, h, :] if smp else stk.t[:, h, 0:n]), [kT_s if smp else stk])
                    if not smp and "q" in kin and "k" in kin:
                        kb.dma('sp', Qs.rearrange("h p t -> p h t")[:, :, t0:t0 + n], stq.t[:, :, 0:n], [stq], ())
                        kb.dma('sp', Ks.rearrange("h p t -> p h t")[:, :, 128 + t0:128 + t0 + n], stk.t[:, :, 0:n], [stk], ())
                    nb = 1 if smp else 4
                    if "v" not in kin:
                        nb = 0
                    kvs = os.environ.get("KVS", "p,s").split(",")
                    if (smp and "s" not in kvs) or ((not smp) and "p" not in kvs):
                        nb = 0
                    kv = os.environ.get("KV", "vn,osv,opv,ktok,vs").split(",")
                    for j in range(nb):
                        m = NS if smp else 128
                        p = nextp()
                        kvx = os.environ.get("KVX", "")
                        for k in range(8):
                            lh = win.t[:, k, 0:m] if kvx == "lw" else h_.t[:, k, j * 128:j * 128 + m]
                            rh = h_.t[:, k, 0:128] if kvx == "rh" else win.t[:, k, 640:768]
                            MM(p.t[0:m, 0:128], lh, rh, k == 0, k == 7, [win, h_], [p])
                        last = (not smp) and (t0 + 512 == T) and j == 3
                        if smp:
                            CP(V, stvf.t[0:m, :], p.t[0:m, 0:128], [p], [stvf])
                            if "vn" in kv:
                                kb.dma('sp', VNs[:, :], stvf.t[0:m, :], [stvf], ())
                            if "osv" in kv:
                                kb.dma('sp', o_sv[l, :, 127, :], stvf.t[0:m, :], [stvf], ())
                        else:
                            kve = os.environ.get("KVE", "")
                            if kve != "noevac":
                                evac(stv.t[:, j, :], p.t[:, 0:128], [p], [stv])
                            if last and kve != "nocp":
                                CP(V, stvf.t[:], p.t[:, 0:128], [p], [stvf])
                                if "opv" in kv:
                                    kb.dma('sp', o_pv[l, :, :], stvf.t[:], [stvf], ())
                        if (smp or last) and "ktok" in kv:
                            p = nextp()
                            for k in range(8):
                                MM(p.t[0:m, 0:128], h_.t[:, k, j * 128:j * 128 + m], win.t[:, k, 512:640], k == 0, k == 7, [win, h_], [p])
                            CP(V, stkf.t[0:m, :], p.t[0:m, 0:128], [p], [stkf])
                            if smp:
                                kb.dma('sp', o_sk[l, :, 127, :], stkf.t[0:m, :], [stkf], ())
                            else:
                                kb.dma('sp', o_pk[l, :, :], stkf.t[:], [stkf], ())
                    if not smp and "v" in kin and "vs" in kv:
                        kb.dma('sp', Vs[128 + t0:128 + t0 + 512, :].rearrange("(j p) c -> p j c", p=128), stv.t[:], [stv], ())
                    for f in range(4 if "u" in kin else 0):
                        proj(768 + 128 * f, 128, (uT_s.t[:, f, :] if smp else stu.t[:, f, 0:n]), [uT_s if smp else stu])
                    for f in range(4 if "x" in kin else 0):
                        proj(1280 + 128 * f, 128, (xqT_s.t[:, f, :] if smp else stx.t[:, f, 0:n]), [xqT_s if smp else stx])
                    if "u" in kin:
                        kb.dma('sp', Us.rearrange("(f p) t -> p f t", p=128)[:, :, t0:t0 + n],
                               (uT_s.t[:] if smp else stu.t[:, :, 0:n]), [uT_s if smp else stu], ())
                    if not smp and "x" in kin:
                        kb.dma('sp', XQs.rearrange("(f p) t -> p f t", p=128)[:, :, t0:t0 + n], stx.t[:, :, 0:n], [stx], ())
                    for gi in range(3 if "g" in kin else 0):
                        s_ = sg[gi % 2]
                        for f in range(8):
                            proj(1792 + gi * 1024 + f * 128, 128, s_.t[:, f, 0:n], [s_], sig=True)
                        kb.dma('sp', SGs.rearrange("(f p) t -> p f t", p=128)[:, gi * 8:(gi + 1) * 8, t0:t0 + n], s_.t[:, :, 0:n], [s_], ())
            kb.barrier()

        def attn_unit(*args):
            for _ in attn_unit_g(*args):
                pass

        def attn_unit_g(st_t, nq, heads, nk, qfn, kfn, vfn, hd, scale, bias_ap, sink_ap, out_fn):
            pS, sS, pS_b, pT, sT, pO, mx, rs, es_, dn = st_t
            nkt = (nk + 127) // 128
            for h in range(heads):
                qa, qb = qfn(h)
                ka, kbuf = kfn(h)
                MM(pS.t[0:nq, h, 0:nk], qa, ka, True, True, qb + kbuf, [pS])
            yield
            if bias_ap is not None:
                STT(sS.t[0:nq, 0:heads, 0:nk], pS.t[0:nq, 0:heads, 0:nk], scale, bias_ap[0], ALU.mult, ALU.add, [pS] + bias_ap[1], [sS])
            else:
                TS(V, sS.t[0:nq, 0:heads, 0:nk], pS.t[0:nq, 0:heads, 0:nk], scale, None, ALU.mult, None, [pS], [sS])
            kb.op(V, lambda: nc.vector.tensor_reduce(out=mx.t[0:nq, 0:heads], in_=sS.t[0:nq, 0:heads, 0:nk], axis=AX.X, op=ALU.max), [sS], [mx])
            if sink_ap is not None:
                TT(V, mx.t[0:nq, 0:heads], mx.t[0:nq, 0:heads], sink_ap[0], ALU.max, [mx] + sink_ap[1], [mx])
            TS(V, dn.t[0:nq, 0:heads], mx.t[0:nq, 0:heads], -1.0, None, ALU.mult, None, [mx], [dn])
            for h in range(heads):
                ACTF(sS.t[0:nq, h, 0:nk], sS.t[0:nq, h, 0:nk], AF.Exp, [sS, dn], [sS, rs],
                     bias=dn.t[0:nq, h:h + 1], accum=rs.t[0:nq, h:h + 1])
            yield
            if sink_ap is not None:
                TT(V, es_.t[0:nq, 0:heads], sink_ap[0], mx.t[0:nq, 0:heads], ALU.subtract, [mx] + sink_ap[1], [es_])
                ACTF(es_.t[0:nq, 0:heads], es_.t[0:nq, 0:heads], AF.Exp, [es_], [es_])
                TT(V, rs.t[0:nq, 0:heads], rs.t[0:nq, 0:heads], es_.t[0:nq, 0:heads], ALU.add, [rs, es_], [rs])
            kb.op(V, lambda: nc.vector.reciprocal(out=dn.t[0:nq, 0:heads], in_=rs.t[0:nq, 0:heads]), [rs], [dn])
            TT(V, pS_b.t[0:nq, 0:heads, 0:nk], sS.t[0:nq, 0:heads, 0:nk],
               dn.t[0:nq, 0:heads].rearrange("p (h o) -> p h o", o=1).to_broadcast([nq, heads, nk]), ALU.mult, [sS, dn], [pS_b])
            for h in range(heads):
                for kt in range(nkt):
                    kn = min(128, nk - kt * 128)
                    TR(pT.t[0:kn, h, kt, 0:nq], pS_b.t[0:nq, h, kt * 128:kt * 128 + kn], identb.t[0:nq, 0:nq], [pS_b, identb], [pT])
            for kt in range(nkt):
                kn = min(128, nk - kt * 128)
                CP(S_, sT.t[0:kn, 0:heads, kt, 0:nq], pT.t[0:kn, 0:heads, kt, 0:nq], [pT], [sT])
            for h in range(heads):
                for kt in range(nkt):
                    kn = min(128, nk - kt * 128)
                    va, vb = vfn(h, kt, kn)
                    MM(pO.t[0:hd, h, 0:nq], va, sT.t[0:kn, h, kt, 0:nq], kt == 0, kt == nkt - 1, vb + [sT], [pO])
            out_fn(pO)

        def run_pipelined(units):
            n = len(units)
            for s in range(n + 2):
                if s < n:
                    if units[s][0] is not None:
                        units[s][0]()
                    next(units[s][1])
                if 0 <= s - 1 < n:
                    next(units[s - 1][1])
                if 0 <= s - 2 < n:
                    for _ in units[s - 2][1]:
                        pass
                    if units[s - 2][2] is not None:
                        units[s - 2][2]()

        def attn_tiles(st, pfx, psum_from=None):
            if psum_from is not None:
                pS, pT, pO = psum_from[0], psum_from[3], psum_from[5]
                sS = sb(st, pfx + "sS", [128, 4, 256], F32)
                pS_b = sb(st, pfx + "pSb", [128, 4, 256], BF)
                sT = sb(st, pfx + "sT", [128, 4, 2, 128], BF)
                mx = sb(st, pfx + "mx", [128, 4], F32)
                rs = sb(st, pfx + "rs", [128, 4], F32)
                es_ = sb(st, pfx + "es", [128, 4], F32)
                dn = sb(st, pfx + "dn", [128, 4], F32)
                return (pS, sS, pS_b, pT, sT, pO, mx, rs, es_, dn)
            pS = psb(st, pfx + "pS", [128, 4, 256], F32)
            sS = sb(st, pfx + "sS", [128, 4, 256], F32)
            pS_b = sb(st, pfx + "pSb", [128, 4, 256], BF)
            pT = psb(st, pfx + "pT", [128, 4, 2, 128], BF)
            sT = sb(st, pfx + "sT", [128, 4, 2, 128], BF)
            pO = psb(st, pfx + "pO", [128, 4, 128], F32)
            mx = sb(st, pfx + "mx", [128, 4], F32)
            rs = sb(st, pfx + "rs", [128, 4], F32)
            es_ = sb(st, pfx + "es", [128, 4], F32)
            dn = sb(st, pfx + "dn", [128, 4], F32)
            return (pS, sS, pS_b, pT, sT, pO, mx, rs, es_, dn)

        def stage_att(l):
            with contextlib.ExitStack() as st:
                tls = [attn_tiles(st, "a0"), attn_tiles(st, "a1")]
                tl = tls[0]
                ucnt = [0]
                snk = sb(st, "snk", [128, 8], F32)
                kb.dma('sp', snk.t[:], sinks[l:l + 1, :].to_broadcast([128, 8]), (), [snk])
                snkS = sb(st, "snkS", [4, 2], F32)
                kb.dma('sp', snkS.t[:], sinks[l].rearrange("(k g) -> g k", g=4), (), [snkS])
                qt = [sb(st, f"aq{i}", [64, 8, 512], BF) for i in range(2)]
                kt_ = [sb(st, f"ak{i}", [64, 2, 640], BF) for i in range(2)]
                vt = [sb(st, f"av{i}", [128, 5, 128], BF) for i in range(2)]
                oa = [sb(st, f"ao{i}", [64, 8, 512], BF) for i in range(2)]
                zk = sb(st, "zk", [64, 2, 128], BF)
                zv = sb(st, "zv", [128, 128], BF)
                MEMSET(V, zk.t[:], 0.0, [zk])
                MEMSET(V, zv.t[:], 0.0, [zv])
                kb.dma('sp', Ks.rearrange("h p t -> p h t")[:, :, 0:128], zk.t[:], [zk], ())
                kb.dma('sp', Vs[0:128, :], zv.t[:], [zv], ())
                kb.barrier()
                units = []
                for ti in range(NTL):
                    a = ti % 2
                    t0 = ti * 512

                    def pre(a=a, t0=t0):
                        kb.dma('sp', qt[a].t[:], Qs.rearrange("h p t -> p h t")[:, :, t0:t0 + 512], (), [qt[a]])
                        kb.dma('sp', kt_[a].t[:], Ks.rearrange("h p t -> p h t")[:, :, t0:t0 + 640], (), [kt_[a]])
                        kb.dma('sp', vt[a].t[:], Vs[t0:t0 + 640, :].rearrange("(j p) c -> p j c", p=128), (), [vt[a]])

                    def post(a=a, t0=t0):
                        kb.dma('sp', OAs.rearrange("(h p) t -> p h t", p=64)[:, :, t0:t0 + 512], oa[a].t[:], [oa[a]], ())
                    for j in range(4):
                        for k2 in range(2):
                            bias = (biasA0 if (ti == 0 and j == 0) else biasA)
                            ucnt[0] += 1
                            g = attn_unit_g(
                                tls[ucnt[0] % 2], 128, 4, 256,
                                lambda h, a=a, j=j, k2=k2: (qt[a].t[:, 4 * k2 + h, j * 128:(j + 1) * 128], [qt[a]]),
                                lambda h, a=a, j=j, k2=k2: (kt_[a].t[:, k2, j * 128:j * 128 + 256], [kt_[a]]),
                                lambda h, kt, kn, a=a, j=j, k2=k2: (vt[a].t[:, j + kt, 64 * k2:64 * k2 + 64], [vt[a]]),
                                64, 0.125, (bias.t[:, 4 * k2:4 * k2 + 4, :], [bias]), (snk.t[:, 4 * k2:4 * k2 + 4], [snk]),
                                lambda pO, a=a, j=j, k2=k2: CP(S_, oa[a].t[:, 4 * k2:4 * k2 + 4, j * 128:(j + 1) * 128], pO.t[0:64, :, :], [pO], [oa[a]]))
                            first = (j == 0 and k2 == 0)
                            lastu = (j == 3 and k2 == 1)
                            units.append((pre if first else None, g, post if lastu else None))
                run_pipelined(units)
                kc = [sb(st, f"skc{i}", [128, 128], F32) for i in range(2)]
                vc = [sb(st, f"svc{i}", [128, 128], F32) for i in range(2)]
                vcb = [sb(st, f"svcb{i}", [128, 128], BF) for i in range(2)]
                vn = [sb(st, f"svn{i}", [1, 128], F32) for i in range(2)]
                vnb = [sb(st, f"svnb{i}", [1, 128], BF) for i in range(2)]
                ktx = [sb(st, f"sktx{i}", [64, 2, 129], BF) for i in range(2)]
                pk = tls[1][5]
                oas = sb(st, "oas", [64, 8, NS], BF)
                for b in range(NS):
                    a = b % 2
                    kb.dma('sp', kc[a].t[:], c_swk[l, b], (), [kc[a]])
                    kb.dma('sp', vc[a].t[:], c_swv[l, b], (), [vc[a]])
                    kb.dma('sp', vn[a].t[:], VNs[b:b + 1, :], (), [vn[a]])
                    kb.dma('sp', o_sk[l, b, 0:127, :], c_swk[l, b, 1:128, :], (), ())
                    kb.dma('sp', o_sv[l, b, 0:127, :], c_swv[l, b, 1:128, :], (), ())
                    for k2 in range(2):
                        TR(pk.t[0:64, k2, 0:128], kc[a].t[:, 64 * k2:64 * k2 + 64], identf.t[:], [kc[a], identf], [pk])
                    evac(ktx[a].t[:, :, 0:128], pk.t[0:64, 0:2, 0:128], [pk], [ktx[a]])
                    CP(V, ktx[a].t[:, :, 128:129], kT_s.t[:, :, b:b + 1], [kT_s], [ktx[a]])
                    CP(V, vcb[a].t[:], vc[a].t[:], [vc[a]], [vcb[a]])
                    CP(V, vnb[a].t[:], vn[a].t[:], [vn[a]], [vnb[a]])
                    for k2 in range(2):
                        attn_unit(
                            tl, 4, 1, 129,
                            lambda h: (qT_s.t[:, 4 * k2:4 * k2 + 4, b], [qT_s]),
                            lambda h: (ktx[a].t[:, k2, :], [ktx[a]]),
                            lambda h, kt, kn: ((vcb[a].t[:, 64 * k2:64 * k2 + 64], [vcb[a]]) if kt == 0 else (vnb[a].t[0:1, 64 * k2:64 * k2 + 64], [vnb[a]])),
                            64, 0.125, (biasS.t[:, k2:k2 + 1, :], [biasS]), (snkS.t[:, k2:k2 + 1], [snkS]),
                            lambda pO: evac(oas.t[:, 4 * k2:4 * k2 + 4, b], pO.t[0:64, 0, 0:4], [pO], [oas]))
                kb.dma('sp', OAs.rearrange("(h p) t -> p h t", p=64)[:, :, T:T + NS], oas.t[:], [oas], ())
            kb.barrier()


        def sin_of(out_ap, ang_ap, shape, tmpf, tmpi, tmpm, bufs_in, buf_out, phase=0.0):
            tf, ti_, tm = tmpf, tmpi, tmpm
            TS(V, tf[0], ang_ap, 1.0, phase, ALU.mult, ALU.add, bufs_in, [tf[1]])
            TS(V, tm[0], tf[0], 1.0 / TWO_PI, None, ALU.mult, None, [tf[1]], [tm[1]])
            CP(V, ti_[0], tm[0], [tm[1]], [ti_[1]])
            CP(V, tm[0], ti_[0], [ti_[1]], [tm[1]])
            STT(tf[0], tm[0], -TWO_PI, tf[0], ALU.mult, ALU.add, [tm[1], tf[1]], [tf[1]])
            TS(V, tm[0], tf[0], math.pi, None, ALU.is_gt, None, [tf[1]], [tm[1]])
            STT(tf[0], tm[0], -TWO_PI, tf[0], ALU.mult, ALU.add, [tm[1], tf[1]], [tf[1]])
            TS(V, tm[0], tf[0], -math.pi, None, ALU.is_lt, None, [tf[1]], [tm[1]])
            STT(tf[0], tm[0], TWO_PI, tf[0], ALU.mult, ALU.add, [tm[1], tf[1]], [tf[1]])
            TS(V, tf[0], tf[0], math.pi, -math.pi, ALU.min, ALU.max, [tf[1]], [tf[1]])
            ACTF(out_ap, tf[0], AF.Sin, [tf[1]], [buf_out])

        def bc3(ap2, n):
            P_, M_ = ap2.shape[0], ap2.shape[1]
            return ap2.rearrange("p (m o) -> p m o", o=1).to_broadcast([P_, M_, n])

        def stage_s5(l):
            with contextlib.ExitStack() as st:
                def t2(name, shape=(128, 16), dt=F32):
                    return sb(st, name, list(shape), dt)
                lr, li, dtl, dtt, th, mag, ar, ai = [t2(n) for n in ("lr", "li", "dtl", "dtt", "th", "mag", "ar", "ai")]
                cs, sn, den, fr, fi, tA, tB = [t2(n) for n in ("cs", "sn", "den", "fr", "fi", "tA", "tB")]
                tf = t2("rtf"); ti_ = t2("rti", dt=I32); tm = t2("rtm")
                kb.dma('sp', lr.t[:], lam_re[l].rearrange("(m two) p -> (two p) m", two=2), (), [lr])
                kb.dma('sp', li.t[:], lam_im[l].rearrange("(m two) p -> (two p) m", two=2), (), [li])
                ldv = log_dt[l].rearrange("(m two) -> two m", two=2)
                kb.dma('sp', dtl.t[0:64, :], ldv[0:1, :].to_broadcast([64, 16]), (), [dtl], acc=True)
                kb.dma('sp', dtl.t[64:128, :], ldv[1:2, :].to_broadcast([64, 16]), (), [dtl], acc=True)
                ACTF(dtt.t[:], dtl.t[:], AF.Exp, [dtl], [dtt])
                TT(V, th.t[:], li.t[:], dtt.t[:], ALU.mult, [li, dtt], [th])
                TT(V, mag.t[:], lr.t[:], dtt.t[:], ALU.mult, [lr, dtt], [mag])
                ACTF(mag.t[:], mag.t[:], AF.Exp, [mag], [mag])
                sin_of(sn.t[:], th.t[:], None, (tf.t[:], tf), (ti_.t[:], ti_), (tm.t[:], tm), [th], sn)
                sin_of(cs.t[:], th.t[:], None, (tf.t[:], tf), (ti_.t[:], ti_), (tm.t[:], tm), [th], cs, phase=math.pi / 2)
                TT(V, ar.t[:], mag.t[:], cs.t[:], ALU.mult, [mag, cs], [ar])
                TT(V, ai.t[:], mag.t[:], sn.t[:], ALU.mult, [mag, sn], [ai])
                TT(V, den.t[:], lr.t[:], lr.t[:], ALU.mult, [lr], [den])
                TT(V, tA.t[:], li.t[:], li.t[:], ALU.mult, [li], [tA])
                TT(V, den.t[:], den.t[:], tA.t[:], ALU.add, [den, tA], [den])
                kb.op(V, lambda: nc.vector.reciprocal(out=den.t[:], in_=den.t[:]), [den], [den])
                TS(V, tA.t[:], ar.t[:], -1.0, None, ALU.add, None, [ar], [tA])
                TT(V, fr.t[:], tA.t[:], lr.t[:], ALU.mult, [tA, lr], [fr])
                TT(V, tB.t[:], ai.t[:], li.t[:], ALU.mult, [ai, li], [tB])
                TT(V, fr.t[:], fr.t[:], tB.t[:], ALU.add, [fr, tB], [fr])
                TT(V, fr.t[:], fr.t[:], den.t[:], ALU.mult, [fr, den], [fr])
                TT(V, fi.t[:], ai.t[:], lr.t[:], ALU.mult, [ai, lr], [fi])
                TT(V, tB.t[:], tA.t[:], li.t[:], ALU.mult, [tA, li], [tB])
                TT(V, fi.t[:], fi.t[:], tB.t[:], ALU.subtract, [fi, tB], [fi])
                TT(V, fi.t[:], fi.t[:], den.t[:], ALU.mult, [fi, den], [fi])
                cosT = sb(st, "cosT", [128, 16, 512], F32)
                sinT = sb(st, "sinT", [128, 16, 512], F32)
                with contextlib.ExitStack() as st2:
                    ang = sb(st2, "ang", [128, 4, 512], F32)
                    rf = sb(st2, "rf", [128, 4, 512], F32); ri2 = sb(st2, "ri2", [128, 4, 512], I32); rm = sb(st2, "rm", [128, 4, 512], F32)
                    for c in range(4):
                        TT(V, ang.t[:], bc3(th.t[:, 4 * c:4 * c + 4], 512),
                           jidx.t[:].rearrange("p (o j) -> p o j", o=1).to_broadcast([128, 4, 512]), ALU.mult, [th, jidx], [ang])
                        sin_of(sinT.t[:, 4 * c:4 * c + 4, :], ang.t[:], None, (rf.t[:], rf), (ri2.t[:], ri2), (rm.t[:], rm), [ang], sinT)
                        sin_of(cosT.t[:, 4 * c:4 * c + 4, :], ang.t[:], None, (rf.t[:], rf), (ri2.t[:], ri2), (rm.t[:], rm), [ang], cosT, phase=math.pi / 2)
                    kb.barrier()
                pX = [psb(st, f"pX{i}", [128, 2, 512], F32) for i in range(2)]
                pY = psb(st, "pY", [128, 4, 512], F32)
                BT = sb(st, "BT", [128, 2, 4, 128], BF)
                BT3 = sb(st, "BT3", [128, 2, 4, 128], BF)
                CTp = sb(st, "CTp", [128, 16, 2, 128], BF)
                dsk = sb(st, "dsk", [128, 4], F32)
                stb = contextlib.ExitStack()
                br = sb(stb, "br", [128, 16, 16], F32); bi = sb(stb, "bi", [128, 16, 16], F32)
                bsr = sb(stb, "bsr", [128, 16, 16], F32); bsi = sb(stb, "bsi", [128, 16, 16], F32); btmp = sb(stb, "btmp", [128, 16, 16], F32)
                kb.dma('sp', br.t[:], b_re[l].rearrange("(m two) p j -> (two p) m j", two=2), (), [br])
                kb.dma('sp', bi.t[:], b_im[l].rearrange("(m two) p j -> (two p) m j", two=2), (), [bi])
                TT(V, bsr.t[:], br.t[:], bc3(fr.t[:], 16), ALU.mult, [br, fr], [bsr])
                TT(V, btmp.t[:], bi.t[:], bc3(fi.t[:], 16), ALU.mult, [bi, fi], [btmp])
                TT(V, bsr.t[:], bsr.t[:], btmp.t[:], ALU.subtract, [bsr, btmp], [bsr])
                TT(V, bsi.t[:], bi.t[:], bc3(fr.t[:], 16), ALU.mult, [bi, fr], [bsi])
                TT(V, btmp.t[:], br.t[:], bc3(fi.t[:], 16), ALU.mult, [br, fi], [btmp])
                TT(V, bsi.t[:], bsi.t[:], btmp.t[:], ALU.add, [bsi, btmp], [bsi])
                Pcat = sb(stb, "Pcat", [128, 2, 4, 4, 2, 16], F32)
                MEMSET(V, Pcat.t[:], 0.0, [Pcat])
                for ri, bs in enumerate((bsr, bsi)):
                    CP(V, Pcat.t[0:64, ri, :, :, 0, :], bs.t[0:64, :, :].rearrange("p (f m) j -> p f m j", f=4), [bs], [Pcat])
                    CP(V, Pcat.t[64:128, ri, :, :, 1, :], bs.t[64:128, :, :].rearrange("p (f m) j -> p f m j", f=4), [bs], [Pcat])
                for ri in range(2):
                    for ft in range(4):
                        TR(pY.t[:, ft, 0:128], Pcat.t[:, ri, ft].rearrange("p m t j -> p (m t j)"), identf.t[:], [Pcat, identf], [pY])
                    evac(BT.t[:, ri, :, :], pY.t[:, :, 0:128], [pY], [BT])
                MEMSET(V, BT3.t[:], 0.0, [BT3])
                CP(V, BT3.t[96:128, :, :, :], BT.t[96:128, :, :, :], [BT], [BT3])
                Cin = sb(stb, "Cin", [128, 4, 2, 128], F32)
                for ri, cc in enumerate((c_re, c_im)):
                    src = cc[l].rearrange("(f g) k p -> (g k) f p", f=4)
                    kb.dma('sp', Cin.t[:, :, ri, 0:64], src, (), [Cin], acc=True)
                    kb.dma('sp', Cin.t[:, :, ri, 64:128], src, (), [Cin], acc=True)
                TS(V, Cin.t[:, :, 1, :], Cin.t[:, :, 1, :], -1.0, None, ALU.mult, None, [Cin], [Cin])
                CTc = sb(stb, "CTc", [128, 4, 2, 128], F32)
                for ri in range(2):
                    for ft in range(4):
                        TR(pY.t[:, ft, 0:128], Cin.t[:, ft, ri, :], identf.t[:], [Cin, identf], [pY])
                    evac(CTc.t[:, :, ri, :], pY.t[:, :, 0:128], [pY], [CTc])
                MEMSET(V, CTp.t[:], 0.0, [CTp])
                for m in range(16):
                    ft, mm = m // 4, m % 4
                    e = (V, G_)[m % 2]
                    CP(e, CTp.t[0:64, m, :, 32 * mm:32 * mm + 16], CTc.t[0:64, ft, :, 32 * mm:32 * mm + 16], [CTc], [CTp])
                    CP(e, CTp.t[64:128, m, :, 32 * mm + 16:32 * mm + 32], CTc.t[64:128, ft, :, 32 * mm + 16:32 * mm + 32], [CTc], [CTp])
                kb.dma('sp', dsk.t[:], d_skip[l].rearrange("(f p) -> p f", p=128), (), [dsk])
                kb.barrier()
                stb.close()
                initR = sb(st, "initR", [128, 16], F32); initI = sb(st, "initI", [128, 16], F32)
                hlR = sb(st, "hlR", [128, 16], F32); hlI = sb(st, "hlI", [128, 16], F32)
                tcA = sb(st, "tcA", [128, 16], F32)
                G5 = sb(st, "G5", [128, 2, 16], F32)
                ysb = sb(st, "ysb", [128, 4, 512], F32)
                st = contextlib.ExitStack()
                MEMSET(V, initR.t[:], 0.0, [initR]); MEMSET(V, initI.t[:], 0.0, [initI])
                uT = [sb(st, f"suT{i}", [128, 4, 512], BF) for i in range(2)]
                w1 = [sb(st, f"w1{i}", [128, 512], F32) for i in range(2)]
                w2 = [sb(st, f"w2{i}", [128, 512], F32) for i in range(2)]
                w3 = [sb(st, f"w3{i}", [128, 512], F32) for i in range(2)]
                w4 = [sb(st, f"w4{i}", [128, 512], F32) for i in range(2)]
                xr_ = [sb(st, f"xr{i}", [128, 512], F32) for i in range(2)]
                xi_ = [sb(st, f"xi{i}", [128, 512], F32) for i in range(2)]
                gr = [sb(st, f"gr{i}", [128, 512], F32) for i in range(2)]
                gi_ = [sb(st, f"gi{i}", [128, 512], F32) for i in range(2)]
                hr = [sb(st, f"hr{i}", [128, 512], BF) for i in range(2)]
                hi = [sb(st, f"hi{i}", [128, 512], BF) for i in range(2)]
                zT = [sb(st, f"zT{i}", [128, 4, 512], BF) for i in range(2)]
                it = 0
                pend = [None]
                ks5 = os.environ.get("KS5", "main,smp").split(",")
                for ti in range(NTL if "main" in ks5 else 0):
                    ua = uT[ti % 2]
                    t0 = ti * 512
                    kb.dma('sp', ua.t[:], Us.rearrange("(f p) t -> p f t", p=128)[:, :, t0:t0 + 512], (), [ua])
                    for m in range(16):
                        a = it % 2
                        it += 1
                        ft, mm = m // 4, m % 4
                        px = pX[a]
                        for ri in range(2):
                            if mm < 3:
                                MM(px.t[:, ri, :], BT.t[32 * mm:32 * mm + 32, ri, ft, :], ua.t[32 * mm:32 * mm + 32, ft, :], True, True, [BT, ua], [px])
                            else:
                                MM(px.t[:, ri, :], BT3.t[64:128, ri, ft, :], ua.t[64:128, ft, :], True, True, [BT3, ua], [px])
                        if pend[0] is not None:
                            pend[0]()
                            pend[0] = None
                        c_, s_ = cosT.t[:, m, :], sinT.t[:, m, :]
                        TT(V, w1[a].t[:], px.t[:, 0, :], c_, ALU.mult, [px, cosT], [w1[a]])
                        TT(V, w2[a].t[:], px.t[:, 1, :], s_, ALU.mult, [px, sinT], [w2[a]])
                        TT(V, xr_[a].t[:], w1[a].t[:], w2[a].t[:], ALU.add, [w1[a], w2[a]], [xr_[a]])
                        TT(V, w3[a].t[:], px.t[:, 1, :], c_, ALU.mult, [px, cosT], [w3[a]])
                        TT(V, w4[a].t[:], px.t[:, 0, :], s_, ALU.mult, [px, sinT], [w4[a]])
                        TT(V, xi_[a].t[:], w3[a].t[:], w4[a].t[:], ALU.subtract, [w3[a], w4[a]], [xi_[a]])
                        magb = mag.t[:, m:m + 1].to_broadcast([128, 512])
                        kb.op(V, lambda: nc.vector.tensor_tensor_scan(out=gr[a].t[:], data0=magb, data1=xr_[a].t[:], initial=initR.t[:, m:m + 1], op0=ALU.mult, op1=ALU.add), [mag, xr_[a], initR], [gr[a]])
                        kb.op(V, lambda: nc.vector.tensor_tensor_scan(out=gi_[a].t[:], data0=magb, data1=xi_[a].t[:], initial=initI.t[:, m:m + 1], op0=ALU.mult, op1=ALU.add), [mag, xi_[a], initI], [gi_[a]])
                        TT(G_, w1[a].t[:], gr[a].t[:], c_, ALU.mult, [gr[a], cosT], [w1[a]])
                        TT(G_, w2[a].t[:], gi_[a].t[:], s_, ALU.mult, [gi_[a], sinT], [w2[a]])
                        TT(G_, hr[a].t[:], w1[a].t[:], w2[a].t[:], ALU.subtract, [w1[a], w2[a]], [hr[a]])
                        TT(G_, w3[a].t[:], gr[a].t[:], s_, ALU.mult, [gr[a], sinT], [w3[a]])
                        TT(G_, w4[a].t[:], gi_[a].t[:], c_, ALU.mult, [gi_[a], cosT], [w4[a]])
                        TT(G_, hi[a].t[:], w3[a].t[:], w4[a].t[:], ALU.add, [w3[a], w4[a]], [hi[a]])
                        CP(S_, G5.t[:, 0, m:m + 1], gr[a].t[:, 511:512], [gr[a]], [G5])
                        CP(S_, G5.t[:, 1, m:m + 1], gi_[a].t[:, 511:512], [gi_[a]], [G5])
                        def ymm(a=a, m=m, ft=ft, mm=mm):
                            MM(pY.t[:, ft, :], CTp.t[:, m, 0, :], hr[a].t[:], mm == 0, False, [CTp, hr[a]], [pY])
                            MM(pY.t[:, ft, :], CTp.t[:, m, 1, :], hi[a].t[:], False, mm == 3, [CTp, hi[a]], [pY])
                        pend[0] = ymm
                    pend[0]()
                    pend[0] = None
                    c5, s5 = cosT.t[:, :, 511], sinT.t[:, :, 511]
                    c1, s1 = cosT.t[:, :, 1], sinT.t[:, :, 1]
                    TT(V, tcA.t[:], G5.t[:, 1, :], s5, ALU.mult, [G5, sinT], [tcA])
                    TT(V, hlR.t[:], G5.t[:, 0, :], c5, ALU.mult, [G5, cosT], [hlR])
                    TT(V, hlR.t[:], hlR.t[:], tcA.t[:], ALU.subtract, [hlR, tcA], [hlR])
                    TT(V, tcA.t[:], G5.t[:, 1, :], c5, ALU.mult, [G5, cosT], [tcA])
                    TT(V, hlI.t[:], G5.t[:, 0, :], s5, ALU.mult, [G5, sinT], [hlI])
                    TT(V, hlI.t[:], hlI.t[:], tcA.t[:], ALU.add, [hlI, tcA], [hlI])
                    TT(V, tcA.t[:], hlI.t[:], s1, ALU.mult, [hlI, sinT], [tcA])
                    TT(V, initR.t[:], hlR.t[:], c1, ALU.mult, [hlR, cosT], [initR])
                    TT(V, initR.t[:], initR.t[:], tcA.t[:], ALU.subtract, [initR, tcA], [initR])
                    TT(V, tcA.t[:], hlI.t[:], c1, ALU.mult, [hlI, cosT], [tcA])
                    TT(V, initI.t[:], hlR.t[:], s1, ALU.mult, [hlR, sinT], [initI])
                    TT(V, initI.t[:], initI.t[:], tcA.t[:], ALU.add, [initI, tcA], [initI])
                    za = zT[ti % 2]
                    for ft in range(4):
                        STT(ysb.t[:, ft, :], ua.t[:, ft, :], dsk.t[:, ft:ft + 1], pY.t[:, ft, :], ALU.mult, ALU.add, [ua, dsk, pY], [ysb])
                    ACTF(za.t[:], ysb.t[:], AF.Gelu, [ysb], [za])
                    kb.dma('sp', Zs.rearrange("(f p) t -> p f t", p=128)[:, :, t0:t0 + 512], za.t[:], [za], ())
                kb.dma('sp', o_pre[l].rearrange("(m r) -> r m", r=128), hlR.t[:], [hlR], ())
                kb.dma('sp', o_pim[l].rearrange("(m r) -> r m", r=128), hlI.t[:], [hlI], ())
                kb.barrier()
                st.close()
                st = contextlib.ExitStack()
                if "smp" not in ks5:
                    st.close()
                    kb.barrier()
                    return
                stok = sb(st, "stok", [NS, 2, 2048], F32)
                kb.dma('sp', stok.t[:, 0, :], c_sre[l], (), [stok], acc=True)
                kb.dma('sp', stok.t[:, 1, :], c_sim[l], (), [stok], acc=True)
                px = pX[0]
                for ri in range(2):
                    for m in range(16):
                        TR(px.t[:, ri, m * 16:(m + 1) * 16], stok.t[:, ri, m * 128:(m + 1) * 128], identf.t[0:NS, 0:NS], [stok, identf], [px])
                prv = sb(st, "prv", [128, 2, 256], F32)
                evac(prv.t[:], px.t[:, :, 0:256], [px], [prv])
                for ri in range(2):
                    for m in range(16):
                        ft, mm = m // 4, m % 4
                        oc_ = (ri * 4 + ft) * 16
                        if mm < 3:
                            MM(pY.t[:, mm, oc_:oc_ + 16], BT.t[32 * mm:32 * mm + 32, ri, ft, :], uT_s.t[32 * mm:32 * mm + 32, ft, :], True, True, [BT, uT_s], [pY])
                        else:
                            MM(pY.t[:, mm, oc_:oc_ + 16], BT3.t[64:128, ri, ft, :], uT_s.t[64:128, ft, :], True, True, [BT3, uT_s], [pY])
                vA = lambda ap: ap.rearrange("p (f m b) -> p m f b", f=4, m=4, b=16)
                vB = lambda ri: pY.t[:, :, ri * 64:(ri + 1) * 64].rearrange("p m (f b) -> p m f b", f=4)
                hs = sb(st, "hs", [128, 2, 256], F32)
                hsb = sb(st, "hsb", [128, 2, 256], BF)
                q1 = sb(st, "q1", [128, 256], F32); q2 = sb(st, "q2", [128, 256], F32)
                arB, aiB = bc3(ar.t[:], 16), bc3(ai.t[:], 16)
                v3 = lambda ap: ap.rearrange("p (m b) -> p m b", b=16)
                TT(V, v3(q1.t[:]), v3(prv.t[:, 0, :]), arB, ALU.mult, [prv, ar], [q1])
                TT(V, v3(q2.t[:]), v3(prv.t[:, 1, :]), aiB, ALU.mult, [prv, ai], [q2])
                TT(V, q1.t[:], q1.t[:], q2.t[:], ALU.subtract, [q1, q2], [q1])
                TT(V, vA(hs.t[:, 0, :]), vA(q1.t[:]), vB(0), ALU.add, [q1, pY], [hs])
                TT(V, v3(q1.t[:]), v3(prv.t[:, 1, :]), arB, ALU.mult, [prv, ar], [q1])
                TT(V, v3(q2.t[:]), v3(prv.t[:, 0, :]), aiB, ALU.mult, [prv, ai], [q2])
                TT(V, q1.t[:], q1.t[:], q2.t[:], ALU.add, [q1, q2], [q1])
                TT(V, vA(hs.t[:, 1, :]), vA(q1.t[:]), vB(1), ALU.add, [q1, pY], [hs])
                CP(V, hsb.t[:], hs.t[:], [hs], [hsb])
                for ft in range(4):
                    for mm in range(4):
                        m = ft * 4 + mm
                        for ri in range(2):
                            MM(pY.t[:, ft, 0:16], CTp.t[:, m, ri, :], hsb.t[:, ri, m * 16:(m + 1) * 16], mm == 0 and ri == 0, mm == 3 and ri == 1, [CTp, hsb], [pY])
                zs = sb(st, "zs", [128, 4, NS], BF)
                for ft in range(4):
                    STT(ysb.t[:, ft, 0:16], uT_s.t[:, ft, :], dsk.t[:, ft:ft + 1], pY.t[:, ft, 0:16], ALU.mult, ALU.add, [uT_s, dsk, pY], [ysb])
                ACTF(zs.t[:], ysb.t[:, :, 0:16], AF.Gelu, [ysb], [zs])
                kb.dma('sp', Zs.rearrange("(f p) t -> p f t", p=128)[:, :, T:T + NS], zs.t[:], [zs], ())
                sout = sb(st, "sout", [NS, 2, 2048], F32)
                for ri in range(2):
                    for m in range(16):
                        TR(pY.t[0:NS, (m // 4), (m % 4) * 128:(m % 4) * 128 + 128], hs.t[:, ri, m * 16:(m + 1) * 16], identf.t[:], [hs, identf], [pY])
                    evac(sout.t[:, ri, :], pY.t[0:NS, :, :].rearrange("p f t -> p (f t)"), [pY], [sout])
                kb.dma('sp', o_sre[l], sout.t[:, 0, :], [sout], ())
                kb.dma('sp', o_sim[l], sout.t[:, 1, :], [sout], ())
                kb.barrier()
                st.close()
            kb.barrier()

        def stage_matt(l):
            with contextlib.ExitStack() as st:
                tls = [attn_tiles(st, "m0"), attn_tiles(st, "m1")]
                tl = tls[0]
                ucnt = [0]
                qt = [sb(st, f"mq{i}", [128, 4, 512], BF) for i in range(2)]
                oc = [sb(st, f"moc{i}", [128, 4, 512], BF) for i in range(2)]
                sc = 128.0 ** -0.5
                units = []
                for ti in range(NTL):
                    a = ti % 2
                    t0 = ti * 512

                    def pre(a=a, t0=t0):
                        kb.dma('sp', qt[a].t[:], XQs.rearrange("(f p) t -> p f t", p=128)[:, :, t0:t0 + 512], (), [qt[a]])

                    def post(a=a, t0=t0):
                        kb.dma('sp', OCs.rearrange("(f p) t -> p f t", p=128)[:, :, t0:t0 + 512], oc[a].t[:], [oc[a]], ())
                    for j in range(4):
                        ucnt[0] += 1
                        g = attn_unit_g(
                            tls[ucnt[0] % 2], 128, 4, 256,
                            lambda h, a=a, j=j: (qt[a].t[:, h, j * 128:(j + 1) * 128], [qt[a]]),
                            lambda h: (mkT.t[:, h, :], [mkT]),
                            lambda h, kt, kn: (mvT.t[:, kt, 128 * h:128 * h + 128], [mvT]),
                            128, sc, None, None,
                            lambda pO, a=a, j=j: CP(S_, oc[a].t[:, :, j * 128:(j + 1) * 128], pO.t[:, :, :], [pO], [oc[a]]))
                        units.append((pre if j == 0 else None, g, post if j == 3 else None))
                run_pipelined(units)
                mk = [sb(st, f"smk{i}", [128, 2, 512], F32) for i in range(2)]
                mv = [sb(st, f"smv{i}", [128, 2, 512], F32) for i in range(2)]
                mvb = [sb(st, f"smvb{i}", [128, 2, 512], BF) for i in range(2)]
                mkTs = [sb(st, f"smkT{i}", [128, 4, 256], BF) for i in range(2)]
                ptr = tls[1][0]
                ocs = sb(st, "ocs", [128, 4, NS], BF)
                for b in range(NS):
                    a = b % 2
                    kb.dma('sp', mk[a].t[:], c_mk[l, b].rearrange("(j p) c -> p j c", p=128), (), [mk[a]])
                    kb.dma('sp', mv[a].t[:], c_mv[l, b].rearrange("(j p) c -> p j c", p=128), (), [mv[a]])
                    for h in range(4):
                        for j in range(2):
                            TR(ptr.t[:, h, j * 128:(j + 1) * 128], mk[a].t[:, j, h * 128:(h + 1) * 128], identf.t[:], [mk[a], identf], [ptr])
                    evac(mkTs[a].t[:], ptr.t[:], [ptr], [mkTs[a]])
                    CP(G_, mvb[a].t[:], mv[a].t[:], [mv[a]], [mvb[a]])
                    attn_unit(
                        tl, 1, 4, 256,
                        lambda h: (xqT_s.t[:, h, b:b + 1], [xqT_s]),
                        lambda h: (mkTs[a].t[:, h, :], [mkTs[a]]),
                        lambda h, kt, kn: (mvb[a].t[:, kt, 128 * h:128 * h + 128], [mvb[a]]),
                        128, sc, None, None,
                        lambda pO: evac(ocs.t[:, :, b:b + 1], pO.t[:, :, 0:1], [pO], [ocs]))
                kb.dma('sp', OCs.rearrange("(f p) t -> p f t", p=128)[:, :, T:T + NS], ocs.t[:], [ocs], ())
            kb.barrier()

        def stage_mrg(l):
            with contextlib.ExitStack() as st:
                wa = sb(st, "wa", [64, 8, D], BF)
                for h in range(8):
                    kb.dma('pool', wa.t[:, h, :], w_bra[l, h * 64:(h + 1) * 64, :], (), [wa], acc=True)
                wga = sb(st, "wga", [128, 4, D], BF); wgb = sb(st, "wgb", [128, 4, D], BF)
                wc = sb(st, "wc", [128, 4, D], BF); wo = sb(st, "wo", [128, 8, D], BF)
                load_w_begin(st)
                load_w(wga, lambda kt: w_ga[l, kt * 128:(kt + 1) * 128, :], 4, D)
                load_w(wgb, lambda kt: w_gb[l, kt * 128:(kt + 1) * 128, :], 4, D)
                load_w(wc, lambda kt: w_brm[l, kt * 128:(kt + 1) * 128, :], 4, D)
                load_w(wo, lambda kt: w_out[l, kt * 128:(kt + 1) * 128, :], 8, D)
                load_w_end()
                oa = [sb(st, f"goa{i}", [64, 8, 512], BF) for i in range(2)]
                zt = [sb(st, f"gz{i}", [128, 4, 512], BF) for i in range(2)]
                oc = [sb(st, f"goc{i}", [128, 4, 512], BF) for i in range(2)]
                sg = [sb(st, "gsg0", [128, 24, 512], BF)] * 2
                xt = [sb(st, "gx0", [128, 8, 512], F32)] * 2
                mg = sb(st, "mg", [128, 8, 512], BF)
                m1s = [sb(st, f"m1{i}", [128, 512], F32) for i in range(2)]
                m2s = [sb(st, f"m2{i}", [128, 512], F32) for i in range(2)]
                m3s = [sb(st, f"m3{i}", [128, 512], F32) for i in range(2)]
                pa = [psb(st, f"gpa{i}", [128, 512], F32) for i in range(8)]
                XTv = XT.rearrange("(k p) t -> p k t", p=128)
                pi = 0
                for ti, (t0, n) in enumerate(tiles):
                    a = ti % 2
                    kb.dma('sp', oa[a].t[:, :, 0:n], OAs.rearrange("(h p) t -> p h t", p=64)[:, :, t0:t0 + n], (), [oa[a]])
                    kb.dma('sp', zt[a].t[:, :, 0:n], Zs.rearrange("(f p) t -> p f t", p=128)[:, :, t0:t0 + n], (), [zt[a]])
                    kb.dma('sp', oc[a].t[:, :, 0:n], OCs.rearrange("(f p) t -> p f t", p=128)[:, :, t0:t0 + n], (), [oc[a]])
                    kb.dma('sp', sg[a].t[:, :, 0:n], SGs.rearrange("(f p) t -> p f t", p=128)[:, :, t0:t0 + n], (), [sg[a]])
                    kb.dma('sp', xt[a].t[:, :, 0:n], XTv[:, :, t0:t0 + n], (), [xt[a]])
                    for dm in range(8):
                        cs_ = slice(dm * 128, (dm + 1) * 128)
                        m1, m2, m3 = m1s[dm % 2], m2s[dm % 2], m3s[dm % 2]
                        pA, pGa, pGb, pC = pa[pi % 8], pa[(pi + 1) % 8], pa[(pi + 2) % 8], pa[(pi + 3) % 8]
                        pi += 4
                        for h in range(8):
                            MM(pA.t[:, 0:n], wa.t[:, h, cs_], oa[a].t[:, h, 0:n], h == 0, h == 7, [wa, oa[a]], [pA])
                        for k in range(4):
                            MM(pGa.t[:, 0:n], wga.t[:, k, cs_], zt[a].t[:, k, 0:n], k == 0, k == 3, [wga, zt[a]], [pGa])
                        for k in range(4):
                            MM(pGb.t[:, 0:n], wgb.t[:, k, cs_], zt[a].t[:, k, 0:n], k == 0, k == 3, [wgb, zt[a]], [pGb])
                        for k in range(4):
                            MM(pC.t[:, 0:n], wc.t[:, k, cs_], oc[a].t[:, k, 0:n], k == 0, k == 3, [wc, oc[a]], [pC])
                        TT(V, m1.t[:, 0:n], pA.t[:, 0:n], sg[a].t[:, dm, 0:n], ALU.mult, [pA, sg[a]], [m1])
                        ACTF(m2.t[:, 0:n], pGb.t[:, 0:n], AF.Sigmoid, [pGb], [m2])
                        TT(V, m2.t[:, 0:n], pGa.t[:, 0:n], m2.t[:, 0:n], ALU.mult, [pGa, m2], [m2])
                        TT(G_, m2.t[:, 0:n], m2.t[:, 0:n], sg[a].t[:, 8 + dm, 0:n], ALU.mult, [m2, sg[a]], [m2])
                        TT(V, m3.t[:, 0:n], pC.t[:, 0:n], sg[a].t[:, 16 + dm, 0:n], ALU.mult, [pC, sg[a]], [m3])
                        TT(G_, m1.t[:, 0:n], m1.t[:, 0:n], m2.t[:, 0:n], ALU.add, [m1, m2], [m1])
                        TT(G_, mg.t[:, dm, 0:n], m1.t[:, 0:n], m3.t[:, 0:n], ALU.add, [m1, m3], [mg])
                    for dm in range(8):
                        pO = pa[pi % 8]
                        pi += 1
                        for k in range(8):
                            MM(pO.t[:, 0:n], wo.t[:, k, dm * 128:(dm + 1) * 128], mg.t[:, k, 0:n], k == 0, k == 7, [wo, mg], [pO])
                        TT(V, xt[a].t[:, dm, 0:n], xt[a].t[:, dm, 0:n], pO.t[:, 0:n], ALU.add, [xt[a], pO], [xt[a]])
                    kb.dma('sp', XTv[:, :, t0:t0 + n], xt[a].t[:, :, 0:n], [xt[a]], ())
            kb.barrier()

        def stage_f1(l, half):
            HF = 11
            F0 = half * HF
            C0 = F0 * 128
            CW = HF * 128
            with contextlib.ExitStack() as st:
                wg = sb(st, "wg", [128, 8, CW], BF); wu = sb(st, "wu", [128, 8, CW], BF)
                load_w_begin(st)
                load_w(wg, lambda kt: w_fg[l, kt * 128:(kt + 1) * 128, C0:C0 + CW], 8, CW)
                load_w(wu, lambda kt: w_fu[l, kt * 128:(kt + 1) * 128, C0:C0 + CW], 8, CW)
                load_w_end()
                gf = sb(st, "gffn", [128, 8], F32)
                kb.dma('sp', gf.t[:], g_ffn[l].rearrange("(k p) -> p k", p=128), (), [gf])
                cw = sb(st, "cw", [128, 3, HF], F32); cb = sb(st, "cb", [128, HF], F32)
                for j3 in range(3):
                    kb.dma('sp', cw.t[:, j3, :], conv_w[l, j3, C0:C0 + CW].rearrange("(f p) -> p f", p=128), (), [cw], acc=True)
                kb.dma('sp', cb.t[:], conv_b[l, C0:C0 + CW].rearrange("(f p) -> p f", p=128), (), [cb])
                xt = [sb(st, "fx0", [128, 8, 512], F32)] * 2
                hT = [sb(st, f"fh{i}", [128, 8, 512], BF) for i in range(2)]
                sq = sb(st, "fsq", [128, 8, 512], BF); rt = sb(st, "frt", [128, 512], F32)
                pss = psb(st, "fpss", [128, 512], F32)
                pg = [psb(st, f"fpg{i}", [128, 512], F32) for i in range(3)]
                pu = [psb(st, f"fpu{i}", [128, 512], F32) for i in range(3)]
                apad = sb(st, "apad", [128, HF, 514], F32)
                MEMSET(V, apad.t[:, :, 0:2], 0.0, [apad])
                apb = [Buf(apad.t) for _ in range(HF)]
                for b_ in apb:
                    b_.w = dict(apad.w)
                cvs = [sb(st, f"cv{i}", [128, 512], F32) for i in range(2)]
                cgs = [sb(st, f"cg{i}", [128, 512], F32) for i in range(2)]
                gout = [sb(st, "gout0", [128, HF, 512], BF)] * 2
                cvt = sb(st, "cvt", [NS, 2, CW], F32)
                kb.dma('sp', cvt.t[:], c_cv[l, :, :, C0:C0 + CW], (), [cvt])
                if half == 0:
                    kb.dma('sp', o_scv[l, :, 0, :], c_cv[l, :, 1, :], (), ())
                sprev = sb(st, "sprev", [128, 2, HF, NS], F32)
                for j in range(2):
                    for f in range(HF):
                        TR(pg[j].t[:, f * 16:(f + 1) * 16], cvt.t[:, j, f * 128:(f + 1) * 128], identf.t[0:NS, 0:NS], [cvt, identf], [pg[j]])
                    evac(sprev.t[:, j, :, :], pg[j].t[:, 0:HF * 16].rearrange("p (f b) -> p f b", b=NS), [pg[j]], [sprev])
                asmp = sb(st, "asmp", [128, HF, NS], F32)
                XTv = XT.rearrange("(k p) t -> p k t", p=128)
                pi = 0
                def prol(ti):
                    t0_, n_ = tiles[ti]
                    kb.dma('sp', xt[ti % 2].t[:, :, 0:n_], XTv[:, :, t0_:t0_ + n_], (), [xt[ti % 2]])
                    rmsnorm_tile(xt[ti % 2], n_, gf, hT[ti % 2], sq, rt, pss)
                prol(0)
                for ti, (t0, n) in enumerate(tiles):
                    a = ti % 2
                    smp = n == NS
                    if ti + 1 < len(tiles):
                        prol(ti + 1)
                    go = gout[a]
                    for f in range(HF):
                        p1, p2 = pg[pi % 3], pu[pi % 3]
                        pi += 1
                        for k in range(8):
                            MM(p1.t[:, 0:n], wg.t[:, k, f * 128:(f + 1) * 128], hT[a].t[:, k, 0:n], k == 0, k == 7, [wg, hT[a]], [p1])
                        for k in range(8):
                            MM(p2.t[:, 0:n], wu.t[:, k, f * 128:(f + 1) * 128], hT[a].t[:, k, 0:n], k == 0, k == 7, [wu, hT[a]], [p2])
                        w0, w1_, w2_ = cw.t[:, 0, f:f + 1], cw.t[:, 1, f:f + 1], cw.t[:, 2, f:f + 1]
                        cv, cg = cvs[f % 2], cgs[f % 2]
                        ab = apb[f]
                        if not smp:
                            CP(S_, apad.t[:, f, 2:514], p1.t[:, :], [p1], [ab])
                            TS(V, cv.t[:], apad.t[:, f, 0:512], w0, cb.t[:, f:f + 1], ALU.mult, ALU.add, [ab, cw, cb], [cv])
                            STT(cv.t[:], apad.t[:, f, 1:513], w1_, cv.t[:], ALU.mult, ALU.add, [ab, cw, cv], [cv])
                            STT(cv.t[:], apad.t[:, f, 2:514], w2_, cv.t[:], ALU.mult, ALU.add, [ab, cw, cv], [cv])
                            ACTF(cg.t[:], cv.t[:], AF.Gelu, [cv], [cg])
                            TT(V, go.t[:, f, :], cg.t[:], p2.t[:, :], ALU.mult, [cg, p2], [go])
                            CP(G_, apad.t[:, f, 0:2], apad.t[:, f, 512:514], [ab], [ab])
                        else:
                            CP(S_, asmp.t[:, f, :], p1.t[:, 0:n], [p1], [asmp])
                            TS(V, cv.t[:, 0:n], sprev.t[:, 0, f, :], w0, cb.t[:, f:f + 1], ALU.mult, ALU.add, [sprev, cw, cb], [cv])
                            STT(cv.t[:, 0:n], sprev.t[:, 1, f, :], w1_, cv.t[:, 0:n], ALU.mult, ALU.add, [sprev, cw, cv], [cv])
                            STT(cv.t[:, 0:n], asmp.t[:, f, :], w2_, cv.t[:, 0:n], ALU.mult, ALU.add, [asmp, cw, cv], [cv])
                            ACTF(cg.t[:, 0:n], cv.t[:, 0:n], AF.Gelu, [cv], [cg])
                            TT(V, go.t[:, f, 0:n], cg.t[:, 0:n], p2.t[:, 0:n], ALU.mult, [cg, p2], [go])
                    kb.dma('sp', Gs.rearrange("(f p) t -> p f t", p=128)[:, F0:F0 + HF, t0:t0 + n], go.t[:, :, 0:n], [go], ())
                pbs = (pg[0], pg[1], pg[2])
                pcs = sb(st, "pcs", [2, CW], F32)
                for f in range(HF):
                    TR(pbs[f // 4].t[0:2, (f % 4) * 128:(f % 4) * 128 + 128], apad.t[:, f, 0:2], identf.t[:], [apb[f], identf], list(pbs))
                for gi, pb in enumerate(pbs):
                    w_ = 512 if gi < 2 else CW - 1024
                    evac(pcs.t[:, gi * 512:gi * 512 + w_], pb.t[0:2, 0:w_], [pb], [pcs])
                kb.dma('sp', o_pcv[l, :, C0:C0 + CW], pcs.t[:], [pcs], ())
                scs = sb(st, "scs", [NS, CW], F32)
                pbu = (pu[0], pu[1], pu[2])
                for f in range(HF):
                    TR(pbu[f // 4].t[0:NS, (f % 4) * 128:(f % 4) * 128 + 128], asmp.t[:, f, :], identf.t[:], [asmp, identf], list(pbu))
                for gi, pb in enumerate(pbu):
                    w_ = 512 if gi < 2 else CW - 1024
                    evac(scs.t[:, gi * 512:gi * 512 + w_], pb.t[0:NS, 0:w_], [pb], [scs])
                kb.dma('sp', o_scv[l, :, 1, C0:C0 + CW], scs.t[:], [scs], ())
            kb.barrier()

        def stage_f2(l):
            with contextlib.ExitStack() as st:
                wd = sb(st, "wd", [128, NFT, D], BF)
                load_w_begin(st)
                load_w(wd, lambda kt: w_fd[l, kt * 128:(kt + 1) * 128, :], NFT, D)
                load_w_end()
                gt = [sb(st, "dg0", [128, NFT, 512], BF)] * 2
                xt = [sb(st, "dx0", [128, 8, 512], F32)] * 2
                pd = [psb(st, f"dpd{i}", [128, 512], F32) for i in range(4)]
                XTv = XT.rearrange("(k p) t -> p k t", p=128)
                pi = 0
                for ti, (t0, n) in enumerate(tiles):
                    a = ti % 2
                    kb.dma('sp', gt[a].t[:, :, 0:n], Gs.rearrange("(f p) t -> p f t", p=128)[:, :, t0:t0 + n], (), [gt[a]])
                    kb.dma('sp', xt[a].t[:, :, 0:n], XTv[:, :, t0:t0 + n], (), [xt[a]])
                    for dm in range(8):
                        p = pd[pi % 4]
                        pi += 1
                        for k in range(NFT):
                            MM(p.t[:, 0:n], wd.t[:, k, dm * 128:(dm + 1) * 128], gt[a].t[:, k, 0:n], k == 0, k == NFT - 1, [wd, gt[a]], [p])
                        TT(V, xt[a].t[:, dm, 0:n], xt[a].t[:, dm, 0:n], p.t[:, 0:n], ALU.add, [xt[a], p], [xt[a]])
                    kb.dma('sp', XTv[:, :, t0:t0 + n], xt[a].t[:, :, 0:n], [xt[a]], ())
            kb.barrier()

        def stage_fin():
            with contextlib.ExitStack() as st:
                gfn = sb(st, "gfin", [128, 8], F32)
                kb.dma('sp', gfn.t[:], g_fin[0].rearrange("(k p) -> p k", p=128), (), [gfn])
                xt = [sb(st, f"nx{i}", [128, 8, 512], F32) for i in range(2)]
                yo = [sb(st, f"ny{i}", [128, 8, 512], F32) for i in range(2)]
                sq = sb(st, "nsq", [128, 8, 512], BF); rt = sb(st, "nrt", [128, 512], F32)
                pss = psb(st, "npss", [128, 512], F32)
                pt = [psb(st, f"npt{i}", [128, 1024], F32) for i in range(2)]
                yt = [sb(st, f"nyt{i}", [128, D], F32) for i in range(2)]
                XTv = XT.rearrange("(k p) t -> p k t", p=128)
                bi = 0
                for ti, (t0, n) in enumerate(tiles):
                    a = ti % 2
                    kb.dma('sp', xt[a].t[:, :, 0:n], XTv[:, :, t0:t0 + n], (), [xt[a]])
                    rmsnorm_tile(xt[a], n, gfn, yo[a], sq, rt, pss)
                    for j in range((n + 127) // 128):
                        m = min(128, n - j * 128)
                        b2 = bi % 2
                        bi += 1
                        for k in range(8):
                            TR(pt[b2].t[0:m, k * 128:(k + 1) * 128], yo[a].t[:, k, j * 128:j * 128 + m], identf.t[:], [yo[a], identf], [pt[b2]])
                        evac(yt[b2].t[0:m, :], pt[b2].t[0:m, :], [pt[b2]], [yt[b2]])
                        if n == NS:
                            kb.dma('sp', o_ys[:, :], yt[b2].t[0:m, :], [yt[b2]], ())
                        else:
                            kb.dma('sp', o_yp[t0 + j * 128:t0 + j * 128 + 128, :], yt[b2].t[:], [yt[b2]], ())
            kb.barrier()

        import os
        dbg = os.environ.get("KSTAGES", "mem,in,att,s5,matt,mrg,f1,f2").split(",")
        nl = int(os.environ.get("KLAYERS", str(DEPTH)))
        stage_p0()
        for l in range(nl):
            if "mem" in dbg: stage_mem(l)
            if "in" in dbg: stage_in(l)
            if "att" in dbg: stage_att(l)
            if "s5" in dbg: stage_s5(l)
            if "matt" in dbg: stage_matt(l)
            if "mrg" in dbg: stage_mrg(l)
            if "f1" in dbg:
                stage_f1(l, 0)
                stage_f1(l, 1)
            if "f2" in dbg: stage_f2(l)
        stage_fin()
        kb.barrier()
    return nc


T_FULL = 4096
_cache = {}


def _in_maps(inp, T):
    maps = []
    for c in range(8):
        b = c % 4
        sl = slice(NS * c, NS * (c + 1))
        m = {
            "xp": np.ascontiguousarray(inp["x_prompt"][b, :T]),
            "xs": np.ascontiguousarray(inp["x_sample"][sl, 0]),
            "mem": np.ascontiguousarray(inp["mem_prompt"][b]),
            "c_swk": np.ascontiguousarray(inp["cache_swa_k"][:, sl].reshape(DEPTH, NS, 128, 128)),
            "c_swv": np.ascontiguousarray(inp["cache_swa_v"][:, sl].reshape(DEPTH, NS, 128, 128)),
            "c_sre": np.ascontiguousarray(inp["state_ssm_re"][:, sl].reshape(DEPTH, NS, 2048)),
            "c_sim": np.ascontiguousarray(inp["state_ssm_im"][:, sl].reshape(DEPTH, NS, 2048)),
            "c_cv": np.ascontiguousarray(inp["cache_ffn_conv"][:, sl]),
            "c_mk": np.ascontiguousarray(inp["cache_mem_k"][:, sl].reshape(DEPTH, NS, 256, 512)),
            "c_mv": np.ascontiguousarray(inp["cache_mem_v"][:, sl].reshape(DEPTH, NS, 256, 512)),
            "norm_final_g": np.ascontiguousarray(inp["norm_final_g"].reshape(1, D)),
        }
        for k in ("norm_mix_g", "norm_ffn_g", "norm_mem_g", "w_in", "sinks", "w_br_attn", "lam_re", "lam_im", "log_dt",
                  "b_re", "b_im", "c_re", "c_im", "d_skip", "w_glu_a", "w_glu_b", "w_mem_kv", "w_br_mem", "w_out",
                  "w_ffn_gate", "w_ffn_up", "conv_w", "conv_b", "w_ffn_down"):
            m[k] = np.ascontiguousarray(inp[k])
        maps.append(m)
    return maps


def run(inp, T):
    if T not in _cache:
        _cache[T] = build(T)
    nc = _cache[T]
    inp = {k: np.asarray(v, dtype=np.float32) for k, v in inp.items()}
    import os
    ncores = int(os.environ.get("KCORES", "8"))
    res = run_bass_kernel_spmd(nc, _in_maps(inp, T)[:ncores], core_ids=list(range(ncores))).results
    res = [res[c % ncores] for c in range(8)]
    P = lambda name: np.stack([res[b][name] for b in range(4)], axis=1)
    Sm = lambda name: np.concatenate([res[c][name] for c in range(8)], axis=1)
    y_p = np.stack([res[b]["o_yp"] for b in range(4)], axis=0)
    y_s = np.concatenate([res[c]["o_ys"] for c in range(8)], axis=0)[:, None, :]
    return (y_p, y_s,
            P("o_pk").reshape(DEPTH, 4, 128, 2, 64), P("o_pv").reshape(DEPTH, 4, 128, 2, 64),
            P("o_pre").reshape(DEPTH, 4, 32, 64), P("o_pim").reshape(DEPTH, 4, 32, 64),
            P("o_pcv"), P("o_pmk").reshape(DEPTH, 4, 256, 4, 128), P("o_pmv").reshape(DEPTH, 4, 256, 4, 128),
            Sm("o_sk").reshape(DEPTH, 128, 128, 2, 64), Sm("o_sv").reshape(DEPTH, 128, 128, 2, 64),
            Sm("o_sre").reshape(DEPTH, 128, 32, 64), Sm("o_sim").reshape(DEPTH, 128, 32, 64), Sm("o_scv"))


def kernel(**inputs):
    return run(inputs, T_FULL)
```

```python
import contextlib
import math
import os
import numpy as np
import concourse.bass as bass
import concourse.mybir as mybir
from concourse.bass_utils import run_bass_kernel_spmd

F32 = mybir.dt.float32
BF = mybir.dt.bfloat16
I32 = mybir.dt.int32
AF = mybir.ActivationFunctionType
ALU = mybir.AluOpType
AX = mybir.AxisListType

D = 1024
DEPTH = 4
NS = 16
DFF = 2816
NFT = 22
INC = 4864
EPS = 1e-6
TWO_PI = 2.0 * math.pi


class Buf:
    def __init__(self, t, psum=False):
        self.t = t
        self.w = {}
        self.r = {}
        self.psum = psum


class KB:
    def __init__(self, nc, es):
        self.nc = nc
        self.es = es
        self.E = {'pe': nc.tensor, 'act': nc.scalar, 'dve': nc.vector, 'pool': nc.gpsimd, 'sp': nc.sync}
        self.cur = {}
        self.cnt = {}
        self.nsem = 0
        self.waited = {e: {} for e in self.E}
        for e in self.E:
            self._newesem(e)
        self.dq = {q: [[self._sem(), 0] for _ in range(8)] for q in ('sp', 'pool', 'act')}
        self.dqi = {q: 0 for q in self.dq}

    def _sem(self):
        self.nsem += 1
        return self.es.enter_context(self.nc.semaphore(f"sm{self.nsem}"))

    def _newesem(self, e):
        self.cur[e] = self._sem()
        self.cnt[e] = 0

    def wait(self, e, sem, val):
        k = id(sem)
        if self.waited[e].get(k, 0) >= val:
            return
        self.E[e].wait_ge(sem, val)
        self.waited[e][k] = val

    def deps(self, e, r, w, acc=False):
        own = self.cur[e]
        for b in r:
            for sem, val in b.w.values():
                if e == 'pe' and sem is own:
                    continue
                self.wait(e, sem, val)
            if b.psum:
                for sem, val in b.r.values():
                    if sem is not own:
                        self.wait(e, sem, val)
        if acc:
            return
        for b in w:
            for sem, val in list(b.w.values()) + list(b.r.values()):
                if e == 'pe' and sem is own:
                    continue
                self.wait(e, sem, val)

    def mark(self, sem, val, r, w, acc=False):
        for b in r:
            b.r[id(sem)] = (sem, val)
        for b in w:
            if acc:
                b.w[id(sem)] = (sem, val)
            else:
                b.w = {id(sem): (sem, val)}
                b.r = {}

    def op(self, e, fn, r=(), w=(), acc=False):
        self.deps(e, r, w, acc)
        ins = fn()
        self.cnt[e] += 1
        ins.then_inc(self.cur[e], 1)
        self.mark(self.cur[e], self.cnt[e], r, w, acc)

    def dma(self, q, out, in_, r=(), w=(), acc=False):
        slot = self.dq[q][self.dqi[q] % 8]
        self.dqi[q] += 1
        self.wait(q, slot[0], slot[1])
        self.deps(q, r, w, acc)
        self.E[q].dma_start(out=out, in_=in_).then_inc(slot[0], 16)
        slot[1] += 16
        self.mark(slot[0], slot[1], r, w, acc)

    def barrier(self):
        for e in self.E:
            for e2 in self.E:
                if e2 != e and self.cnt[e2] > 0:
                    self.wait(e, self.cur[e2], self.cnt[e2])
            for q in self.dq:
                for sem, val in self.dq[q]:
                    if val > 0:
                        self.wait(e, sem, val)
        for e in self.E:
            if self.cnt[e] > 20000:
                self._newesem(e)


def build(T):
    assert T % 512 == 0
    NTL = T // 512
    TA = T + NS
    nc = bass.Bass("TRN2", target_bir_lowering=False)

    def din(name, shape, dt=F32):
        return nc.dram_tensor(name, list(shape), dt, kind="ExternalInput").ap()

    def dout(name, shape):
        return nc.dram_tensor(name, list(shape), F32, kind="ExternalOutput").ap()

    def dscr(name, shape, dt):
        return nc.dram_tensor(name, list(shape), dt, kind="Internal").ap()

    xp = din("xp", [T, D]); xs = din("xs", [NS, D]); mem = din("mem", [256, D])
    c_swk = din("c_swk", [DEPTH, NS, 128, 128]); c_swv = din("c_swv", [DEPTH, NS, 128, 128])
    c_sre = din("c_sre", [DEPTH, NS, 2048]); c_sim = din("c_sim", [DEPTH, NS, 2048])
    c_cv = din("c_cv", [DEPTH, NS, 2, DFF])
    c_mk = din("c_mk", [DEPTH, NS, 256, 512]); c_mv = din("c_mv", [DEPTH, NS, 256, 512])
    g_mix = din("norm_mix_g", [DEPTH, D]); g_ffn = din("norm_ffn_g", [DEPTH, D])
    g_mem = din("norm_mem_g", [DEPTH, D]); g_fin = din("norm_final_g", [1, D])
    w_in = din("w_in", [DEPTH, D, INC]); sinks = din("sinks", [DEPTH, 8])
    w_bra = din("w_br_attn", [DEPTH, 512, D])
    lam_re = din("lam_re", [DEPTH, 32, 64]); lam_im = din("lam_im", [DEPTH, 32, 64])
    log_dt = din("log_dt", [DEPTH, 32])
    b_re = din("b_re", [DEPTH, 32, 64, 16]); b_im = din("b_im", [DEPTH, 32, 64, 16])
    c_re = din("c_re", [DEPTH, 32, 16, 64]); c_im = din("c_im", [DEPTH, 32, 16, 64])
    d_skip = din("d_skip", [DEPTH, 512])
    w_ga = din("w_glu_a", [DEPTH, 512, D]); w_gb = din("w_glu_b", [DEPTH, 512, D])
    w_mkv = din("w_mem_kv", [DEPTH, D, 1024]); w_brm = din("w_br_mem", [DEPTH, 512, D])
    w_out = din("w_out", [DEPTH, D, D])
    w_fg = din("w_ffn_gate", [DEPTH, D, DFF]); w_fu = din("w_ffn_up", [DEPTH, D, DFF])
    conv_w = din("conv_w", [DEPTH, 3, DFF]); conv_b = din("conv_b", [DEPTH, DFF])
    w_fd = din("w_ffn_down", [DEPTH, DFF, D])
    o_yp = dout("o_yp", [T, D]); o_ys = dout("o_ys", [NS, D])
    o_pk = dout("o_pk", [DEPTH, 128, 128]); o_pv = dout("o_pv", [DEPTH, 128, 128])
    o_pre = dout("o_pre", [DEPTH, 2048]); o_pim = dout("o_pim", [DEPTH, 2048])
    o_pcv = dout("o_pcv", [DEPTH, 2, DFF])
    o_pmk = dout("o_pmk", [DEPTH, 256, 512]); o_pmv = dout("o_pmv", [DEPTH, 256, 512])
    o_sk = dout("o_sk", [DEPTH, NS, 128, 128]); o_sv = dout("o_sv", [DEPTH, NS, 128, 128])
    o_sre = dout("o_sre", [DEPTH, NS, 2048]); o_sim = dout("o_sim", [DEPTH, NS, 2048])
    o_scv = dout("o_scv", [DEPTH, NS, 2, DFF])
    XT = dscr("XT", [D, TA], F32)
    Qs = dscr("Qs", [8, 64, TA], BF)
    Ks = dscr("Ks", [2, 64, 128 + TA], BF)
    Vs = dscr("Vs", [128 + T, 128], BF)
    Us = dscr("Us", [512, TA], BF)
    XQs = dscr("XQs", [512, TA], BF)
    SGs = dscr("SGs", [3072, TA], BF)
    OAs = dscr("OAs", [512, TA], BF)
    Zs = dscr("Zs", [512, TA], BF)
    OCs = dscr("OCs", [512, TA], BF)
    Gs = dscr("Gs", [DFF, TA], BF)
    VNs = dscr("VNs", [NS, 128], F32)

    tiles = [(i * 512, 512) for i in range(NTL)] + [(T, NS)]

    with contextlib.ExitStack() as es:
        es.enter_context(nc.allow_non_contiguous_dma(reason="small parameter tables"))
        kb = KB(nc, es)

        uid = [0]

        def sb(st, name, shape, dt):
            uid[0] += 1
            return Buf(st.enter_context(nc.sbuf_tensor(f"{name}_{uid[0]}", list(shape), dt)))

        def sbn(st, name, shape, dt, n):
            t = st.enter_context(nc.sbuf_tensor(name, list(shape), dt))
            return t, [Buf(t) for _ in range(n)]

        def psb(st, name, shape, dt):
            uid[0] += 1
            nb = int(np.prod(shape[1:])) * (2 if dt == BF else 4)
            assert nb % 2048 == 0, (name, shape)
            return Buf(st.enter_context(nc.psum_tensor(f"{name}_{uid[0]}", list(shape), dt)), psum=True)

        V, S_, G_, PE = 'dve', 'act', 'pool', 'pe'

        def TT(e, out, in0, in1, op, r, w):
            kb.op(e, lambda: kb.E[e].tensor_tensor(out=out, in0=in0, in1=in1, op=op), r, w)

        def TS(e, out, in0, s1, s2, op0, op1, r, w):
            if op1 is None:
                kb.op(e, lambda: kb.E[e].tensor_scalar(out=out, in0=in0, scalar1=s1, scalar2=None, op0=op0), r, w)
            else:
                kb.op(e, lambda: kb.E[e].tensor_scalar(out=out, in0=in0, scalar1=s1, scalar2=s2, op0=op0, op1=op1), r, w)

        def STT(out, in0, sc, in1, op0, op1, r, w):
            kb.op(V, lambda: nc.vector.scalar_tensor_tensor(out=out, in0=in0, scalar=sc, in1=in1, op0=op0, op1=op1), r, w)

        def ACTF(out, in_, func, r, w, bias=None, scale=None, accum=None):
            kw = {}
            if bias is not None:
                kw['bias'] = bias
            if scale is not None:
                kw['scale'] = scale
            if accum is not None:
                kw['accum_out'] = accum
            kb.op(S_, lambda: nc.scalar.activation(out=out, in_=in_, func=func, **kw), r, w)

        def CP(e, out, in_, r, w, acc=False):
            if e == S_:
                kb.op(e, lambda: nc.scalar.copy(out=out, in_=in_), r, w, acc)
            else:
                kb.op(e, lambda: kb.E[e].tensor_copy(out=out, in_=in_), r, w, acc)

        def MM(out, lhsT, rhs, start, stop, r, w):
            kb.op(PE, lambda: nc.tensor.matmul(out, lhsT, rhs, start=start, stop=stop), r, w)

        def TR(out, in_, ident, r, w):
            kb.op(PE, lambda: nc.tensor.transpose(out, in_, ident), r, w)

        def MEMSET(e, ap, val, w):
            kb.op(e, lambda: kb.E[e].memset(ap, val), (), w)

        wst = {"bufs": None, "i": 0}

        def load_w_begin(st):
            wst["bufs"] = [sb(st, f"wstg{i}", [128, 2048], F32) for i in range(2)]

        def load_w(dst, dram_rows_fn, nkt, ncols, q='pool'):
            for kt in range(nkt):
                src_ = dram_rows_fn(kt)
                for c0 in range(0, ncols, 2048):
                    c1 = min(ncols, c0 + 2048)
                    wst["n"] = wst.get("n", 0) + 1
                    if wst["bufs"] is None or wst["n"] % 3 != 0:
                        kb.dma('pool', dst.t[:, kt, c0:c1], src_[:, c0:c1], (), [dst], acc=True)
                        continue
                    sg_ = wst["bufs"][wst["i"] % 2]
                    e = (S_, V)[wst["i"] % 2]
                    wst["i"] += 1
                    kb.dma('sp', sg_.t[:, 0:c1 - c0], src_[:, c0:c1], (), [sg_])
                    CP(e, dst.t[:, kt, c0:c1], sg_.t[:, 0:c1 - c0], [sg_], [dst], acc=True)

        def load_w_end():
            wst["bufs"] = None

        evac_rr = [0]

        def evac(out, in_, r, w):
            e = (S_, V)[evac_rr[0] % 2]
            evac_rr[0] += 1
            CP(e, out, in_, r, w)

        cst = es
        ident_i = sb(cst, "ident_i", [128, 128], I32)
        identf = sb(cst, "identf", [128, 128], F32)
        identb = sb(cst, "identb", [128, 128], BF)
        onesf = sb(cst, "onesf", [128, 128], F32)
        kb.op(G_, lambda: nc.gpsimd.iota(ident_i.t[:], pattern=[[1, 128]], base=0, channel_multiplier=-1), (), [ident_i])
        TS(V, identf.t[:], ident_i.t[:], 0, None, ALU.is_equal, None, [ident_i], [identf])
        CP(V, identb.t[:], identf.t[:], [identf], [identb])
        MEMSET(V, onesf.t[:], 1.0, [onesf])
        onesb = sb(cst, "onesb", [128, 128], BF)
        MEMSET(V, onesb.t[:], 1.0, [onesb])
        dist_i = sb(cst, "dist_i", [128, 256], I32)
        distf = sb(cst, "distf", [128, 256], F32)
        mskf = sb(cst, "mskf", [128, 256], F32)
        msk2 = sb(cst, "msk2", [128, 256], F32)
        biasA = sb(cst, "biasA", [128, 8, 256], F32)
        biasA0 = sb(cst, "biasA0", [128, 8, 256], F32)
        kb.op(G_, lambda: nc.gpsimd.iota(dist_i.t[:], pattern=[[-1, 256]], base=128, channel_multiplier=1), (), [dist_i])
        CP(V, distf.t[:], dist_i.t[:], [dist_i], [distf])
        TS(V, mskf.t[:], distf.t[:], 0.0, None, ALU.is_ge, None, [distf], [mskf])
        TS(V, msk2.t[:], distf.t[:], 128.0, None, ALU.is_le, None, [distf], [msk2])
        TT(V, mskf.t[:], mskf.t[:], msk2.t[:], ALU.mult, [mskf, msk2], [mskf])
        TS(V, msk2.t[:], mskf.t[:], -1.0, 30000.0, ALU.add, ALU.mult, [mskf], [msk2])
        TT(V, distf.t[:], distf.t[:], mskf.t[:], ALU.mult, [distf, mskf], [distf])
        for h in range(8):
            slope = 2.0 ** (-(h + 1))
            STT(biasA.t[:, h, :], distf.t[:], -slope, msk2.t[:], ALU.mult, ALU.add, [distf, msk2], [biasA])
        CP(V, biasA0.t[:], biasA.t[:], [biasA], [biasA0])
        MEMSET(V, biasA0.t[:, :, 0:128], -30000.0, [biasA0])
        bs_i = sb(cst, "bs_i", [4, 129], I32)
        bs_f = sb(cst, "bs_f", [4, 129], F32)
        biasS = sb(cst, "biasS", [4, 2, 129], F32)
        slp = sb(cst, "slp", [4, 2], F32)
        slp_i = sb(cst, "slp_i", [4, 2], I32)
        kb.op(G_, lambda: nc.gpsimd.iota(bs_i.t[:], pattern=[[-1, 129]], base=128, channel_multiplier=0), (), [bs_i])
        CP(V, bs_f.t[:], bs_i.t[:], [bs_i], [bs_f])
        kb.op(G_, lambda: nc.gpsimd.iota(slp_i.t[:], pattern=[[4, 2]], base=1, channel_multiplier=1), (), [slp_i])
        CP(V, slp.t[:], slp_i.t[:], [slp_i], [slp])
        ACTF(slp.t[:], slp.t[:], AF.Exp, [slp], [slp], scale=-math.log(2.0))
        for k2 in range(2):
            TS(V, biasS.t[:, k2, :], bs_f.t[:], slp.t[:, k2:k2 + 1], -1.0, ALU.mult, ALU.mult, [bs_f, slp], [biasS])
        jidx_i = sb(cst, "jidx_i", [128, 512], I32)
        jidx = sb(cst, "jidx", [128, 512], F32)
        kb.op(G_, lambda: nc.gpsimd.iota(jidx_i.t[:], pattern=[[1, 512]], base=0, channel_multiplier=0), (), [jidx_i])
        CP(V, jidx.t[:], jidx_i.t[:], [jidx_i], [jidx])
        xqT_s = sb(cst, "xqT_s", [128, 4, NS], BF)
        qT_s = sb(cst, "qT_s", [64, 8, NS], BF)
        kT_s = sb(cst, "kT_s", [64, 2, NS], BF)
        uT_s = sb(cst, "uT_s", [128, 4, NS], BF)
        mkT = sb(cst, "mkT", [128, 4, 256], BF)
        mvT = sb(cst, "mvT", [128, 2, 512], BF)
        kb.barrier()

        def stage_p0():
            with contextlib.ExitStack() as st:
                xin = [sb(st, f"xin{i}", [128, D], F32) for i in range(2)]
                xo = [sb(st, f"xo{i}", [128, 8, 128], F32) for i in range(2)]
                pt = [psb(st, f"p0t{i}", [128, 1024], F32) for i in range(2)]
                blocks = [(j * 128, 128, xp[j * 128:(j + 1) * 128, :]) for j in range(T // 128)] + [(T, NS, xs[:, :])]
                for bi, (t0, n, src) in enumerate(blocks):
                    a = bi % 2
                    kb.dma('sp', xin[a].t[0:n, :], src, (), [xin[a]])
                    for k in range(8):
                        TR(pt[a].t[:, k * 128:k * 128 + n], xin[a].t[0:n, k * 128:(k + 1) * 128], identf.t[0:n, 0:n], [xin[a], identf], [pt[a]])
                    evac(xo[a].t[:, :, 0:n], pt[a].t[:].rearrange("p (k t) -> p k t", k=8)[:, :, 0:n], [pt[a]], [xo[a]])
                    kb.dma('sp', XT.rearrange("(k p) t -> p k t", p=128)[:, :, t0:t0 + n], xo[a].t[:, :, 0:n], [xo[a]], ())
            kb.barrier()

        def rmsnorm_tile(xt, n, gcol, hout, sq, rt, pss, r_extra=()):
            ACTF(sq.t[:, :, 0:n], xt.t[:, :, 0:n], AF.Square, [xt], [sq])
            for k in range(8):
                MM(pss.t[:, 0:n], onesb.t[:], sq.t[:, k, 0:n], k == 0, k == 7, [onesb, sq], [pss])
            ACTF(rt.t[:, 0:n], pss.t[:, 0:n], AF.Sqrt, [pss], [rt], bias=EPS, scale=1.0 / D)
            kb.op(V, lambda: nc.vector.reciprocal(out=rt.t[:, 0:n], in_=rt.t[:, 0:n]), [rt], [rt])
            for k in range(8):
                STT(hout.t[:, k, 0:n], xt.t[:, k, 0:n], gcol.t[:, k:k + 1], rt.t[:, 0:n], ALU.mult, ALU.mult, [xt, gcol, rt], [hout])

        def stage_mem(l):
            with contextlib.ExitStack() as st:
                wm = sb(st, "wm", [128, 8, 1024], BF)
                load_w_begin(st)
                load_w(wm, lambda kt: w_mkv[l, kt * 128:(kt + 1) * 128, :], 8, 1024)
                load_w_end()
                gm = sb(st, "gm", [128, 8], F32)
                kb.dma('sp', gm.t[:], g_mem[l].rearrange("(k p) -> p k", p=128), (), [gm])
                min_ = sb(st, "min_", [128, 2, D], F32)
                kb.dma('sp', min_.t[:], mem.rearrange("(j p) d -> p j d", p=128), (), [min_])
                mx = sb(st, "mx", [128, 8, 512], F32)
                mh = sb(st, "mh", [128, 8, 512], BF)
                sq = sb(st, "msq", [128, 8, 512], BF)
                rt = sb(st, "mrt", [128, 512], F32)
                pss = psb(st, "mpss", [128, 512], F32)
                pt = psb(st, "mpt", [128, 1024], F32)
                po = [psb(st, f"mpo{i}", [128, 512], F32) for i in range(2)]
                mo = [sb(st, f"mo{i}", [128, 512], F32) for i in range(2)]
                for j in range(2):
                    for k in range(8):
                        TR(pt.t[:, k * 128:(k + 1) * 128], min_.t[:, j, k * 128:(k + 1) * 128], identf.t[:], [min_, identf], [pt])
                    evac(mx.t[:, :, j * 128:(j + 1) * 128], pt.t[:].rearrange("p (k t) -> p k t", k=8), [pt], [mx])
                rmsnorm_tile(mx, 256, gm, mh, sq, rt, pss)
                cnt = 0
                for j in range(2):
                    for half in range(2):
                        a = cnt % 2
                        cnt += 1
                        for k in range(8):
                            MM(po[a].t[:, :], mh.t[:, k, j * 128:(j + 1) * 128], wm.t[:, k, half * 512:(half + 1) * 512], k == 0, k == 7, [mh, wm], [po[a]])
                        evac(mo[a].t[:], po[a].t[:], [po[a]], [mo[a]])
                        dst = (o_pmk, o_pmv)[half]
                        kb.dma('sp', dst[l, j * 128:(j + 1) * 128, :], mo[a].t[:], [mo[a]], ())
                        if half == 1:
                            CP(V, mvT.t[:, j, :], mo[a].t[:], [mo[a]], [mvT])
                for h in range(4):
                    a = h % 2
                    for k in range(8):
                        MM(po[a].t[:, 0:256], wm.t[:, k, h * 128:(h + 1) * 128], mh.t[:, k, 0:256], k == 0, k == 7, [mh, wm], [po[a]])
                    evac(mkT.t[:, h, :], po[a].t[:, 0:256], [po[a]], [mkT])
            kb.barrier()

        def stage_in(l):
            with contextlib.ExitStack() as st:
                win = sb(st, "win", [128, 8, INC], BF)
                load_w_begin(st)
                load_w(win, lambda kt: w_in[l, kt * 128:(kt + 1) * 128, :], 8, INC)
                load_w_end()
                gm = sb(st, "gmix", [128, 8], F32)
                kb.dma('sp', gm.t[:], g_mix[l].rearrange("(k p) -> p k", p=128), (), [gm])
                xt = [sb(st, "xt0", [128, 8, 512], F32)] * 2
                hT = [sb(st, f"hT{i}", [128, 8, 512], BF) for i in range(2)]
                sq = sb(st, "sq", [128, 8, 512], BF)
                rt = sb(st, "rt", [128, 512], F32)
                pss = psb(st, "pss", [128, 512], F32)
                pp = [psb(st, f"pp{i}", [128, 512], F32) for i in range(5)]
                stq = sb(st, "stq", [64, 8, 512], BF)
                stk = sb(st, "stk", [64, 2, 512], BF)
                stv = sb(st, "stv", [128, 4, 128], BF)
                stvf = sb(st, "stvf", [128, 128], F32)
                stkf = sb(st, "stkf", [128, 128], F32)
                stu = sb(st, "stu", [128, 4, 512], BF)
                stx = sb(st, "stx", [128, 4, 512], BF)
                sg = [sb(st, "sg0", [128, 8, 512], BF)] * 2
                ppi = [0]

                def nextp():
                    p = pp[ppi[0] % 5]
                    ppi[0] += 1
                    return p
                XTv = XT.rearrange("(k p) t -> p k t", p=128)
                def prol(ti):
                    t0_, n_ = tiles[ti]
                    kb.dma('sp', xt[ti % 2].t[:, :, 0:n_], XTv[:, :, t0_:t0_ + n_], (), [xt[ti % 2]])
                    rmsnorm_tile(xt[ti % 2], n_, gm, hT[ti % 2], sq, rt, pss)
                prol(0)
                for ti, (t0, n) in enumerate(tiles):
                    a = ti % 2
                    smp = (n == NS)
                    if ti + 1 < len(tiles):
                        prol(ti + 1)
                    h_ = hT[a]

                    def proj(c0, M, outap, r_w, sig=False):
                        p = nextp()
                        for k in range(8):
                            MM(p.t[0:M, 0:n], win.t[:, k, c0:c0 + M], h_.t[:, k, 0:n], k == 0, k == 7, [win, h_], [p])
                        if sig:
                            ACTF(outap, p.t[0:M, 0:n], AF.Sigmoid, [p], r_w)
                        else:
                            evac(outap, p.t[0:M, 0:n], [p], r_w)
                    kin = os.environ.get("KIN", "q,k,v,u,x,g").split(",")
                    for h in range(8 if "q" in kin else 0):
                        proj(64 * h, 64, (qT_s.t[:, h, :] if smp else stq.t[:, h, 0:n]), [qT_s if smp else stq])
                    for h in range(2 if "k" in kin else 0):
                        proj(512 + 64 * h, 64, (kT_s.t[:, h, :] if smp else stk.t[:, h, 0:n]), [kT_s if smp else stk])
                    if not smp and "q" in kin and "k" in kin:
                        kb.dma('sp', Qs.rearrange("h p t -> p h t")[:, :, t0:t0 + n], stq.t[:, :, 0:n], [stq], ())
                        kb.dma('sp', Ks.rearrange("h p t -> p h t")[:, :, 128 + t0:128 + t0 + n], stk.t[:, :, 0:n], [stk], ())
                    nb = 1 if smp else 4
                    if "v" not in kin:
                        nb = 0
                    kvs = os.environ.get("KVS", "p,s").split(",")
                    if (smp and "s" not in kvs) or ((not smp) and "p" not in kvs):
                        nb = 0
                    kv = os.environ.get("KV", "vn,osv,opv,ktok,vs").split(",")
                    for j in range(nb):
                        m = NS if smp else 128
                        p = nextp()
                        kvx = os.environ.get("KVX", "")
                        for k in range(8):
                            lh = win.t[:, k, 0:m] if kvx == "lw" else h_.t[:, k, j * 128:j * 128 + m]
                            rh = h_.t[:, k, 0:128] if kvx == "rh" else win.t[:, k, 640:768]
                            MM(p.t[0:m, 0:128], lh, rh, k == 0, k == 7, [win, h_], [p])
                        last = (not smp) and (t0 + 512 == T) and j == 3
                        if smp:
                            CP(V, stvf.t[0:m, :], p.t[0:m, 0:128], [p], [stvf])
                            if "vn" in kv:
                                kb.dma('sp', VNs[:, :], stvf.t[0:m, :], [stvf], ())
                            if "osv" in kv:
                                kb.dma('sp', o_sv[l, :, 127, :], stvf.t[0:m, :], [stvf], ())
                        else:
                            kve = os.environ.get("KVE", "")
                            if kve != "noevac":
                                evac(stv.t[:, j, :], p.t[:, 0:128], [p], [stv])
                            if last and kve != "nocp":
                                CP(V, stvf.t[:], p.t[:, 0:128], [p], [stvf])
                                if "opv" in kv:
                                    kb.dma('sp', o_pv[l, :, :], stvf.t[:], [stvf], ())
                        if (smp or last) and "ktok" in kv:
                            p = nextp()
                            for k in range(8):
                                MM(p.t[0:m, 0:128], h_.t[:, k, j * 128:j * 128 + m], win.t[:, k, 512:640], k == 0, k == 7, [win, h_], [p])
                            CP(V, stkf.t[0:m, :], p.t[0:m, 0:128], [p], [stkf])
                            if smp:
                                kb.dma('sp', o_sk[l, :, 127, :], stkf.t[0:m, :], [stkf], ())
                            else:
                                kb.dma('sp', o_pk[l, :, :], stkf.t[:], [stkf], ())
                    if not smp and "v" in kin and "vs" in kv:
                        kb.dma('sp', Vs[128 + t0:128 + t0 + 512, :].rearrange("(j p) c -> p j c", p=128), stv.t[:], [stv], ())
                    for f in range(4 if "u" in kin else 0):
                        proj(768 + 128 * f, 128, (uT_s.t[:, f, :] if smp else stu.t[:, f, 0:n]), [uT_s if smp else stu])
                    for f in range(4 if "x" in kin else 0):
                        proj(1280 + 128 * f, 128, (xqT_s.t[:, f, :] if smp else stx.t[:, f, 0:n]), [xqT_s if smp else stx])
                    if "u" in kin:
                        kb.dma('sp', Us.rearrange("(f p) t -> p f t", p=128)[:, :, t0:t0 + n],
                               (uT_s.t[:] if smp else stu.t[:, :, 0:n]), [uT_s if smp else stu], ())
                    if not smp and "x" in kin:
                        kb.dma('sp', XQs.rearrange("(f p) t -> p f t", p=128)[:, :, t0:t0 + n], stx.t[:, :, 0:n], [stx], ())
                    for gi in range(3 if "g" in kin else 0):
                        s_ = sg[gi % 2]
                        for f in range(8):
                            proj(1792 + gi * 1024 + f * 128, 128, s_.t[:, f, 0:n], [s_], sig=True)
                        kb.dma('sp', SGs.rearrange("(f p) t -> p f t", p=128)[:, gi * 8:(gi + 1) * 8, t0:t0 + n], s_.t[:, :, 0:n], [s_], ())
            kb.barrier()

        def attn_unit(*args):
            for _ in attn_unit_g(*args):
                pass

        def attn_unit_g(st_t, nq, heads, nk, qfn, kfn, vfn, hd, scale, bias_ap, sink_ap, out_fn):
            pS, sS, pS_b, pT, sT, pO, mx, rs, es_, dn = st_t
            nkt = (nk + 127) // 128
            for h in range(heads):
                qa, qb = qfn(h)
                ka, kbuf = kfn(h)
                MM(pS.t[0:nq, h, 0:nk], qa, ka, True, True, qb + kbuf, [pS])
            yield
            if bias_ap is not None:
                STT(sS.t[0:nq, 0:heads, 0:nk], pS.t[0:nq, 0:heads, 0:nk], scale, bias_ap[0], ALU.mult, ALU.add, [pS] + bias_ap[1], [sS])
            else:
                TS(V, sS.t[0:nq, 0:heads, 0:nk], pS.t[0:nq, 0:heads, 0:nk], scale, None, ALU.mult, None, [pS], [sS])
            kb.op(V, lambda: nc.vector.tensor_reduce(out=mx.t[0:nq, 0:heads], in_=sS.t[0:nq, 0:heads, 0:nk], axis=AX.X, op=ALU.max), [sS], [mx])
            if sink_ap is not None:
                TT(V, mx.t[0:nq, 0:heads], mx.t[0:nq, 0:heads], sink_ap[0], ALU.max, [mx] + sink_ap[1], [mx])
            TS(V, dn.t[0:nq, 0:heads], mx.t[0:nq, 0:heads], -1.0, None, ALU.mult, None, [mx], [dn])
            for h in range(heads):
                ACTF(sS.t[0:nq, h, 0:nk], sS.t[0:nq, h, 0:nk], AF.Exp, [sS, dn], [sS, rs],
                     bias=dn.t[0:nq, h:h + 1], accum=rs.t[0:nq, h:h + 1])
            yield
            if sink_ap is not None:
                TT(V, es_.t[0:nq, 0:heads], sink_ap[0], mx.t[0:nq, 0:heads], ALU.subtract, [mx] + sink_ap[1], [es_])
                ACTF(es_.t[0:nq, 0:heads], es_.t[0:nq, 0:heads], AF.Exp, [es_], [es_])
                TT(V, rs.t[0:nq, 0:heads], rs.t[0:nq, 0:heads], es_.t[0:nq, 0:heads], ALU.add, [rs, es_], [rs])
            kb.op(V, lambda: nc.vector.reciprocal(out=dn.t[0:nq, 0:heads], in_=rs.t[0:nq, 0:heads]), [rs], [dn])
            TT(V, pS_b.t[0:nq, 0:heads, 0:nk], sS.t[0:nq, 0:heads, 0:nk],
               dn.t[0:nq, 0:heads].rearrange("p (h o) -> p h o", o=1).to_broadcast([nq, heads, nk]), ALU.mult, [sS, dn], [pS_b])
            for h in range(heads):
                for kt in range(nkt):
                    kn = min(128, nk - kt * 128)
                    TR(pT.t[0:kn, h, kt, 0:nq], pS_b.t[0:nq, h, kt * 128:kt * 128 + kn], identb.t[0:nq, 0:nq], [pS_b, identb], [pT])
            for kt in range(nkt):
                kn = min(128, nk - kt * 128)
                CP(S_, sT.t[0:kn, 0:heads, kt, 0:nq], pT.t[0:kn, 0:heads, kt, 0:nq], [pT], [sT])
            for h in range(heads):
                for kt in range(nkt):
                    kn = min(128, nk - kt * 128)
                    va, vb = vfn(h, kt, kn)
                    MM(pO.t[0:hd, h, 0:nq], va, sT.t[0:kn, h, kt, 0:nq], kt == 0, kt == nkt - 1, vb + [sT], [pO])
            out_fn(pO)

        def run_pipelined(units):
            n = len(units)
            for s in range(n + 2):
                if s < n:
                    if units[s][0] is not None:
                        units[s][0]()
                    next(units[s][1])
                if 0 <= s - 1 < n:
                    next(units[s - 1][1])
                if 0 <= s - 2 < n:
                    for _ in units[s - 2][1]:
                        pass
                    if units[s - 2][2] is not None:
                        units[s - 2][2]()

        def attn_tiles(st, pfx, psum_from=None):
            if psum_from is not None:
                pS, pT, pO = psum_from[0], psum_from[3], psum_from[5]
                sS = sb(st, pfx + "sS", [128, 4, 256], F32)
                pS_b = sb(st, pfx + "pSb", [128, 4, 256], BF)
                sT = sb(st, pfx + "sT", [128, 4, 2, 128], BF)
                mx = sb(st, pfx + "mx", [128, 4], F32)
                rs = sb(st, pfx + "rs", [128, 4], F32)
                es_ = sb(st, pfx + "es", [128, 4], F32)
                dn = sb(st, pfx + "dn", [128, 4], F32)
                return (pS, sS, pS_b, pT, sT, pO, mx, rs, es_, dn)
            pS = psb(st, pfx + "pS", [128, 4, 256], F32)
            sS = sb(st, pfx + "sS", [128, 4, 256], F32)
            pS_b = sb(st, pfx + "pSb", [128, 4, 256], BF)
            pT = psb(st, pfx + "pT", [128, 4, 2, 128], BF)
            sT = sb(st, pfx + "sT", [128, 4, 2, 128], BF)
            pO = psb(st, pfx + "pO", [128, 4, 128], F32)
            mx = sb(st, pfx + "mx", [128, 4], F32)
            rs = sb(st, pfx + "rs", [128, 4], F32)
            es_ = sb(st, pfx + "es", [128, 4], F32)
            dn = sb(st, pfx + "dn", [128, 4], F32)
            return (pS, sS, pS_b, pT, sT, pO, mx, rs, es_, dn)

        def stage_att(l):
            with contextlib.ExitStack() as st:
                tls = [attn_tiles(st, "a0"), attn_tiles(st, "a1")]
                tl = tls[0]
                ucnt = [0]
                snk = sb(st, "snk", [128, 8], F32)
                kb.dma('sp', snk.t[:], sinks[l:l + 1, :].to_broadcast([128, 8]), (), [snk])
                snkS = sb(st, "snkS", [4, 2], F32)
                kb.dma('sp', snkS.t[:], sinks[l].rearrange("(k g) -> g k", g=4), (), [snkS])
                qt = [sb(st, f"aq{i}", [64, 8, 512], BF) for i in range(2)]
                kt_ = [sb(st, f"ak{i}", [64, 2, 640], BF) for i in range(2)]
                vt = [sb(st, f"av{i}", [128, 5, 128], BF) for i in range(2)]
                oa = [sb(st, f"ao{i}", [64, 8, 512], BF) for i in range(2)]
                zk = sb(st, "zk", [64, 2, 128], BF)
                zv = sb(st, "zv", [128, 128], BF)
                MEMSET(V, zk.t[:], 0.0, [zk])
                MEMSET(V, zv.t[:], 0.0, [zv])
                kb.dma('sp', Ks.rearrange("h p t -> p h t")[:, :, 0:128], zk.t[:], [zk], ())
                kb.dma('sp', Vs[0:128, :], zv.t[:], [zv], ())
                kb.barrier()
                units = []
                for ti in range(NTL):
                    a = ti % 2
                    t0 = ti * 512

                    def pre(a=a, t0=t0):
                        kb.dma('sp', qt[a].t[:], Qs.rearrange("h p t -> p h t")[:, :, t0:t0 + 512], (), [qt[a]])
                        kb.dma('sp', kt_[a].t[:], Ks.rearrange("h p t -> p h t")[:, :, t0:t0 + 640], (), [kt_[a]])
                        kb.dma('sp', vt[a].t[:], Vs[t0:t0 + 640, :].rearrange("(j p) c -> p j c", p=128), (), [vt[a]])

                    def post(a=a, t0=t0):
                        kb.dma('sp', OAs.rearrange("(h p) t -> p h t", p=64)[:, :, t0:t0 + 512], oa[a].t[:], [oa[a]], ())
                    for j in range(4):
                        for k2 in range(2):
                            bias = (biasA0 if (ti == 0 and j == 0) else biasA)
                            ucnt[0] += 1
                            g = attn_unit_g(
                                tls[ucnt[0] % 2], 128, 4, 256,
                                lambda h, a=a, j=j, k2=k2: (qt[a].t[:, 4 * k2 + h, j * 128:(j + 1) * 128], [qt[a]]),
                                lambda h, a=a, j=j, k2=k2: (kt_[a].t[:, k2, j * 128:j * 128 + 256], [kt_[a]]),
                                lambda h, kt, kn, a=a, j=j, k2=k2: (vt[a].t[:, j + kt, 64 * k2:64 * k2 + 64], [vt[a]]),
                                64, 0.125, (bias.t[:, 4 * k2:4 * k2 + 4, :], [bias]), (snk.t[:, 4 * k2:4 * k2 + 4], [snk]),
                                lambda pO, a=a, j=j, k2=k2: CP(S_, oa[a].t[:, 4 * k2:4 * k2 + 4, j * 128:(j + 1) * 128], pO.t[0:64, :, :], [pO], [oa[a]]))
                            first = (j == 0 and k2 == 0)
                            lastu = (j == 3 and k2 == 1)
                            units.append((pre if first else None, g, post if lastu else None))
                run_pipelined(units)
                kc = [sb(st, f"skc{i}", [128, 128], F32) for i in range(2)]
                vc = [sb(st, f"svc{i}", [128, 128], F32) for i in range(2)]
                vcb = [sb(st, f"svcb{i}", [128, 128], BF) for i in range(2)]
                vn = [sb(st, f"svn{i}", [1, 128], F32) for i in range(2)]
                vnb = [sb(st, f"svnb{i}", [1, 128], BF) for i in range(2)]
                ktx = [sb(st, f"sktx{i}", [64, 2, 129], BF) for i in range(2)]
                pk = tls[1][5]
                oas = sb(st, "oas", [64, 8, NS], BF)
                for b in range(NS):
                    a = b % 2
                    kb.dma('sp', kc[a].t[:], c_swk[l, b], (), [kc[a]])
                    kb.dma('sp', vc[a].t[:], c_swv[l, b], (), [vc[a]])
                    kb.dma('sp', vn[a].t[:], VNs[b:b + 1, :], (), [vn[a]])
                    kb.dma('sp', o_sk[l, b, 0:127, :], c_swk[l, b, 1:128, :], (), ())
                    kb.dma('sp', o_sv[l, b, 0:127, :], c_swv[l, b, 1:128, :], (), ())
                    for k2 in range(2):
                        TR(pk.t[0:64, k2, 0:128], kc[a].t[:, 64 * k2:64 * k2 + 64], identf.t[:], [kc[a], identf], [pk])
                    evac(ktx[a].t[:, :, 0:128], pk.t[0:64, 0:2, 0:128], [pk], [ktx[a]])
                    CP(V, ktx[a].t[:, :, 128:129], kT_s.t[:, :, b:b + 1], [kT_s], [ktx[a]])
                    CP(V, vcb[a].t[:], vc[a].t[:], [vc[a]], [vcb[a]])
                    CP(V, vnb[a].t[:], vn[a].t[:], [vn[a]], [vnb[a]])
                    for k2 in range(2):
                        attn_unit(
                            tl, 4, 1, 129,
                            lambda h: (qT_s.t[:, 4 * k2:4 * k2 + 4, b], [qT_s]),
                            lambda h: (ktx[a].t[:, k2, :], [ktx[a]]),
                            lambda h, kt, kn: ((vcb[a].t[:, 64 * k2:64 * k2 + 64], [vcb[a]]) if kt == 0 else (vnb[a].t[0:1, 64 * k2:64 * k2 + 64], [vnb[a]])),
                            64, 0.125, (biasS.t[:, k2:k2 + 1, :], [biasS]), (snkS.t[:, k2:k2 + 1], [snkS]),
                            lambda pO: evac(oas.t[:, 4 * k2:4 * k2 + 4, b], pO.t[0:64, 0, 0:4], [pO], [oas]))
                kb.dma('sp', OAs.rearrange("(h p) t -> p h t", p=64)[:, :, T:T + NS], oas.t[:], [oas], ())
            kb.barrier()


        def sin_of(out_ap, ang_ap, shape, tmpf, tmpi, tmpm, bufs_in, buf_out, phase=0.0):
            tf, ti_, tm = tmpf, tmpi, tmpm
            TS(V, tf[0], ang_ap, 1.0, phase, ALU.mult, ALU.add, bufs_in, [tf[1]])
            TS(V, tm[0], tf[0], 1.0 / TWO_PI, None, ALU.mult, None, [tf[1]], [tm[1]])
            CP(V, ti_[0], tm[0], [tm[1]], [ti_[1]])
            CP(V, tm[0], ti_[0], [ti_[1]], [tm[1]])
            STT(tf[0], tm[0], -TWO_PI, tf[0], ALU.mult, ALU.add, [tm[1], tf[1]], [tf[1]])
            TS(V, tm[0], tf[0], math.pi, None, ALU.is_gt, None, [tf[1]], [tm[1]])
            STT(tf[0], tm[0], -TWO_PI, tf[0], ALU.mult, ALU.add, [tm[1], tf[1]], [tf[1]])
            TS(V, tm[0], tf[0], -math.pi, None, ALU.is_lt, None, [tf[1]], [tm[1]])
            STT(tf[0], tm[0], TWO_PI, tf[0], ALU.mult, ALU.add, [tm[1], tf[1]], [tf[1]])
            TS(V, tf[0], tf[0], math.pi, -math.pi, ALU.min, ALU.max, [tf[1]], [tf[1]])
            ACTF(out_ap, tf[0], AF.Sin, [tf[1]], [buf_out])

        def bc3(ap2, n):
            P_, M_ = ap2.shape[0], ap2.shape[1]
            return ap2.rearrange("p (m o) -> p m o", o=1).to_broadcast([P_, M_, n])

        def stage_s5(l):
            with contextlib.ExitStack() as st:
                def t2(name, shape=(128, 16), dt=F32):
                    return sb(st, name, list(shape), dt)
                lr, li, dtl, dtt, th, mag, ar, ai = [t2(n) for n in ("lr", "li", "dtl", "dtt", "th", "mag", "ar", "ai")]
                cs, sn, den, fr, fi, tA, tB = [t2(n) for n in ("cs", "sn", "den", "fr", "fi", "tA", "tB")]
                tf = t2("rtf"); ti_ = t2("rti", dt=I32); tm = t2("rtm")
                kb.dma('sp', lr.t[:], lam_re[l].rearrange("(m two) p -> (two p) m", two=2), (), [lr])
                kb.dma('sp', li.t[:], lam_im[l].rearrange("(m two) p -> (two p) m", two=2), (), [li])
                ldv = log_dt[l].rearrange("(m two) -> two m", two=2)
                kb.dma('sp', dtl.t[0:64, :], ldv[0:1, :].to_broadcast([64, 16]), (), [dtl], acc=True)
                kb.dma('sp', dtl.t[64:128, :], ldv[1:2, :].to_broadcast([64, 16]), (), [dtl], acc=True)
                ACTF(dtt.t[:], dtl.t[:], AF.Exp, [dtl], [dtt])
                TT(V, th.t[:], li.t[:], dtt.t[:], ALU.mult, [li, dtt], [th])
                TT(V, mag.t[:], lr.t[:], dtt.t[:], ALU.mult, [lr, dtt], [mag])
                ACTF(mag.t[:], mag.t[:], AF.Exp, [mag], [mag])
                sin_of(sn.t[:], th.t[:], None, (tf.t[:], tf), (ti_.t[:], ti_), (tm.t[:], tm), [th], sn)
                sin_of(cs.t[:], th.t[:], None, (tf.t[:], tf), (ti_.t[:], ti_), (tm.t[:], tm), [th], cs, phase=math.pi / 2)
                TT(V, ar.t[:], mag.t[:], cs.t[:], ALU.mult, [mag, cs], [ar])
                TT(V, ai.t[:], mag.t[:], sn.t[:], ALU.mult, [mag, sn], [ai])
                TT(V, den.t[:], lr.t[:], lr.t[:], ALU.mult, [lr], [den])
                TT(V, tA.t[:], li.t[:], li.t[:], ALU.mult, [li], [tA])
                TT(V, den.t[:], den.t[:], tA.t[:], ALU.add, [den, tA], [den])
                kb.op(V, lambda: nc.vector.reciprocal(out=den.t[:], in_=den.t[:]), [den], [den])
                TS(V, tA.t[:], ar.t[:], -1.0, None, ALU.add, None, [ar], [tA])
                TT(V, fr.t[:], tA.t[:], lr.t[:], ALU.mult, [tA, lr], [fr])
                TT(V, tB.t[:], ai.t[:], li.t[:], ALU.mult, [ai, li], [tB])
                TT(V, fr.t[:], fr.t[:], tB.t[:], ALU.add, [fr, tB], [fr])
                TT(V, fr.t[:], fr.t[:], den.t[:], ALU.mult, [fr, den], [fr])
                TT(V, fi.t[:], ai.t[:], lr.t[:], ALU.mult, [ai, lr], [fi])
                TT(V, tB.t[:], tA.t[:], li.t[:], ALU.mult, [tA, li], [tB])
                TT(V, fi.t[:], fi.t[:], tB.t[:], ALU.subtract, [fi, tB], [fi])
                TT(V, fi.t[:], fi.t[:], den.t[:], ALU.mult, [fi, den], [fi])
                cosT = sb(st, "cosT", [128, 16, 512], F32)
                sinT = sb(st, "sinT", [128, 16, 512], F32)
                with contextlib.ExitStack() as st2:
                    ang = sb(st2, "ang", [128, 4, 512], F32)
                    rf = sb(st2, "rf", [128, 4, 512], F32); ri2 = sb(st2, "ri2", [128, 4, 512], I32); rm = sb(st2, "rm", [128, 4, 512], F32)
                    for c in range(4):
                        TT(V, ang.t[:], bc3(th.t[:, 4 * c:4 * c + 4], 512),
                           jidx.t[:].rearrange("p (o j) -> p o j", o=1).to_broadcast([128, 4, 512]), ALU.mult, [th, jidx], [ang])
                        sin_of(sinT.t[:, 4 * c:4 * c + 4, :], ang.t[:], None, (rf.t[:], rf), (ri2.t[:], ri2), (rm.t[:], rm), [ang], sinT)
                        sin_of(cosT.t[:, 4 * c:4 * c + 4, :], ang.t[:], None, (rf.t[:], rf), (ri2.t[:], ri2), (rm.t[:], rm), [ang], cosT, phase=math.pi / 2)
                    kb.barrier()
                pX = [psb(st, f"pX{i}", [128, 2, 512], F32) for i in range(2)]
                pY = psb(st, "pY", [128, 4, 512], F32)
                BT = sb(st, "BT", [128, 2, 4, 128], BF)
                BT3 = sb(st, "BT3", [128, 2, 4, 128], BF)
                CTp = sb(st, "CTp", [128, 16, 2, 128], BF)
                dsk = sb(st, "dsk", [128, 4], F32)
                stb = contextlib.ExitStack()
                br = sb(stb, "br", [128, 16, 16], F32); bi = sb(stb, "bi", [128, 16, 16], F32)
                bsr = sb(stb, "bsr", [128, 16, 16], F32); bsi = sb(stb, "bsi", [128, 16, 16], F32); btmp = sb(stb, "btmp", [128, 16, 16], F32)
                kb.dma('sp', br.t[:], b_re[l].rearrange("(m two) p j -> (two p) m j", two=2), (), [br])
                kb.dma('sp', bi.t[:], b_im[l].rearrange("(m two) p j -> (two p) m j", two=2), (), [bi])
                TT(V, bsr.t[:], br.t[:], bc3(fr.t[:], 16), ALU.mult, [br, fr], [bsr])
                TT(V, btmp.t[:], bi.t[:], bc3(fi.t[:], 16), ALU.mult, [bi, fi], [btmp])
                TT(V, bsr.t[:], bsr.t[:], btmp.t[:], ALU.subtract, [bsr, btmp], [bsr])
                TT(V, bsi.t[:], bi.t[:], bc3(fr.t[:], 16), ALU.mult, [bi, fr], [bsi])
                TT(V, btmp.t[:], br.t[:], bc3(fi.t[:], 16), ALU.mult, [br, fi], [btmp])
                TT(V, bsi.t[:], bsi.t[:], btmp.t[:], ALU.add, [bsi, btmp], [bsi])
                Pcat = sb(stb, "Pcat", [128, 2, 4, 4, 2, 16], F32)
                MEMSET(V, Pcat.t[:], 0.0, [Pcat])
                for ri, bs in enumerate((bsr, bsi)):
                    CP(V, Pcat.t[0:64, ri, :, :, 0, :], bs.t[0:64, :, :].rearrange("p (f m) j -> p f m j", f=4), [bs], [Pcat])
                    CP(V, Pcat.t[64:128, ri, :, :, 1, :], bs.t[64:128, :, :].rearrange("p (f m) j -> p f m j", f=4), [bs], [Pcat])
                for ri in range(2):
                    for ft in range(4):
                        TR(pY.t[:, ft, 0:128], Pcat.t[:, ri, ft].rearrange("p m t j -> p (m t j)"), identf.t[:], [Pcat, identf], [pY])
                    evac(BT.t[:, ri, :, :], pY.t[:, :, 0:128], [pY], [BT])
                MEMSET(V, BT3.t[:], 0.0, [BT3])
                CP(V, BT3.t[96:128, :, :, :], BT.t[96:128, :, :, :], [BT], [BT3])
                Cin = sb(stb, "Cin", [128, 4, 2, 128], F32)
                for ri, cc in enumerate((c_re, c_im)):
                    src = cc[l].rearrange("(f g) k p -> (g k) f p", f=4)
                    kb.dma('sp', Cin.t[:, :, ri, 0:64], src, (), [Cin], acc=True)
                    kb.dma('sp', Cin.t[:, :, ri, 64:128], src, (), [Cin], acc=True)
                TS(V, Cin.t[:, :, 1, :], Cin.t[:, :, 1, :], -1.0, None, ALU.mult, None, [Cin], [Cin])
                CTc = sb(stb, "CTc", [128, 4, 2, 128], F32)
                for ri in range(2):
                    for ft in range(4):
                        TR(pY.t[:, ft, 0:128], Cin.t[:, ft, ri, :], identf.t[:], [Cin, identf], [pY])
                    evac(CTc.t[:, :, ri, :], pY.t[:, :, 0:128], [pY], [CTc])
                MEMSET(V, CTp.t[:], 0.0, [CTp])
                for m in range(16):
                    ft, mm = m // 4, m % 4
                    e = (V, G_)[m % 2]
                    CP(e, CTp.t[0:64, m, :, 32 * mm:32 * mm + 16], CTc.t[0:64, ft, :, 32 * mm:32 * mm + 16], [CTc], [CTp])
                    CP(e, CTp.t[64:128, m, :, 32 * mm + 16:32 * mm + 32], CTc.t[64:128, ft, :, 32 * mm + 16:32 * mm + 32], [CTc], [CTp])
                kb.dma('sp', dsk.t[:], d_skip[l].rearrange("(f p) -> p f", p=128), (), [dsk])
                kb.barrier()
                stb.close()
                initR = sb(st, "initR", [128, 16], F32); initI = sb(st, "initI", [128, 16], F32)
                hlR = sb(st, "hlR", [128, 16], F32); hlI = sb(st, "hlI", [128, 16], F32)
                tcA = sb(st, "tcA", [128, 16], F32)
                G5 = sb(st, "G5", [128, 2, 16], F32)
                ysb = sb(st, "ysb", [128, 4, 512], F32)
                st = contextlib.ExitStack()
                MEMSET(V, initR.t[:], 0.0, [initR]); MEMSET(V, initI.t[:], 0.0, [initI])
                uT = [sb(st, f"suT{i}", [128, 4, 512], BF) for i in range(2)]
                w1 = [sb(st, f"w1{i}", [128, 512], F32) for i in range(2)]
                w2 = [sb(st, f"w2{i}", [128, 512], F32) for i in range(2)]
                w3 = [sb(st, f"w3{i}", [128, 512], F32) for i in range(2)]
                w4 = [sb(st, f"w4{i}", [128, 512], F32) for i in range(2)]
                xr_ = [sb(st, f"xr{i}", [128, 512], F32) for i in range(2)]
                xi_ = [sb(st, f"xi{i}", [128, 512], F32) for i in range(2)]
                gr = [sb(st, f"gr{i}", [128, 512], F32) for i in range(2)]
                gi_ = [sb(st, f"gi{i}", [128, 512], F32) for i in range(2)]
                hr = [sb(st, f"hr{i}", [128, 512], BF) for i in range(2)]
                hi = [sb(st, f"hi{i}", [128, 512], BF) for i in range(2)]
                zT = [sb(st, f"zT{i}", [128, 4, 512], BF) for i in range(2)]
                it = 0
                pend = [None]
                ks5 = os.environ.get("KS5", "main,smp").split(",")
                for ti in range(NTL if "main" in ks5 else 0):
                    ua = uT[ti % 2]
                    t0 = ti * 512
                    kb.dma('sp', ua.t[:], Us.rearrange("(f p) t -> p f t", p=128)[:, :, t0:t0 + 512], (), [ua])
                    for m in range(16):
                        a = it % 2
                        it += 1
                        ft, mm = m // 4, m % 4
                        px = pX[a]
                        for ri in range(2):
                            if mm < 3:
                                MM(px.t[:, ri, :], BT.t[32 * mm:32 * mm + 32, ri, ft, :], ua.t[32 * mm:32 * mm + 32, ft, :], True, True, [BT, ua], [px])
                            else:
                                MM(px.t[:, ri, :], BT3.t[64:128, ri, ft, :], ua.t[64:128, ft, :], True, True, [BT3, ua], [px])
                        if pend[0] is not None:
                            pend[0]()
                            pend[0] = None
                        c_, s_ = cosT.t[:, m, :], sinT.t[:, m, :]
                        TT(V, w1[a].t[:], px.t[:, 0, :], c_, ALU.mult, [px, cosT], [w1[a]])
                        TT(V, w2[a].t[:], px.t[:, 1, :], s_, ALU.mult, [px, sinT], [w2[a]])
                        TT(V, xr_[a].t[:], w1[a].t[:], w2[a].t[:], ALU.add, [w1[a], w2[a]], [xr_[a]])
                        TT(V, w3[a].t[:], px.t[:, 1, :], c_, ALU.mult, [px, cosT], [w3[a]])
                        TT(V, w4[a].t[:], px.t[:, 0, :], s_, ALU.mult, [px, sinT], [w4[a]])
                        TT(V, xi_[a].t[:], w3[a].t[:], w4[a].t[:], ALU.subtract, [w3[a], w4[a]], [xi_[a]])
                        magb = mag.t[:, m:m + 1].to_broadcast([128, 512])
                        kb.op(V, lambda: nc.vector.tensor_tensor_scan(out=gr[a].t[:], data0=magb, data1=xr_[a].t[:], initial=initR.t[:, m:m + 1], op0=ALU.mult, op1=ALU.add), [mag, xr_[a], initR], [gr[a]])
                        kb.op(V, lambda: nc.vector.tensor_tensor_scan(out=gi_[a].t[:], data0=magb, data1=xi_[a].t[:], initial=initI.t[:, m:m + 1], op0=ALU.mult, op1=ALU.add), [mag, xi_[a], initI], [gi_[a]])
                        TT(G_, w1[a].t[:], gr[a].t[:], c_, ALU.mult, [gr[a], cosT], [w1[a]])
                        TT(G_, w2[a].t[:], gi_[a].t[:], s_, ALU.mult, [gi_[a], sinT], [w2[a]])
                        TT(G_, hr[a].t[:], w1[a].t[:], w2[a].t[:], ALU.subtract, [w1[a], w2[a]], [hr[a]])
                        TT(G_, w3[a].t[:], gr[a].t[:], s_, ALU.mult, [gr[a], sinT], [w3[a]])
                        TT(G_, w4[a].t[:], gi_[a].t[:], c_, ALU.mult, [gi_[a], cosT], [w4[a]])
                        TT(G_, hi[a].t[:], w3[a].t[:], w4[a].t[:], ALU.add, [w3[a], w4[a]], [hi[a]])
                        CP(S_, G5.t[:, 0, m:m + 1], gr[a].t[:, 511:512], [gr[a]], [G5])
                        CP(S_, G5.t[:, 1, m:m + 1], gi_[a].t[:, 511:512], [gi_[a]], [G5])
                        def ymm(a=a, m=m, ft=ft, mm=mm):
                            MM(pY.t[:, ft, :], CTp.t[:, m, 0, :], hr[a].t[:], mm == 0, False, [CTp, hr[a]], [pY])
                            MM(pY.t[:, ft, :], CTp.t[:, m, 1, :], hi[a].t[:], False, mm == 3, [CTp, hi[a]], [pY])
                        pend[0] = ymm
                    pend[0]()
                    pend[0] = None
                    c5, s5 = cosT.t[:, :, 511], sinT.t[:, :, 511]
                    c1, s1 = cosT.t[:, :, 1], sinT.t[:, :, 1]
                    TT(V, tcA.t[:], G5.t[:, 1, :], s5, ALU.mult, [G5, sinT], [tcA])
                    TT(V, hlR.t[:], G5.t[:, 0, :], c5, ALU.mult, [G5, cosT], [hlR])
                    TT(V, hlR.t[:], hlR.t[:], tcA.t[:], ALU.subtract, [hlR, tcA], [hlR])
                    TT(V, tcA.t[:], G5.t[:, 1, :], c5, ALU.mult, [G5, cosT], [tcA])
                    TT(V, hlI.t[:], G5.t[:, 0, :], s5, ALU.mult, [G5, sinT], [hlI])
                    TT(V, hlI.t[:], hlI.t[:], tcA.t[:], ALU.add, [hlI, tcA], [hlI])
                    TT(V, tcA.t[:], hlI.t[:], s1, ALU.mult, [hlI, sinT], [tcA])
                    TT(V, initR.t[:], hlR.t[:], c1, ALU.mult, [hlR, cosT], [initR])
                    TT(V, initR.t[:], initR.t[:], tcA.t[:], ALU.subtract, [initR, tcA], [initR])
                    TT(V, tcA.t[:], hlI.t[:], c1, ALU.mult, [hlI, cosT], [tcA])
                    TT(V, initI.t[:], hlR.t[:], s1, ALU.mult, [hlR, sinT], [initI])
                    TT(V, initI.t[:], initI.t[:], tcA.t[:], ALU.add, [initI, tcA], [initI])
                    za = zT[ti % 2]
                    for ft in range(4):
                        STT(ysb.t[:, ft, :], ua.t[:, ft, :], dsk.t[:, ft:ft + 1], pY.t[:, ft, :], ALU.mult, ALU.add, [ua, dsk, pY], [ysb])
                    ACTF(za.t[:], ysb.t[:], AF.Gelu, [ysb], [za])
                    kb.dma('sp', Zs.rearrange("(f p) t -> p f t", p=128)[:, :, t0:t0 + 512], za.t[:], [za], ())
                kb.dma('sp', o_pre[l].rearrange("(m r) -> r m", r=128), hlR.t[:], [hlR], ())
                kb.dma('sp', o_pim[l].rearrange("(m r) -> r m", r=128), hlI.t[:], [hlI], ())
                kb.barrier()
                st.close()
                st = contextlib.ExitStack()
                if "smp" not in ks5:
                    st.close()
                    kb.barrier()
                    return
                stok = sb(st, "stok", [NS, 2, 2048], F32)
                kb.dma('sp', stok.t[:, 0, :], c_sre[l], (), [stok], acc=True)
                kb.dma('sp', stok.t[:, 1, :], c_sim[l], (), [stok], acc=True)
                px = pX[0]
                for ri in range(2):
                    for m in range(16):
                        TR(px.t[:, ri, m * 16:(m + 1) * 16], stok.t[:, ri, m * 128:(m + 1) * 128], identf.t[0:NS, 0:NS], [stok, identf], [px])
                prv = sb(st, "prv", [128, 2, 256], F32)
                evac(prv.t[:], px.t[:, :, 0:256], [px], [prv])
                for ri in range(2):
                    for m in range(16):
                        ft, mm = m // 4, m % 4
                        oc_ = (ri * 4 + ft) * 16
                        if mm < 3:
                            MM(pY.t[:, mm, oc_:oc_ + 16], BT.t[32 * mm:32 * mm + 32, ri, ft, :], uT_s.t[32 * mm:32 * mm + 32, ft, :], True, True, [BT, uT_s], [pY])
                        else:
                            MM(pY.t[:, mm, oc_:oc_ + 16], BT3.t[64:128, ri, ft, :], uT_s.t[64:128, ft, :], True, True, [BT3, uT_s], [pY])
                vA = lambda ap: ap.rearrange("p (f m b) -> p m f b", f=4, m=4, b=16)
                vB = lambda ri: pY.t[:, :, ri * 64:(ri + 1) * 64].rearrange("p m (f b) -> p m f b", f=4)
                hs = sb(st, "hs", [128, 2, 256], F32)
                hsb = sb(st, "hsb", [128, 2, 256], BF)
                q1 = sb(st, "q1", [128, 256], F32); q2 = sb(st, "q2", [128, 256], F32)
                arB, aiB = bc3(ar.t[:], 16), bc3(ai.t[:], 16)
                v3 = lambda ap: ap.rearrange("p (m b) -> p m b", b=16)
                TT(V, v3(q1.t[:]), v3(prv.t[:, 0, :]), arB, ALU.mult, [prv, ar], [q1])
                TT(V, v3(q2.t[:]), v3(prv.t[:, 1, :]), aiB, ALU.mult, [prv, ai], [q2])
                TT(V, q1.t[:], q1.t[:], q2.t[:], ALU.subtract, [q1, q2], [q1])
                TT(V, vA(hs.t[:, 0, :]), vA(q1.t[:]), vB(0), ALU.add, [q1, pY], [hs])
                TT(V, v3(q1.t[:]), v3(prv.t[:, 1, :]), arB, ALU.mult, [prv, ar], [q1])
                TT(V, v3(q2.t[:]), v3(prv.t[:, 0, :]), aiB, ALU.mult, [prv, ai], [q2])
                TT(V, q1.t[:], q1.t[:], q2.t[:], ALU.add, [q1, q2], [q1])
                TT(V, vA(hs.t[:, 1, :]), vA(q1.t[:]), vB(1), ALU.add, [q1, pY], [hs])
                CP(V, hsb.t[:], hs.t[:], [hs], [hsb])
                for ft in range(4):
                    for mm in range(4):
                        m = ft * 4 + mm
                        for ri in range(2):
                            MM(pY.t[:, ft, 0:16], CTp.t[:, m, ri, :], hsb.t[:, ri, m * 16:(m + 1) * 16], mm == 0 and ri == 0, mm == 3 and ri == 1, [CTp, hsb], [pY])
                zs = sb(st, "zs", [128, 4, NS], BF)
                for ft in range(4):
                    STT(ysb.t[:, ft, 0:16], uT_s.t[:, ft, :], dsk.t[:, ft:ft + 1], pY.t[:, ft, 0:16], ALU.mult, ALU.add, [uT_s, dsk, pY], [ysb])
                ACTF(zs.t[:], ysb.t[:, :, 0:16], AF.Gelu, [ysb], [zs])
                kb.dma('sp', Zs.rearrange("(f p) t -> p f t", p=128)[:, :, T:T + NS], zs.t[:], [zs], ())
                sout = sb(st, "sout", [NS, 2, 2048], F32)
                for ri in range(2):
                    for m in range(16):
                        TR(pY.t[0:NS, (m // 4), (m % 4) * 128:(m % 4) * 128 + 128], hs.t[:, ri, m * 16:(m + 1) * 16], identf.t[:], [hs, identf], [pY])
                    evac(sout.t[:, ri, :], pY.t[0:NS, :, :].rearrange("p f t -> p (f t)"), [pY], [sout])
                kb.dma('sp', o_sre[l], sout.t[:, 0, :], [sout], ())
                kb.dma('sp', o_sim[l], sout.t[:, 1, :], [sout], ())
                kb.barrier()
                st.close()
            kb.barrier()

        def stage_matt(l):
            with contextlib.ExitStack() as st:
                tls = [attn_tiles(st, "m0"), attn_tiles(st, "m1")]
                tl = tls[0]
                ucnt = [0]
                qt = [sb(st, f"mq{i}", [128, 4, 512], BF) for i in range(2)]
                oc = [sb(st, f"moc{i}", [128, 4, 512], BF) for i in range(2)]
                sc = 128.0 ** -0.5
                units = []
                for ti in range(NTL):
                    a = ti % 2
                    t0 = ti * 512

                    def pre(a=a, t0=t0):
                        kb.dma('sp', qt[a].t[:], XQs.rearrange("(f p) t -> p f t", p=128)[:, :, t0:t0 + 512], (), [qt[a]])

                    def post(a=a, t0=t0):
                        kb.dma('sp', OCs.rearrange("(f p) t -> p f t", p=128)[:, :, t0:t0 + 512], oc[a].t[:], [oc[a]], ())
                    for j in range(4):
                        ucnt[0] += 1
                        g = attn_unit_g(
                            tls[ucnt[0] % 2], 128, 4, 256,
                            lambda h, a=a, j=j: (qt[a].t[:, h, j * 128:(j + 1) * 128], [qt[a]]),
                            lambda h: (mkT.t[:, h, :], [mkT]),
                            lambda h, kt, kn: (mvT.t[:, kt, 128 * h:128 * h + 128], [mvT]),
                            128, sc, None, None,
                            lambda pO, a=a, j=j: CP(S_, oc[a].t[:, :, j * 128:(j + 1) * 128], pO.t[:, :, :], [pO], [oc[a]]))
                        units.append((pre if j == 0 else None, g, post if j == 3 else None))
                run_pipelined(units)
                mk = [sb(st, f"smk{i}", [128, 2, 512], F32) for i in range(2)]
                mv = [sb(st, f"smv{i}", [128, 2, 512], F32) for i in range(2)]
                mvb = [sb(st, f"smvb{i}", [128, 2, 512], BF) for i in range(2)]
                mkTs = [sb(st, f"smkT{i}", [128, 4, 256], BF) for i in range(2)]
                ptr = tls[1][0]
                ocs = sb(st, "ocs", [128, 4, NS], BF)
                for b in range(NS):
                    a = b % 2
                    kb.dma('sp', mk[a].t[:], c_mk[l, b].rearrange("(j p) c -> p j c", p=128), (), [mk[a]])
                    kb.dma('sp', mv[a].t[:], c_mv[l, b].rearrange("(j p) c -> p j c", p=128), (), [mv[a]])
                    for h in range(4):
                        for j in range(2):
                            TR(ptr.t[:, h, j * 128:(j + 1) * 128], mk[a].t[:, j, h * 128:(h + 1) * 128], identf.t[:], [mk[a], identf], [ptr])
                    evac(mkTs[a].t[:], ptr.t[:], [ptr], [mkTs[a]])
                    CP(G_, mvb[a].t[:], mv[a].t[:], [mv[a]], [mvb[a]])
                    attn_unit(
                        tl, 1, 4, 256,
                        lambda h: (xqT_s.t[:, h, b:b + 1], [xqT_s]),
                        lambda h: (mkTs[a].t[:, h, :], [mkTs[a]]),
                        lambda h, kt, kn: (mvb[a].t[:, kt, 128 * h:128 * h + 128], [mvb[a]]),
                        128, sc, None, None,
                        lambda pO: evac(ocs.t[:, :, b:b + 1], pO.t[:, :, 0:1], [pO], [ocs]))
                kb.dma('sp', OCs.rearrange("(f p) t -> p f t", p=128)[:, :, T:T + NS], ocs.t[:], [ocs], ())
            kb.barrier()

        def stage_mrg(l):
            with contextlib.ExitStack() as st:
                wa = sb(st, "wa", [64, 8, D], BF)
                for h in range(8):
                    kb.dma('pool', wa.t[:, h, :], w_bra[l, h * 64:(h + 1) * 64, :], (), [wa], acc=True)
                wga = sb(st, "wga", [128, 4, D], BF); wgb = sb(st, "wgb", [128, 4, D], BF)
                wc = sb(st, "wc", [128, 4, D], BF); wo = sb(st, "wo", [128, 8, D], BF)
                load_w_begin(st)
                load_w(wga, lambda kt: w_ga[l, kt * 128:(kt + 1) * 128, :], 4, D)
                load_w(wgb, lambda kt: w_gb[l, kt * 128:(kt + 1) * 128, :], 4, D)
                load_w(wc, lambda kt: w_brm[l, kt * 128:(kt + 1) * 128, :], 4, D)
                load_w(wo, lambda kt: w_out[l, kt * 128:(kt + 1) * 128, :], 8, D)
                load_w_end()
                oa = [sb(st, f"goa{i}", [64, 8, 512], BF) for i in range(2)]
                zt = [sb(st, f"gz{i}", [128, 4, 512], BF) for i in range(2)]
                oc = [sb(st, f"goc{i}", [128, 4, 512], BF) for i in range(2)]
                sg = [sb(st, "gsg0", [128, 24, 512], BF)] * 2
                xt = [sb(st, "gx0", [128, 8, 512], F32)] * 2
                mg = sb(st, "mg", [128, 8, 512], BF)
                m1s = [sb(st, f"m1{i}", [128, 512], F32) for i in range(2)]
                m2s = [sb(st, f"m2{i}", [128, 512], F32) for i in range(2)]
                m3s = [sb(st, f"m3{i}", [128, 512], F32) for i in range(2)]
                pa = [psb(st, f"gpa{i}", [128, 512], F32) for i in range(8)]
                XTv = XT.rearrange("(k p) t -> p k t", p=128)
                pi = 0
                for ti, (t0, n) in enumerate(tiles):
                    a = ti % 2
                    kb.dma('sp', oa[a].t[:, :, 0:n], OAs.rearrange("(h p) t -> p h t", p=64)[:, :, t0:t0 + n], (), [oa[a]])
                    kb.dma('sp', zt[a].t[:, :, 0:n], Zs.rearrange("(f p) t -> p f t", p=128)[:, :, t0:t0 + n], (), [zt[a]])
                    kb.dma('sp', oc[a].t[:, :, 0:n], OCs.rearrange("(f p) t -> p f t", p=128)[:, :, t0:t0 + n], (), [oc[a]])
                    kb.dma('sp', sg[a].t[:, :, 0:n], SGs.rearrange("(f p) t -> p f t", p=128)[:, :, t0:t0 + n], (), [sg[a]])
                    kb.dma('sp', xt[a].t[:, :, 0:n], XTv[:, :, t0:t0 + n], (), [xt[a]])
                    for dm in range(8):
                        cs_ = slice(dm * 128, (dm + 1) * 128)
                        m1, m2, m3 = m1s[dm % 2], m2s[dm % 2], m3s[dm % 2]
                        pA, pGa, pGb, pC = pa[pi % 8], pa[(pi + 1) % 8], pa[(pi + 2) % 8], pa[(pi + 3) % 8]
                        pi += 4
                        for h in range(8):
                            MM(pA.t[:, 0:n], wa.t[:, h, cs_], oa[a].t[:, h, 0:n], h == 0, h == 7, [wa, oa[a]], [pA])
                        for k in range(4):
                            MM(pGa.t[:, 0:n], wga.t[:, k, cs_], zt[a].t[:, k, 0:n], k == 0, k == 3, [wga, zt[a]], [pGa])
                        for k in range(4):
                            MM(pGb.t[:, 0:n], wgb.t[:, k, cs_], zt[a].t[:, k, 0:n], k == 0, k == 3, [wgb, zt[a]], [pGb])
                        for k in range(4):
                            MM(pC.t[:, 0:n], wc.t[:, k, cs_], oc[a].t[:, k, 0:n], k == 0, k == 3, [wc, oc[a]], [pC])
                        TT(V, m1.t[:, 0:n], pA.t[:, 0:n], sg[a].t[:, dm, 0:n], ALU.mult, [pA, sg[a]], [m1])
                        ACTF(m2.t[:, 0:n], pGb.t[:, 0:n], AF.Sigmoid, [pGb], [m2])
                        TT(V, m2.t[:, 0:n], pGa.t[:, 0:n], m2.t[:, 0:n], ALU.mult, [pGa, m2], [m2])
                        TT(G_, m2.t[:, 0:n], m2.t[:, 0:n], sg[a].t[:, 8 + dm, 0:n], ALU.mult, [m2, sg[a]], [m2])
                        TT(V, m3.t[:, 0:n], pC.t[:, 0:n], sg[a].t[:, 16 + dm, 0:n], ALU.mult, [pC, sg[a]], [m3])
                        TT(G_, m1.t[:, 0:n], m1.t[:, 0:n], m2.t[:, 0:n], ALU.add, [m1, m2], [m1])
                        TT(G_, mg.t[:, dm, 0:n], m1.t[:, 0:n], m3.t[:, 0:n], ALU.add, [m1, m3], [mg])
                    for dm in range(8):
                        pO = pa[pi % 8]
                        pi += 1
                        for k in range(8):
                            MM(pO.t[:, 0:n], wo.t[:, k, dm * 128:(dm + 1) * 128], mg.t[:, k, 0:n], k == 0, k == 7, [wo, mg], [pO])
                        TT(V, xt[a].t[:, dm, 0:n], xt[a].t[:, dm, 0:n], pO.t[:, 0:n], ALU.add, [xt[a], pO], [xt[a]])
                    kb.dma('sp', XTv[:, :, t0:t0 + n], xt[a].t[:, :, 0:n], [xt[a]], ())
            kb.barrier()

        def stage_f1(l, half):
            HF = 11
            F0 = half * HF
            C0 = F0 * 128
            CW = HF * 128
            with contextlib.ExitStack() as st:
                wg = sb(st, "wg", [128, 8, CW], BF); wu = sb(st, "wu", [128, 8, CW], BF)
                load_w_begin(st)
                load_w(wg, lambda kt: w_fg[l, kt * 128:(kt + 1) * 128, C0:C0 + CW], 8, CW)
                load_w(wu, lambda kt: w_fu[l, kt * 128:(kt + 1) * 128, C0:C0 + CW], 8, CW)
                load_w_end()
                gf = sb(st, "gffn", [128, 8], F32)
                kb.dma('sp', gf.t[:], g_ffn[l].rearrange("(k p) -> p k", p=128), (), [gf])
                cw = sb(st, "cw", [128, 3, HF], F32); cb = sb(st, "cb", [128, HF], F32)
                for j3 in range(3):
                    kb.dma('sp', cw.t[:, j3, :], conv_w[l, j3, C0:C0 + CW].rearrange("(f p) -> p f", p=128), (), [cw], acc=True)
                kb.dma('sp', cb.t[:], conv_b[l, C0:C0 + CW].rearrange("(f p) -> p f", p=128), (), [cb])
                xt = [sb(st, "fx0", [128, 8, 512], F32)] * 2
                hT = [sb(st, f"fh{i}", [128, 8, 512], BF) for i in range(2)]
                sq = sb(st, "fsq", [128, 8, 512], BF); rt = sb(st, "frt", [128, 512], F32)
                pss = psb(st, "fpss", [128, 512], F32)
                pg = [psb(st, f"fpg{i}", [128, 512], F32) for i in range(3)]
                pu = [psb(st, f"fpu{i}", [128, 512], F32) for i in range(3)]
                apad = sb(st, "apad", [128, HF, 514], F32)
                MEMSET(V, apad.t[:, :, 0:2], 0.0, [apad])
                apb = [Buf(apad.t) for _ in range(HF)]
                for b_ in apb:
                    b_.w = dict(apad.w)
                cvs = [sb(st, f"cv{i}", [128, 512], F32) for i in range(2)]
                cgs = [sb(st, f"cg{i}", [128, 512], F32) for i in range(2)]
                gout = [sb(st, "gout0", [128, HF, 512], BF)] * 2
                cvt = sb(st, "cvt", [NS, 2, CW], F32)
                kb.dma('sp', cvt.t[:], c_cv[l, :, :, C0:C0 + CW], (), [cvt])
                if half == 0:
                    kb.dma('sp', o_scv[l, :, 0, :], c_cv[l, :, 1, :], (), ())
                sprev = sb(st, "sprev", [128, 2, HF, NS], F32)
                for j in range(2):
                    for f in range(HF):
                        TR(pg[j].t[:, f * 16:(f + 1) * 16], cvt.t[:, j, f * 128:(f + 1) * 128], identf.t[0:NS, 0:NS], [cvt, identf], [pg[j]])
                    evac(sprev.t[:, j, :, :], pg[j].t[:, 0:HF * 16].rearrange("p (f b) -> p f b", b=NS), [pg[j]], [sprev])
                asmp = sb(st, "asmp", [128, HF, NS], F32)
                XTv = XT.rearrange("(k p) t -> p k t", p=128)
                pi = 0
                def prol(ti):
                    t0_, n_ = tiles[ti]
                    kb.dma('sp', xt[ti % 2].t[:, :, 0:n_], XTv[:, :, t0_:t0_ + n_], (), [xt[ti % 2]])
                    rmsnorm_tile(xt[ti % 2], n_, gf, hT[ti % 2], sq, rt, pss)
                prol(0)
                for ti, (t0, n) in enumerate(tiles):
                    a = ti % 2
                    smp = n == NS
                    if ti + 1 < len(tiles):
                        prol(ti + 1)
                    go = gout[a]
                    for f in range(HF):
                        p1, p2 = pg[pi % 3], pu[pi % 3]
                        pi += 1
                        for k in range(8):
                            MM(p1.t[:, 0:n], wg.t[:, k, f * 128:(f + 1) * 128], hT[a].t[:, k, 0:n], k == 0, k == 7, [wg, hT[a]], [p1])
                        for k in range(8):
                            MM(p2.t[:, 0:n], wu.t[:, k, f * 128:(f + 1) * 128], hT[a].t[:, k, 0:n], k == 0, k == 7, [wu, hT[a]], [p2])
                        w0, w1_, w2_ = cw.t[:, 0, f:f + 1], cw.t[:, 1, f:f + 1], cw.t[:, 2, f:f + 1]
                        cv, cg = cvs[f % 2], cgs[f % 2]
                        ab = apb[f]
                        if not smp:
                            CP(S_, apad.t[:, f, 2:514], p1.t[:, :], [p1], [ab])
                            TS(V, cv.t[:], apad.t[:, f, 0:512], w0, cb.t[:, f:f + 1], ALU.mult, ALU.add, [ab, cw, cb], [cv])
                            STT(cv.t[:], apad.t[:, f, 1:513], w1_, cv.t[:], ALU.mult, ALU.add, [ab, cw, cv], [cv])
                            STT(cv.t[:], apad.t[:, f, 2:514], w2_, cv.t[:], ALU.mult, ALU.add, [ab, cw, cv], [cv])
                            ACTF(cg.t[:], cv.t[:], AF.Gelu, [cv], [cg])
                            TT(V, go.t[:, f, :], cg.t[:], p2.t[:, :], ALU.mult, [cg, p2], [go])
                            CP(G_, apad.t[:, f, 0:2], apad.t[:, f, 512:514], [ab], [ab])
                        else:
                            CP(S_, asmp.t[:, f, :], p1.t[:, 0:n], [p1], [asmp])
                            TS(V, cv.t[:, 0:n], sprev.t[:, 0, f, :], w0, cb.t[:, f:f + 1], ALU.mult, ALU.add, [sprev, cw, cb], [cv])
                            STT(cv.t[:, 0:n], sprev.t[:, 1, f, :], w1_, cv.t[:, 0:n], ALU.mult, ALU.add, [sprev, cw, cv], [cv])
                            STT(cv.t[:, 0:n], asmp.t[:, f, :], w2_, cv.t[:, 0:n], ALU.mult, ALU.add, [asmp, cw, cv], [cv])
                            ACTF(cg.t[:, 0:n], cv.t[:, 0:n], AF.Gelu, [cv], [cg])
                            TT(V, go.t[:, f, 0:n], cg.t[:, 0:n], p2.t[:, 0:n], ALU.mult, [cg, p2], [go])
                    kb.dma('sp', Gs.rearrange("(f p) t -> p f t", p=128)[:, F0:F0 + HF, t0:t0 + n], go.t[:, :, 0:n], [go], ())
                pbs = (pg[0], pg[1], pg[2])
                pcs = sb(st, "pcs", [2, CW], F32)
                for f in range(HF):
                    TR(pbs[f // 4].t[0:2, (f % 4) * 128:(f % 4) * 128 + 128], apad.t[:, f, 0:2], identf.t[:], [apb[f], identf], list(pbs))
                for gi, pb in enumerate(pbs):
                    w_ = 512 if gi < 2 else CW - 1024
                    evac(pcs.t[:, gi * 512:gi * 512 + w_], pb.t[0:2, 0:w_], [pb], [pcs])
                kb.dma('sp', o_pcv[l, :, C0:C0 + CW], pcs.t[:], [pcs], ())
                scs = sb(st, "scs", [NS, CW], F32)
                pbu = (pu[0], pu[1], pu[2])
                for f in range(HF):
                    TR(pbu[f // 4].t[0:NS, (f % 4) * 128:(f % 4) * 128 + 128], asmp.t[:, f, :], identf.t[:], [asmp, identf], list(pbu))
                for gi, pb in enumerate(pbu):
                    w_ = 512 if gi < 2 else CW - 1024
                    evac(scs.t[:, gi * 512:gi * 512 + w_], pb.t[0:NS, 0:w_], [pb], [scs])
                kb.dma('sp', o_scv[l, :, 1, C0:C0 + CW], scs.t[:], [scs], ())
            kb.barrier()

        def stage_f2(l):
            with contextlib.ExitStack() as st:
                wd = sb(st, "wd", [128, NFT, D], BF)
                load_w_begin(st)
                load_w(wd, lambda kt: w_fd[l, kt * 128:(kt + 1) * 128, :], NFT, D)
                load_w_end()
                gt = [sb(st, "dg0", [128, NFT, 512], BF)] * 2
                xt = [sb(st, "dx0", [128, 8, 512], F32)] * 2
                pd = [psb(st, f"dpd{i}", [128, 512], F32) for i in range(4)]
                XTv = XT.rearrange("(k p) t -> p k t", p=128)
                pi = 0
                for ti, (t0, n) in enumerate(tiles):
                    a = ti % 2
                    kb.dma('sp', gt[a].t[:, :, 0:n], Gs.rearrange("(f p) t -> p f t", p=128)[:, :, t0:t0 + n], (), [gt[a]])
                    kb.dma('sp', xt[a].t[:, :, 0:n], XTv[:, :, t0:t0 + n], (), [xt[a]])
                    for dm in range(8):
                        p = pd[pi % 4]
                        pi += 1
                        for k in range(NFT):
                            MM(p.t[:, 0:n], wd.t[:, k, dm * 128:(dm + 1) * 128], gt[a].t[:, k, 0:n], k == 0, k == NFT - 1, [wd, gt[a]], [p])
                        TT(V, xt[a].t[:, dm, 0:n], xt[a].t[:, dm, 0:n], p.t[:, 0:n], ALU.add, [xt[a], p], [xt[a]])
                    kb.dma('sp', XTv[:, :, t0:t0 + n], xt[a].t[:, :, 0:n], [xt[a]], ())
            kb.barrier()

        def stage_fin():
            with contextlib.ExitStack() as st:
                gfn = sb(st, "gfin", [128, 8], F32)
                kb.dma('sp', gfn.t[:], g_fin[0].rearrange("(k p) -> p k", p=128), (), [gfn])
                xt = [sb(st, f"nx{i}", [128, 8, 512], F32) for i in range(2)]
                yo = [sb(st, f"ny{i}", [128, 8, 512], F32) for i in range(2)]
                sq = sb(st, "nsq", [128, 8, 512], BF); rt = sb(st, "nrt", [128, 512], F32)
                pss = psb(st, "npss", [128, 512], F32)
                pt = [psb(st, f"npt{i}", [128, 1024], F32) for i in range(2)]
                yt = [sb(st, f"nyt{i}", [128, D], F32) for i in range(2)]
                XTv = XT.rearrange("(k p) t -> p k t", p=128)
                bi = 0
                for ti, (t0, n) in enumerate(tiles):
                    a = ti % 2
                    kb.dma('sp', xt[a].t[:, :, 0:n], XTv[:, :, t0:t0 + n], (), [xt[a]])
                    rmsnorm_tile(xt[a], n, gfn, yo[a], sq, rt, pss)
                    for j in range((n + 127) // 128):
                        m = min(128, n - j * 128)
                        b2 = bi % 2
                        bi += 1
                        for k in range(8):
                            TR(pt[b2].t[0:m, k * 128:(k + 1) * 128], yo[a].t[:, k, j * 128:j * 128 + m], identf.t[:], [yo[a], identf], [pt[b2]])
                        evac(yt[b2].t[0:m, :], pt[b2].t[0:m, :], [pt[b2]], [yt[b2]])
                        if n == NS:
                            kb.dma('sp', o_ys[:, :], yt[b2].t[0:m, :], [yt[b2]], ())
                        else:
                            kb.dma('sp', o_yp[t0 + j * 128:t0 + j * 128 + 128, :], yt[b2].t[:], [yt[b2]], ())
            kb.barrier()

        import os
        dbg = os.environ.get("KSTAGES", "mem,in,att,s5,matt,mrg,f1,f2").split(",")
        nl = int(os.environ.get("KLAYERS", str(DEPTH)))
        stage_p0()
        for l in range(nl):
            if "mem" in dbg: stage_mem(l)
            if "in" in dbg: stage_in(l)
            if "att" in dbg: stage_att(l)
            if "s5" in dbg: stage_s5(l)
            if "matt" in dbg: stage_matt(l)
            if "mrg" in dbg: stage_mrg(l)
            if "f1" in dbg:
                stage_f1(l, 0)
                stage_f1(l, 1)
            if "f2" in dbg: stage_f2(l)
        stage_fin()
        kb.barrier()
    return nc


T_FULL = 4096
_cache = {}


def _in_maps(inp, T):
    maps = []
    for c in range(8):
        b = c % 4
        sl = slice(NS * c, NS * (c + 1))
        m = {
            "xp": np.ascontiguousarray(inp["x_prompt"][b, :T]),
            "xs": np.ascontiguousarray(inp["x_sample"][sl, 0]),
            "mem": np.ascontiguousarray(inp["mem_prompt"][b]),
            "c_swk": np.ascontiguousarray(inp["cache_swa_k"][:, sl].reshape(DEPTH, NS, 128, 128)),
            "c_swv": np.ascontiguousarray(inp["cache_swa_v"][:, sl].reshape(DEPTH, NS, 128, 128)),
            "c_sre": np.ascontiguousarray(inp["state_ssm_re"][:, sl].reshape(DEPTH, NS, 2048)),
            "c_sim": np.ascontiguousarray(inp["state_ssm_im"][:, sl].reshape(DEPTH, NS, 2048)),
            "c_cv": np.ascontiguousarray(inp["cache_ffn_conv"][:, sl]),
            "c_mk": np.ascontiguousarray(inp["cache_mem_k"][:, sl].reshape(DEPTH, NS, 256, 512)),
            "c_mv": np.ascontiguousarray(inp["cache_mem_v"][:, sl].reshape(DEPTH, NS, 256, 512)),
            "norm_final_g": np.ascontiguousarray(inp["norm_final_g"].reshape(1, D)),
        }
        for k in ("norm_mix_g", "norm_ffn_g", "norm_mem_g", "w_in", "sinks", "w_br_attn", "lam_re", "lam_im", "log_dt",
                  "b_re", "b_im", "c_re", "c_im", "d_skip", "w_glu_a", "w_glu_b", "w_mem_kv", "w_br_mem", "w_out",
                  "w_ffn_gate", "w_ffn_up", "conv_w", "conv_b", "w_ffn_down"):
            m[k] = np.ascontiguousarray(inp[k])
        maps.append(m)
    return maps


def run(inp, T):
    if T not in _cache:
        _cache[T] = build(T)
    nc = _cache[T]
    inp = {k: np.asarray(v, dtype=np.float32) for k, v in inp.items()}
    import os
    ncores = int(os.environ.get("KCORES", "8"))
    res = run_bass_kernel_spmd(nc, _in_maps(inp, T)[:ncores], core_ids=list(range(ncores))).results
    res = [res[c % ncores] for c in range(8)]
    P = lambda name: np.stack([res[b][name] for b in range(4)], axis=1)
    Sm = lambda name: np.concatenate([res[c][name] for c in range(8)], axis=1)
    y_p = np.stack([res[b]["o_yp"] for b in range(4)], axis=0)
    y_s = np.concatenate([res[c]["o_ys"] for c in range(8)], axis=0)[:, None, :]
    return (y_p, y_s,
            P("o_pk").reshape(DEPTH, 4, 128, 2, 64), P("o_pv").reshape(DEPTH, 4, 128, 2, 64),
            P("o_pre").reshape(DEPTH, 4, 32, 64), P("o_pim").reshape(DEPTH, 4, 32, 64),
            P("o_pcv"), P("o_pmk").reshape(DEPTH, 4, 256, 4, 128), P("o_pmv").reshape(DEPTH, 4, 256, 4, 128),
            Sm("o_sk").reshape(DEPTH, 128, 128, 2, 64), Sm("o_sv").reshape(DEPTH, 128, 128, 2, 64),
            Sm("o_sre").reshape(DEPTH, 128, 32, 64), Sm("o_sim").reshape(DEPTH, 128, 32, 64), Sm("o_scv"))


def kernel(**inputs):
    return run(inputs, T_FULL)
```

```python
import contextlib
import math
import os
import numpy as np
import concourse.bass as bass
import concourse.mybir as mybir
from concourse.bass_utils import run_bass_kernel_spmd

F32 = mybir.dt.float32
BF = mybir.dt.bfloat16
I32 = mybir.dt.int32
AF = mybir.ActivationFunctionType
ALU = mybir.AluOpType
AX = mybir.AxisListType

D = 1024
DEPTH = 4
NS = 16
DFF = 2816
NFT = 22
INC = 4864
EPS = 1e-6
TWO_PI = 2.0 * math.pi


class Buf:
    def __init__(self, t, psum=False):
        self.t = t
        self.w = {}
        self.r = {}
        self.psum = psum


class KB:
    def __init__(self, nc, es):
        self.nc = nc
        self.es = es
        self.E = {'pe': nc.tensor, 'act': nc.scalar, 'dve': nc.vector, 'pool': nc.gpsimd, 'sp': nc.sync}
        self.cur = {}
        self.cnt = {}
        self.nsem = 0
        self.waited = {e: {} for e in self.E}
        for e in self.E:
            self._newesem(e)
        self.dq = {q: [[self._sem(), 0] for _ in range(8)] for q in ('sp', 'pool', 'act')}
        self.dqi = {q: 0 for q in self.dq}

    def _sem(self):
        self.nsem += 1
        return self.es.enter_context(self.nc.semaphore(f"sm{self.nsem}"))

    def _newesem(self, e):
        self.cur[e] = self._sem()
        self.cnt[e] = 0

    def wait(self, e, sem, val):
        k = id(sem)
        if self.waited[e].get(k, 0) >= val:
            return
        self.E[e].wait_ge(sem, val)
        self.waited[e][k] = val

    def deps(self, e, r, w, acc=False):
        own = self.cur[e]
        for b in r:
            for sem, val in b.w.values():
                if e == 'pe' and sem is own:
                    continue
                self.wait(e, sem, val)
            if b.psum:
                for sem, val in b.r.values():
                    if sem is not own:
                        self.wait(e, sem, val)
        if acc:
            return
        for b in w:
            for sem, val in list(b.w.values()) + list(b.r.values()):
                if e == 'pe' and sem is own:
                    continue
                self.wait(e, sem, val)

    def mark(self, sem, val, r, w, acc=False):
        for b in r:
            b.r[id(sem)] = (sem, val)
        for b in w:
            if acc:
                b.w[id(sem)] = (sem, val)
            else:
                b.w = {id(sem): (sem, val)}
                b.r = {}

    def op(self, e, fn, r=(), w=(), acc=False):
        self.deps(e, r, w, acc)
        ins = fn()
        self.cnt[e] += 1
        ins.then_inc(self.cur[e], 1)
        self.mark(self.cur[e], self.cnt[e], r, w, acc)

    def dma(self, q, out, in_, r=(), w=(), acc=False):
        slot = self.dq[q][self.dqi[q] % 8]
        self.dqi[q] += 1
        self.wait(q, slot[0], slot[1])
        self.deps(q, r, w, acc)
        self.E[q].dma_start(out=out, in_=in_).then_inc(slot[0], 16)
        slot[1] += 16
        self.mark(slot[0], slot[1], r, w, acc)

    def barrier(self):
        for e in self.E:
            for e2 in self.E:
                if e2 != e and self.cnt[e2] > 0:
                    self.wait(e, self.cur[e2], self.cnt[e2])
            for q in self.dq:
                for sem, val in self.dq[q]:
                    if val > 0:
                        self.wait(e, sem, val)
        for e in self.E:
            if self.cnt[e] > 20000:
                self._newesem(e)


def build(T):
    assert T % 512 == 0
    NTL = T // 512
    TA = T + NS
    nc = bass.Bass("TRN2", target_bir_lowering=False)

    def din(name, shape, dt=F32):
        return nc.dram_tensor(name, list(shape), dt, kind="ExternalInput").ap()

    def dout(name, shape):
        return nc.dram_tensor(name, list(shape), F32, kind="ExternalOutput").ap()

    def dscr(name, shape, dt):
        return nc.dram_tensor(name, list(shape), dt, kind="Internal").ap()

    xp = din("xp", [T, D]); xs = din("xs", [NS, D]); mem = din("mem", [256, D])
    c_swk = din("c_swk", [DEPTH, NS, 128, 128]); c_swv = din("c_swv", [DEPTH, NS, 128, 128])
    c_sre = din("c_sre", [DEPTH, NS, 2048]); c_sim = din("c_sim", [DEPTH, NS, 2048])
    c_cv = din("c_cv", [DEPTH, NS, 2, DFF])
    c_mk = din("c_mk", [DEPTH, NS, 256, 512]); c_mv = din("c_mv", [DEPTH, NS, 256, 512])
    g_mix = din("norm_mix_g", [DEPTH, D]); g_ffn = din("norm_ffn_g", [DEPTH, D])
    g_mem = din("norm_mem_g", [DEPTH, D]); g_fin = din("norm_final_g", [1, D])
    w_in = din("w_in", [DEPTH, D, INC]); sinks = din("sinks", [DEPTH, 8])
    w_bra = din("w_br_attn", [DEPTH, 512, D])
    lam_re = din("lam_re", [DEPTH, 32, 64]); lam_im = din("lam_im", [DEPTH, 32, 64])
    log_dt = din("log_dt", [DEPTH, 32])
    b_re = din("b_re", [DEPTH, 32, 64, 16]); b_im = din("b_im", [DEPTH, 32, 64, 16])
    c_re = din("c_re", [DEPTH, 32, 16, 64]); c_im = din("c_im", [DEPTH, 32, 16, 64])
    d_skip = din("d_skip", [DEPTH, 512])
    w_ga = din("w_glu_a", [DEPTH, 512, D]); w_gb = din("w_glu_b", [DEPTH, 512, D])
    w_mkv = din("w_mem_kv", [DEPTH, D, 1024]); w_brm = din("w_br_mem", [DEPTH, 512, D])
    w_out = din("w_out", [DEPTH, D, D])
    w_fg = din("w_ffn_gate", [DEPTH, D, DFF]); w_fu = din("w_ffn_up", [DEPTH, D, DFF])
    conv_w = din("conv_w", [DEPTH, 3, DFF]); conv_b = din("conv_b", [DEPTH, DFF])
    w_fd = din("w_ffn_down", [DEPTH, DFF, D])
    o_yp = dout("o_yp", [T, D]); o_ys = dout("o_ys", [NS, D])
    o_pk = dout("o_pk", [DEPTH, 128, 128]); o_pv = dout("o_pv", [DEPTH, 128, 128])
    o_pre = dout("o_pre", [DEPTH, 2048]); o_pim = dout("o_pim", [DEPTH, 2048])
    o_pcv = dout("o_pcv", [DEPTH, 2, DFF])
    o_pmk = dout("o_pmk", [DEPTH, 256, 512]); o_pmv = dout("o_pmv", [DEPTH, 256, 512])
    o_sk = dout("o_sk", [DEPTH, NS, 128, 128]); o_sv = dout("o_sv", [DEPTH, NS, 128, 128])
    o_sre = dout("o_sre", [DEPTH, NS, 2048]); o_sim = dout("o_sim", [DEPTH, NS, 2048])
    o_scv = dout("o_scv", [DEPTH, NS, 2, DFF])
    XT = dscr("XT", [D, TA], F32)
    Qs = dscr("Qs", [8, 64, TA], BF)
    Ks = dscr("Ks", [2, 64, 128 + TA], BF)
    Vs = dscr("Vs", [128 + T, 128], BF)
    Us = dscr("Us", [512, TA], BF)
    XQs = dscr("XQs", [512, TA], BF)
    SGs = dscr("SGs", [3072, TA], BF)
    OAs = dscr("OAs", [512, TA], BF)
    Zs = dscr("Zs", [512, TA], BF)
    OCs = dscr("OCs", [512, TA], BF)
    Gs = dscr("Gs", [DFF, TA], BF)
    VNs = dscr("VNs", [NS, 128], F32)

    tiles = [(i * 512, 512) for i in range(NTL)] + [(T, NS)]

    with contextlib.ExitStack() as es:
        es.enter_context(nc.allow_non_contiguous_dma(reason="small parameter tables"))
        kb = KB(nc, es)

        uid = [0]

        def sb(st, name, shape, dt):
            uid[0] += 1
            return Buf(st.enter_context(nc.sbuf_tensor(f"{name}_{uid[0]}", list(shape), dt)))

        def sbn(st, name, shape, dt, n):
            t = st.enter_context(nc.sbuf_tensor(name, list(shape), dt))
            return t, [Buf(t) for _ in range(n)]

        def psb(st, name, shape, dt):
            uid[0] += 1
            nb = int(np.prod(shape[1:])) * (2 if dt == BF else 4)
            assert nb % 2048 == 0, (name, shape)
            return Buf(st.enter_context(nc.psum_tensor(f"{name}_{uid[0]}", list(shape), dt)), psum=True)

        V, S_, G_, PE = 'dve', 'act', 'pool', 'pe'

        def TT(e, out, in0, in1, op, r, w):
            kb.op(e, lambda: kb.E[e].tensor_tensor(out=out, in0=in0, in1=in1, op=op), r, w)

        def TS(e, out, in0, s1, s2, op0, op1, r, w):
            if op1 is None:
                kb.op(e, lambda: kb.E[e].tensor_scalar(out=out, in0=in0, scalar1=s1, scalar2=None, op0=op0), r, w)
            else:
                kb.op(e, lambda: kb.E[e].tensor_scalar(out=out, in0=in0, scalar1=s1, scalar2=s2, op0=op0, op1=op1), r, w)

        def STT(out, in0, sc, in1, op0, op1, r, w):
            kb.op(V, lambda: nc.vector.scalar_tensor_tensor(out=out, in0=in0, scalar=sc, in1=in1, op0=op0, op1=op1), r, w)

        def ACTF(out, in_, func, r, w, bias=None, scale=None, accum=None):
            kw = {}
            if bias is not None:
                kw['bias'] = bias
            if scale is not None:
                kw['scale'] = scale
            if accum is not None:
                kw['accum_out'] = accum
            kb.op(S_, lambda: nc.scalar.activation(out=out, in_=in_, func=func, **kw), r, w)

        def CP(e, out, in_, r, w, acc=False):
            if e == S_:
                kb.op(e, lambda: nc.scalar.copy(out=out, in_=in_), r, w, acc)
            else:
                kb.op(e, lambda: kb.E[e].tensor_copy(out=out, in_=in_), r, w, acc)

        def MM(out, lhsT, rhs, start, stop, r, w):
            kb.op(PE, lambda: nc.tensor.matmul(out, lhsT, rhs, start=start, stop=stop), r, w)

        def TR(out, in_, ident, r, w):
            kb.op(PE, lambda: nc.tensor.transpose(out, in_, ident), r, w)

        def MEMSET(e, ap, val, w):
            kb.op(e, lambda: kb.E[e].memset(ap, val), (), w)

        wst = {"bufs": None, "i": 0}

        def load_w_begin(st):
            wst["bufs"] = [sb(st, f"wstg{i}", [128, 2048], F32) for i in range(2)]

        def load_w(dst, dram_rows_fn, nkt, ncols, q='pool'):
            for kt in range(nkt):
                src_ = dram_rows_fn(kt)
                for c0 in range(0, ncols, 2048):
                    c1 = min(ncols, c0 + 2048)
                    wst["n"] = wst.get("n", 0) + 1
                    if wst["bufs"] is None or wst["n"] % 3 != 0:
                        kb.dma('pool', dst.t[:, kt, c0:c1], src_[:, c0:c1], (), [dst], acc=True)
                        continue
                    sg_ = wst["bufs"][wst["i"] % 2]
                    e = (S_, V)[wst["i"] % 2]
                    wst["i"] += 1
                    kb.dma('sp', sg_.t[:, 0:c1 - c0], src_[:, c0:c1], (), [sg_])
                    CP(e, dst.t[:, kt, c0:c1], sg_.t[:, 0:c1 - c0], [sg_], [dst], acc=True)

        def load_w_end():
            wst["bufs"] = None

        evac_rr = [0]

        def evac(out, in_, r, w):
            e = (S_, V)[evac_rr[0] % 2]
            evac_rr[0] += 1
            CP(e, out, in_, r, w)

        cst = es
        ident_i = sb(cst, "ident_i", [128, 128], I32)
        identf = sb(cst, "identf", [128, 128], F32)
        identb = sb(cst, "identb", [128, 128], BF)
        onesf = sb(cst, "onesf", [128, 128], F32)
        kb.op(G_, lambda: nc.gpsimd.iota(ident_i.t[:], pattern=[[1, 128]], base=0, channel_multiplier=-1), (), [ident_i])
        TS(V, identf.t[:], ident_i.t[:], 0, None, ALU.is_equal, None, [ident_i], [identf])
        CP(V, identb.t[:], identf.t[:], [identf], [identb])
        MEMSET(V, onesf.t[:], 1.0, [onesf])
        onesb = sb(cst, "onesb", [128, 128], BF)
        MEMSET(V, onesb.t[:], 1.0, [onesb])
        dist_i = sb(cst, "dist_i", [128, 256], I32)
        distf = sb(cst, "distf", [128, 256], F32)
        mskf = sb(cst, "mskf", [128, 256], F32)
        msk2 = sb(cst, "msk2", [128, 256], F32)
        biasA = sb(cst, "biasA", [128, 8, 256], F32)
        biasA0 = sb(cst, "biasA0", [128, 8, 256], F32)
        kb.op(G_, lambda: nc.gpsimd.iota(dist_i.t[:], pattern=[[-1, 256]], base=128, channel_multiplier=1), (), [dist_i])
        CP(V, distf.t[:], dist_i.t[:], [dist_i], [distf])
        TS(V, mskf.t[:], distf.t[:], 0.0, None, ALU.is_ge, None, [distf], [mskf])
        TS(V, msk2.t[:], distf.t[:], 128.0, None, ALU.is_le, None, [distf], [msk2])
        TT(V, mskf.t[:], mskf.t[:], msk2.t[:], ALU.mult, [mskf, msk2], [mskf])
        TS(V, msk2.t[:], mskf.t[:], -1.0, 30000.0, ALU.add, ALU.mult, [mskf], [msk2])
        TT(V, distf.t[:], distf.t[:], mskf.t[:], ALU.mult, [distf, mskf], [distf])
        for h in range(8):
            slope = 2.0 ** (-(h + 1))
            STT(biasA.t[:, h, :], distf.t[:], -slope, msk2.t[:], ALU.mult, ALU.add, [distf, msk2], [biasA])
        CP(V, biasA0.t[:], biasA.t[:], [biasA], [biasA0])
        MEMSET(V, biasA0.t[:, :, 0:128], -30000.0, [biasA0])
        bs_i = sb(cst, "bs_i", [4, 129], I32)
        bs_f = sb(cst, "bs_f", [4, 129], F32)
        biasS = sb(cst, "biasS", [4, 2, 129], F32)
        slp = sb(cst, "slp", [4, 2], F32)
        slp_i = sb(cst, "slp_i", [4, 2], I32)
        kb.op(G_, lambda: nc.gpsimd.iota(bs_i.t[:], pattern=[[-1, 129]], base=128, channel_multiplier=0), (), [bs_i])
        CP(V, bs_f.t[:], bs_i.t[:], [bs_i], [bs_f])
        kb.op(G_, lambda: nc.gpsimd.iota(slp_i.t[:], pattern=[[4, 2]], base=1, channel_multiplier=1), (), [slp_i])
        CP(V, slp.t[:], slp_i.t[:], [slp_i], [slp])
        ACTF(slp.t[:], slp.t[:], AF.Exp, [slp], [slp], scale=-math.log(2.0))
        for k2 in range(2):
            TS(V, biasS.t[:, k2, :], bs_f.t[:], slp.t[:, k2:k2 + 1], -1.0, ALU.mult, ALU.mult, [bs_f, slp], [biasS])
        jidx_i = sb(cst, "jidx_i", [128, 512], I32)
        jidx = sb(cst, "jidx", [128, 512], F32)
        kb.op(G_, lambda: nc.gpsimd.iota(jidx_i.t[:], pattern=[[1, 512]], base=0, channel_multiplier=0), (), [jidx_i])
        CP(V, jidx.t[:], jidx_i.t[:], [jidx_i], [jidx])
        xqT_s = sb(cst, "xqT_s", [128, 4, NS], BF)
        qT_s = sb(cst, "qT_s", [64, 8, NS], BF)
        kT_s = sb(cst, "kT_s", [64, 2, NS], BF)
        uT_s = sb(cst, "uT_s", [128, 4, NS], BF)
        mkT = sb(cst, "mkT", [128, 4, 256], BF)
        mvT = sb(cst, "mvT", [128, 2, 512], BF)
        kb.barrier()

        def stage_p0():
            with contextlib.ExitStack() as st:
                xin = [sb(st, f"xin{i}", [128, D], F32) for i in range(2)]
                xo = [sb(st, f"xo{i}", [128, 8, 128], F32) for i in range(2)]
                pt = [psb(st, f"p0t{i}", [128, 1024], F32) for i in range(2)]
                blocks = [(j * 128, 128, xp[j * 128:(j + 1) * 128, :]) for j in range(T // 128)] + [(T, NS, xs[:, :])]
                for bi, (t0, n, src) in enumerate(blocks):
                    a = bi % 2
                    kb.dma('sp', xin[a].t[0:n, :], src, (), [xin[a]])
                    for k in range(8):
                        TR(pt[a].t[:, k * 128:k * 128 + n], xin[a].t[0:n, k * 128:(k + 1) * 128], identf.t[0:n, 0:n], [xin[a], identf], [pt[a]])
                    evac(xo[a].t[:, :, 0:n], pt[a].t[:].rearrange("p (k t) -> p k t", k=8)[:, :, 0:n], [pt[a]], [xo[a]])
                    kb.dma('sp', XT.rearrange("(k p) t -> p k t", p=128)[:, :, t0:t0 + n], xo[a].t[:, :, 0:n], [xo[a]], ())
            kb.barrier()

        def rmsnorm_tile(xt, n, gcol, hout, sq, rt, pss, r_extra=()):
            ACTF(sq.t[:, :, 0:n], xt.t[:, :, 0:n], AF.Square, [xt], [sq])
            for k in range(8):
                MM(pss.t[:, 0:n], onesb.t[:], sq.t[:, k, 0:n], k == 0, k == 7, [onesb, sq], [pss])
            ACTF(rt.t[:, 0:n], pss.t[:, 0:n], AF.Sqrt, [pss], [rt], bias=EPS, scale=1.0 / D)
            kb.op(V, lambda: nc.vector.reciprocal(out=rt.t[:, 0:n], in_=rt.t[:, 0:n]), [rt], [rt])
            for k in range(8):
                STT(hout.t[:, k, 0:n], xt.t[:, k, 0:n], gcol.t[:, k:k + 1], rt.t[:, 0:n], ALU.mult, ALU.mult, [xt, gcol, rt], [hout])

        def stage_mem(l):
            with contextlib.ExitStack() as st:
                wm = sb(st, "wm", [128, 8, 1024], BF)
                load_w_begin(st)
                load_w(wm, lambda kt: w_mkv[l, kt * 128:(kt + 1) * 128, :], 8, 1024)
                load_w_end()
                gm = sb(st, "gm", [128, 8], F32)
                kb.dma('sp', gm.t[:], g_mem[l].rearrange("(k p) -> p k", p=128), (), [gm])
                min_ = sb(st, "min_", [128, 2, D], F32)
                kb.dma('sp', min_.t[:], mem.rearrange("(j p) d -> p j d", p=128), (), [min_])
                mx = sb(st, "mx", [128, 8, 512], F32)
                mh = sb(st, "mh", [128, 8, 512], BF)
                sq = sb(st, "msq", [128, 8, 512], BF)
                rt = sb(st, "mrt", [128, 512], F32)
                pss = psb(st, "mpss", [128, 512], F32)
                pt = psb(st, "mpt", [128, 1024], F32)
                po = [psb(st, f"mpo{i}", [128, 512], F32) for i in range(2)]
                mo = [sb(st, f"mo{i}", [128, 512], F32) for i in range(2)]
                for j in range(2):
                    for k in range(8):
                        TR(pt.t[:, k * 128:(k + 1) * 128], min_.t[:, j, k * 128:(k + 1) * 128], identf.t[:], [min_, identf], [pt])
                    evac(mx.t[:, :, j * 128:(j + 1) * 128], pt.t[:].rearrange("p (k t) -> p k t", k=8), [pt], [mx])
                rmsnorm_tile(mx, 256, gm, mh, sq, rt, pss)
                cnt = 0
                for j in range(2):
                    for half in range(2):
                        a = cnt % 2
                        cnt += 1
                        for k in range(8):
                            MM(po[a].t[:, :], mh.t[:, k, j * 128:(j + 1) * 128], wm.t[:, k, half * 512:(half + 1) * 512], k == 0, k == 7, [mh, wm], [po[a]])
                        evac(mo[a].t[:], po[a].t[:], [po[a]], [mo[a]])
                        dst = (o_pmk, o_pmv)[half]
                        kb.dma('sp', dst[l, j * 128:(j + 1) * 128, :], mo[a].t[:], [mo[a]], ())
                        if half == 1:
                            CP(V, mvT.t[:, j, :], mo[a].t[:], [mo[a]], [mvT])
                for h in range(4):
                    a = h % 2
                    for k in range(8):
                        MM(po[a].t[:, 0:256], wm.t[:, k, h * 128:(h + 1) * 128], mh.t[:, k, 0:256], k == 0, k == 7, [mh, wm], [po[a]])
                    evac(mkT.t[:, h, :], po[a].t[:, 0:256], [po[a]], [mkT])
            kb.barrier()

        def stage_in(l):
            with contextlib.ExitStack() as st:
                win = sb(st, "win", [128, 8, INC], BF)
                load_w_begin(st)
                load_w(win, lambda kt: w_in[l, kt * 128:(kt + 1) * 128, :], 8, INC)
                load_w_end()
                gm = sb(st, "gmix", [128, 8], F32)
                kb.dma('sp', gm.t[:], g_mix[l].rearrange("(k p) -> p k", p=128), (), [gm])
                xt = [sb(st, "xt0", [128, 8, 512], F32)] * 2
                hT = [sb(st, f"hT{i}", [128, 8, 512], BF) for i in range(2)]
                sq = sb(st, "sq", [128, 8, 512], BF)
                rt = sb(st, "rt", [128, 512], F32)
                pss = psb(st, "pss", [128, 512], F32)
                pp = [psb(st, f"pp{i}", [128, 512], F32) for i in range(5)]
                stq = sb(st, "stq", [64, 8, 512], BF)
                stk = sb(st, "stk", [64, 2, 512], BF)
                stv = sb(st, "stv", [128, 4, 128], BF)
                stvf = sb(st, "stvf", [128, 128], F32)
                stkf = sb(st, "stkf", [128, 128], F32)
                stu = sb(st, "stu", [128, 4, 512], BF)
                stx = sb(st, "stx", [128, 4, 512], BF)
                sg = [sb(st, f"sg{i}", [128, 8, 512], BF) for i in range(2)]
                ppi = [0]

                def nextp():
                    p = pp[ppi[0] % 5]
                    ppi[0] += 1
                    return p
                XTv = XT.rearrange("(k p) t -> p k t", p=128)
                def prol(ti):
                    t0_, n_ = tiles[ti]
                    kb.dma('sp', xt[ti % 2].t[:, :, 0:n_], XTv[:, :, t0_:t0_ + n_], (), [xt[ti % 2]])
                    rmsnorm_tile(xt[ti % 2], n_, gm, hT[ti % 2], sq, rt, pss)
                prol(0)
                for ti, (t0, n) in enumerate(tiles):
                    a = ti % 2
                    smp = (n == NS)
                    if ti + 1 < len(tiles):
                        prol(ti + 1)
                    h_ = hT[a]

                    def proj(c0, M, outap, r_w, sig=False):
                        p = nextp()
                        for k in range(8):
                            MM(p.t[0:M, 0:n], win.t[:, k, c0:c0 + M], h_.t[:, k, 0:n], k == 0, k == 7, [win, h_], [p])
                        if sig:
                            ACTF(outap, p.t[0:M, 0:n], AF.Sigmoid, [p], r_w)
                        else:
                            evac(outap, p.t[0:M, 0:n], [p], r_w)
                    kin = os.environ.get("KIN", "q,k,v,u,x,g").split(",")
                    for h in range(8 if "q" in kin else 0):
                        proj(64 * h, 64, (qT_s.t[:, h, :] if smp else stq.t[:, h, 0:n]), [qT_s if smp else stq])
                    for h in range(2 if "k" in kin else 0):
                        proj(512 + 64 * h, 64, (kT_s.t[:, h, :] if smp else stk.t[:, h, 0:n]), [kT_s if smp else stk])
                    if not smp and "q" in kin and "k" in kin:
                        kb.dma('sp', Qs.rearrange("h p t -> p h t")[:, :, t0:t0 + n], stq.t[:, :, 0:n], [stq], ())
                        kb.dma('sp', Ks.rearrange("h p t -> p h t")[:, :, 128 + t0:128 + t0 + n], stk.t[:, :, 0:n], [stk], ())
                    nb = 1 if smp else 4
                    if "v" not in kin:
                        nb = 0
                    kvs = os.environ.get("KVS", "p,s").split(",")
                    if (smp and "s" not in kvs) or ((not smp) and "p" not in kvs):
                        nb = 0
                    kv = os.environ.get("KV", "vn,osv,opv,ktok,vs").split(",")
                    for j in range(nb):
                        m = NS if smp else 128
                        p = nextp()
                        kvx = os.environ.get("KVX", "")
                        for k in range(8):
                            lh = win.t[:, k, 0:m] if kvx == "lw" else h_.t[:, k, j * 128:j * 128 + m]
                            rh = h_.t[:, k, 0:128] if kvx == "rh" else win.t[:, k, 640:768]
                            MM(p.t[0:m, 0:128], lh, rh, k == 0, k == 7, [win, h_], [p])
                        last = (not smp) and (t0 + 512 == T) and j == 3
                        if smp:
                            CP(V, stvf.t[0:m, :], p.t[0:m, 0:128], [p], [stvf])
                            if "vn" in kv:
                                kb.dma('sp', VNs[:, :], stvf.t[0:m, :], [stvf], ())
                            if "osv" in kv:
                                kb.dma('sp', o_sv[l, :, 127, :], stvf.t[0:m, :], [stvf], ())
                        else:
                            kve = os.environ.get("KVE", "")
                            if kve != "noevac":
                                evac(stv.t[:, j, :], p.t[:, 0:128], [p], [stv])
                            if last and kve != "nocp":
                                CP(V, stvf.t[:], p.t[:, 0:128], [p], [stvf])
                                if "opv" in kv:
                                    kb.dma('sp', o_pv[l, :, :], stvf.t[:], [stvf], ())
                        if (smp or last) and "ktok" in kv:
                            p = nextp()
                            for k in range(8):
                                MM(p.t[0:m, 0:128], h_.t[:, k, j * 128:j * 128 + m], win.t[:, k, 512:640], k == 0, k == 7, [win, h_], [p])
                            CP(V, stkf.t[0:m, :], p.t[0:m, 0:128], [p], [stkf])
                            if smp:
                                kb.dma('sp', o_sk[l, :, 127, :], stkf.t[0:m, :], [stkf], ())
                            else:
                                kb.dma('sp', o_pk[l, :, :], stkf.t[:], [stkf], ())
                    if not smp and "v" in kin and "vs" in kv:
                        kb.dma('sp', Vs[128 + t0:128 + t0 + 512, :].rearrange("(j p) c -> p j c", p=128), stv.t[:], [stv], ())
                    for f in range(4 if "u" in kin else 0):
                        proj(768 + 128 * f, 128, (uT_s.t[:, f, :] if smp else stu.t[:, f, 0:n]), [uT_s if smp else stu])
                    for f in range(4 if "x" in kin else 0):
                        proj(1280 + 128 * f, 128, (xqT_s.t[:, f, :] if smp else stx.t[:, f, 0:n]), [xqT_s if smp else stx])
                    if "u" in kin:
                        kb.dma('sp', Us.rearrange("(f p) t -> p f t", p=128)[:, :, t0:t0 + n],
                               (uT_s.t[:] if smp else stu.t[:, :, 0:n]), [uT_s if smp else stu], ())
                    if not smp and "x" in kin:
                        kb.dma('sp', XQs.rearrange("(f p) t -> p f t", p=128)[:, :, t0:t0 + n], stx.t[:, :, 0:n], [stx], ())
                    for gi in range(3 if "g" in kin else 0):
                        s_ = sg[gi % 2]
                        for f in range(8):
                            proj(1792 + gi * 1024 + f * 128, 128, s_.t[:, f, 0:n], [s_], sig=True)
                        kb.dma('sp', SGs.rearrange("(f p) t -> p f t", p=128)[:, gi * 8:(gi + 1) * 8, t0:t0 + n], s_.t[:, :, 0:n], [s_], ())
            kb.barrier()

        def attn_unit(*args):
            for _ in attn_unit_g(*args):
                pass

        def attn_unit_g(st_t, nq, heads, nk, qfn, kfn, vfn, hd, scale, bias_ap, sink_ap, out_fn):
            pS, sS, pS_b, pT, sT, pO, mx, rs, es_, dn = st_t
            nkt = (nk + 127) // 128
            for h in range(heads):
                qa, qb = qfn(h)
                ka, kbuf = kfn(h)
                MM(pS.t[0:nq, h, 0:nk], qa, ka, True, True, qb + kbuf, [pS])
            yield
            if bias_ap is not None:
                STT(sS.t[0:nq, 0:heads, 0:nk], pS.t[0:nq, 0:heads, 0:nk], scale, bias_ap[0], ALU.mult, ALU.add, [pS] + bias_ap[1], [sS])
            else:
                TS(V, sS.t[0:nq, 0:heads, 0:nk], pS.t[0:nq, 0:heads, 0:nk], scale, None, ALU.mult, None, [pS], [sS])
            kb.op(V, lambda: nc.vector.tensor_reduce(out=mx.t[0:nq, 0:heads], in_=sS.t[0:nq, 0:heads, 0:nk], axis=AX.X, op=ALU.max), [sS], [mx])
            if sink_ap is not None:
                TT(V, mx.t[0:nq, 0:heads], mx.t[0:nq, 0:heads], sink_ap[0], ALU.max, [mx] + sink_ap[1], [mx])
            TS(V, dn.t[0:nq, 0:heads], mx.t[0:nq, 0:heads], -1.0, None, ALU.mult, None, [mx], [dn])
            for h in range(heads):
                ACTF(sS.t[0:nq, h, 0:nk], sS.t[0:nq, h, 0:nk], AF.Exp, [sS, dn], [sS, rs],
                     bias=dn.t[0:nq, h:h + 1], accum=rs.t[0:nq, h:h + 1])
            yield
            if sink_ap is not None:
                TT(V, es_.t[0:nq, 0:heads], sink_ap[0], mx.t[0:nq, 0:heads], ALU.subtract, [mx] + sink_ap[1], [es_])
                ACTF(es_.t[0:nq, 0:heads], es_.t[0:nq, 0:heads], AF.Exp, [es_], [es_])
                TT(V, rs.t[0:nq, 0:heads], rs.t[0:nq, 0:heads], es_.t[0:nq, 0:heads], ALU.add, [rs, es_], [rs])
            kb.op(V, lambda: nc.vector.reciprocal(out=dn.t[0:nq, 0:heads], in_=rs.t[0:nq, 0:heads]), [rs], [dn])
            TT(V, pS_b.t[0:nq, 0:heads, 0:nk], sS.t[0:nq, 0:heads, 0:nk],
               dn.t[0:nq, 0:heads].rearrange("p (h o) -> p h o", o=1).to_broadcast([nq, heads, nk]), ALU.mult, [sS, dn], [pS_b])
            for h in range(heads):
                for kt in range(nkt):
                    kn = min(128, nk - kt * 128)
                    TR(pT.t[0:kn, h, kt, 0:nq], pS_b.t[0:nq, h, kt * 128:kt * 128 + kn], identb.t[0:nq, 0:nq], [pS_b, identb], [pT])
            for kt in range(nkt):
                kn = min(128, nk - kt * 128)
                CP(S_, sT.t[0:kn, 0:heads, kt, 0:nq], pT.t[0:kn, 0:heads, kt, 0:nq], [pT], [sT])
            for h in range(heads):
                for kt in range(nkt):
                    kn = min(128, nk - kt * 128)
                    va, vb = vfn(h, kt, kn)
                    MM(pO.t[0:hd, h, 0:nq], va, sT.t[0:kn, h, kt, 0:nq], kt == 0, kt == nkt - 1, vb + [sT], [pO])
            out_fn(pO)

        def run_pipelined(units):
            n = len(units)
            for s in range(n + 2):
                if s < n:
                    if units[s][0] is not None:
                        units[s][0]()
                    next(units[s][1])
                if 0 <= s - 1 < n:
                    next(units[s - 1][1])
                if 0 <= s - 2 < n:
                    for _ in units[s - 2][1]:
                        pass
                    if units[s - 2][2] is not None:
                        units[s - 2][2]()

        def attn_tiles(st, pfx, psum_from=None):
            if psum_from is not None:
                pS, pT, pO = psum_from[0], psum_from[3], psum_from[5]
                sS = sb(st, pfx + "sS", [128, 4, 256], F32)
                pS_b = sb(st, pfx + "pSb", [128, 4, 256], BF)
                sT = sb(st, pfx + "sT", [128, 4, 2, 128], BF)
                mx = sb(st, pfx + "mx", [128, 4], F32)
                rs = sb(st, pfx + "rs", [128, 4], F32)
                es_ = sb(st, pfx + "es", [128, 4], F32)
                dn = sb(st, pfx + "dn", [128, 4], F32)
                return (pS, sS, pS_b, pT, sT, pO, mx, rs, es_, dn)
            pS = psb(st, pfx + "pS", [128, 4, 256], F32)
            sS = sb(st, pfx + "sS", [128, 4, 256], F32)
            pS_b = sb(st, pfx + "pSb", [128, 4, 256], BF)
            pT = psb(st, pfx + "pT", [128, 4, 2, 128], BF)
            sT = sb(st, pfx + "sT", [128, 4, 2, 128], BF)
            pO = psb(st, pfx + "pO", [128, 4, 128], F32)
            mx = sb(st, pfx + "mx", [128, 4], F32)
            rs = sb(st, pfx + "rs", [128, 4], F32)
            es_ = sb(st, pfx + "es", [128, 4], F32)
            dn = sb(st, pfx + "dn", [128, 4], F32)
            return (pS, sS, pS_b, pT, sT, pO, mx, rs, es_, dn)

        def stage_att(l):
            with contextlib.ExitStack() as st:
                tls = [attn_tiles(st, "a0"), attn_tiles(st, "a1")]
                tl = tls[0]
                ucnt = [0]
                snk = sb(st, "snk", [128, 8], F32)
                kb.dma('sp', snk.t[:], sinks[l:l + 1, :].to_broadcast([128, 8]), (), [snk])
                snkS = sb(st, "snkS", [4, 2], F32)
                kb.dma('sp', snkS.t[:], sinks[l].rearrange("(k g) -> g k", g=4), (), [snkS])
                qt = [sb(st, f"aq{i}", [64, 8, 512], BF) for i in range(2)]
                kt_ = [sb(st, f"ak{i}", [64, 2, 640], BF) for i in range(2)]
                vt = [sb(st, f"av{i}", [128, 5, 128], BF) for i in range(2)]
                oa = [sb(st, f"ao{i}", [64, 8, 512], BF) for i in range(2)]
                zk = sb(st, "zk", [64, 2, 128], BF)
                zv = sb(st, "zv", [128, 128], BF)
                MEMSET(V, zk.t[:], 0.0, [zk])
                MEMSET(V, zv.t[:], 0.0, [zv])
                kb.dma('sp', Ks.rearrange("h p t -> p h t")[:, :, 0:128], zk.t[:], [zk], ())
                kb.dma('sp', Vs[0:128, :], zv.t[:], [zv], ())
                kb.barrier()
                units = []
                for ti in range(NTL):
                    a = ti % 2
                    t0 = ti * 512

                    def pre(a=a, t0=t0):
                        kb.dma('sp', qt[a].t[:], Qs.rearrange("h p t -> p h t")[:, :, t0:t0 + 512], (), [qt[a]])
                        kb.dma('sp', kt_[a].t[:], Ks.rearrange("h p t -> p h t")[:, :, t0:t0 + 640], (), [kt_[a]])
                        kb.dma('sp', vt[a].t[:], Vs[t0:t0 + 640, :].rearrange("(j p) c -> p j c", p=128), (), [vt[a]])

                    def post(a=a, t0=t0):
                        kb.dma('sp', OAs.rearrange("(h p) t -> p h t", p=64)[:, :, t0:t0 + 512], oa[a].t[:], [oa[a]], ())
                    for j in range(4):
                        for k2 in range(2):
                            bias = (biasA0 if (ti == 0 and j == 0) else biasA)
                            ucnt[0] += 1
                            g = attn_unit_g(
                                tls[ucnt[0] % 2], 128, 4, 256,
                                lambda h, a=a, j=j, k2=k2: (qt[a].t[:, 4 * k2 + h, j * 128:(j + 1) * 128], [qt[a]]),
                                lambda h, a=a, j=j, k2=k2: (kt_[a].t[:, k2, j * 128:j * 128 + 256], [kt_[a]]),
                                lambda h, kt, kn, a=a, j=j, k2=k2: (vt[a].t[:, j + kt, 64 * k2:64 * k2 + 64], [vt[a]]),
                                64, 0.125, (bias.t[:, 4 * k2:4 * k2 + 4, :], [bias]), (snk.t[:, 4 * k2:4 * k2 + 4], [snk]),
                                lambda pO, a=a, j=j, k2=k2: CP(S_, oa[a].t[:, 4 * k2:4 * k2 + 4, j * 128:(j + 1) * 128], pO.t[0:64, :, :], [pO], [oa[a]]))
                            first = (j == 0 and k2 == 0)
                            lastu = (j == 3 and k2 == 1)
                            units.append((pre if first else None, g, post if lastu else None))
                run_pipelined(units)
                kc = [sb(st, f"skc{i}", [128, 128], F32) for i in range(2)]
                vc = [sb(st, f"svc{i}", [128, 128], F32) for i in range(2)]
                vcb = [sb(st, f"svcb{i}", [128, 128], BF) for i in range(2)]
                vn = [sb(st, f"svn{i}", [1, 128], F32) for i in range(2)]
                vnb = [sb(st, f"svnb{i}", [1, 128], BF) for i in range(2)]
                ktx = [sb(st, f"sktx{i}", [64, 2, 129], BF) for i in range(2)]
                pk = tls[1][5]
                oas = sb(st, "oas", [64, 8, NS], BF)
                for b in range(NS):
                    a = b % 2
                    kb.dma('sp', kc[a].t[:], c_swk[l, b], (), [kc[a]])
                    kb.dma('sp', vc[a].t[:], c_swv[l, b], (), [vc[a]])
                    kb.dma('sp', vn[a].t[:], VNs[b:b + 1, :], (), [vn[a]])
                    kb.dma('sp', o_sk[l, b, 0:127, :], c_swk[l, b, 1:128, :], (), ())
                    kb.dma('sp', o_sv[l, b, 0:127, :], c_swv[l, b, 1:128, :], (), ())
                    for k2 in range(2):
                        TR(pk.t[0:64, k2, 0:128], kc[a].t[:, 64 * k2:64 * k2 + 64], identf.t[:], [kc[a], identf], [pk])
                    evac(ktx[a].t[:, :, 0:128], pk.t[0:64, 0:2, 0:128], [pk], [ktx[a]])
                    CP(V, ktx[a].t[:, :, 128:129], kT_s.t[:, :, b:b + 1], [kT_s], [ktx[a]])
                    CP(V, vcb[a].t[:], vc[a].t[:], [vc[a]], [vcb[a]])
                    CP(V, vnb[a].t[:], vn[a].t[:], [vn[a]], [vnb[a]])
                    for k2 in range(2):
                        attn_unit(
                            tl, 4, 1, 129,
                            lambda h: (qT_s.t[:, 4 * k2:4 * k2 + 4, b], [qT_s]),
                            lambda h: (ktx[a].t[:, k2, :], [ktx[a]]),
                            lambda h, kt, kn: ((vcb[a].t[:, 64 * k2:64 * k2 + 64], [vcb[a]]) if kt == 0 else (vnb[a].t[0:1, 64 * k2:64 * k2 + 64], [vnb[a]])),
                            64, 0.125, (biasS.t[:, k2:k2 + 1, :], [biasS]), (snkS.t[:, k2:k2 + 1], [snkS]),
                            lambda pO: evac(oas.t[:, 4 * k2:4 * k2 + 4, b], pO.t[0:64, 0, 0:4], [pO], [oas]))
                kb.dma('sp', OAs.rearrange("(h p) t -> p h t", p=64)[:, :, T:T + NS], oas.t[:], [oas], ())
            kb.barrier()


        def sin_of(out_ap, ang_ap, shape, tmpf, tmpi, tmpm, bufs_in, buf_out, phase=0.0):
            tf, ti_, tm = tmpf, tmpi, tmpm
            TS(V, tf[0], ang_ap, 1.0, phase, ALU.mult, ALU.add, bufs_in, [tf[1]])
            TS(V, tm[0], tf[0], 1.0 / TWO_PI, None, ALU.mult, None, [tf[1]], [tm[1]])
            CP(V, ti_[0], tm[0], [tm[1]], [ti_[1]])
            CP(V, tm[0], ti_[0], [ti_[1]], [tm[1]])
            STT(tf[0], tm[0], -TWO_PI, tf[0], ALU.mult, ALU.add, [tm[1], tf[1]], [tf[1]])
            TS(V, tm[0], tf[0], math.pi, None, ALU.is_gt, None, [tf[1]], [tm[1]])
            STT(tf[0], tm[0], -TWO_PI, tf[0], ALU.mult, ALU.add, [tm[1], tf[1]], [tf[1]])
            TS(V, tm[0], tf[0], -math.pi, None, ALU.is_lt, None, [tf[1]], [tm[1]])
            STT(tf[0], tm[0], TWO_PI, tf[0], ALU.mult, ALU.add, [tm[1], tf[1]], [tf[1]])
            TS(V, tf[0], tf[0], math.pi, -math.pi, ALU.min, ALU.max, [tf[1]], [tf[1]])
            ACTF(out_ap, tf[0], AF.Sin, [tf[1]], [buf_out])

        def bc3(ap2, n):
            P_, M_ = ap2.shape[0], ap2.shape[1]
            return ap2.rearrange("p (m o) -> p m o", o=1).to_broadcast([P_, M_, n])

        def stage_s5(l):
            with contextlib.ExitStack() as st:
                def t2(name, shape=(128, 16), dt=F32):
                    return sb(st, name, list(shape), dt)
                lr, li, dtl, dtt, th, mag, ar, ai = [t2(n) for n in ("lr", "li", "dtl", "dtt", "th", "mag", "ar", "ai")]
                cs, sn, den, fr, fi, tA, tB = [t2(n) for n in ("cs", "sn", "den", "fr", "fi", "tA", "tB")]
                tf = t2("rtf"); ti_ = t2("rti", dt=I32); tm = t2("rtm")
                kb.dma('sp', lr.t[:], lam_re[l].rearrange("(m two) p -> (two p) m", two=2), (), [lr])
                kb.dma('sp', li.t[:], lam_im[l].rearrange("(m two) p -> (two p) m", two=2), (), [li])
                ldv = log_dt[l].rearrange("(m two) -> two m", two=2)
                kb.dma('sp', dtl.t[0:64, :], ldv[0:1, :].to_broadcast([64, 16]), (), [dtl], acc=True)
                kb.dma('sp', dtl.t[64:128, :], ldv[1:2, :].to_broadcast([64, 16]), (), [dtl], acc=True)
                ACTF(dtt.t[:], dtl.t[:], AF.Exp, [dtl], [dtt])
                TT(V, th.t[:], li.t[:], dtt.t[:], ALU.mult, [li, dtt], [th])
                TT(V, mag.t[:], lr.t[:], dtt.t[:], ALU.mult, [lr, dtt], [mag])
                ACTF(mag.t[:], mag.t[:], AF.Exp, [mag], [mag])
                sin_of(sn.t[:], th.t[:], None, (tf.t[:], tf), (ti_.t[:], ti_), (tm.t[:], tm), [th], sn)
                sin_of(cs.t[:], th.t[:], None, (tf.t[:], tf), (ti_.t[:], ti_), (tm.t[:], tm), [th], cs, phase=math.pi / 2)
                TT(V, ar.t[:], mag.t[:], cs.t[:], ALU.mult, [mag, cs], [ar])
                TT(V, ai.t[:], mag.t[:], sn.t[:], ALU.mult, [mag, sn], [ai])
                TT(V, den.t[:], lr.t[:], lr.t[:], ALU.mult, [lr], [den])
                TT(V, tA.t[:], li.t[:], li.t[:], ALU.mult, [li], [tA])
                TT(V, den.t[:], den.t[:], tA.t[:], ALU.add, [den, tA], [den])
                kb.op(V, lambda: nc.vector.reciprocal(out=den.t[:], in_=den.t[:]), [den], [den])
                TS(V, tA.t[:], ar.t[:], -1.0, None, ALU.add, None, [ar], [tA])
                TT(V, fr.t[:], tA.t[:], lr.t[:], ALU.mult, [tA, lr], [fr])
                TT(V, tB.t[:], ai.t[:], li.t[:], ALU.mult, [ai, li], [tB])
                TT(V, fr.t[:], fr.t[:], tB.t[:], ALU.add, [fr, tB], [fr])
                TT(V, fr.t[:], fr.t[:], den.t[:], ALU.mult, [fr, den], [fr])
                TT(V, fi.t[:], ai.t[:], lr.t[:], ALU.mult, [ai, lr], [fi])
                TT(V, tB.t[:], tA.t[:], li.t[:], ALU.mult, [tA, li], [tB])
                TT(V, fi.t[:], fi.t[:], tB.t[:], ALU.subtract, [fi, tB], [fi])
                TT(V, fi.t[:], fi.t[:], den.t[:], ALU.mult, [fi, den], [fi])
                cosT = sb(st, "cosT", [128, 16, 512], F32)
                sinT = sb(st, "sinT", [128, 16, 512], F32)
                with contextlib.ExitStack() as st2:
                    ang = sb(st2, "ang", [128, 4, 512], F32)
                    rf = sb(st2, "rf", [128, 4, 512], F32); ri2 = sb(st2, "ri2", [128, 4, 512], I32); rm = sb(st2, "rm", [128, 4, 512], F32)
                    for c in range(4):
                        TT(V, ang.t[:], bc3(th.t[:, 4 * c:4 * c + 4], 512),
                           jidx.t[:].rearrange("p (o j) -> p o j", o=1).to_broadcast([128, 4, 512]), ALU.mult, [th, jidx], [ang])
                        sin_of(sinT.t[:, 4 * c:4 * c + 4, :], ang.t[:], None, (rf.t[:], rf), (ri2.t[:], ri2), (rm.t[:], rm), [ang], sinT)
                        sin_of(cosT.t[:, 4 * c:4 * c + 4, :], ang.t[:], None, (rf.t[:], rf), (ri2.t[:], ri2), (rm.t[:], rm), [ang], cosT, phase=math.pi / 2)
                    kb.barrier()
                pX = [psb(st, f"pX{i}", [128, 2, 512], F32) for i in range(2)]
                pY = psb(st, "pY", [128, 4, 512], F32)
                BT = sb(st, "BT", [128, 2, 4, 128], BF)
                BT3 = sb(st, "BT3", [128, 2, 4, 128], BF)
                CTp = sb(st, "CTp", [128, 16, 2, 128], BF)
                dsk = sb(st, "dsk", [128, 4], F32)
                stb = contextlib.ExitStack()
                br = sb(stb, "br", [128, 16, 16], F32); bi = sb(stb, "bi", [128, 16, 16], F32)
                bsr = sb(stb, "bsr", [128, 16, 16], F32); bsi = sb(stb, "bsi", [128, 16, 16], F32); btmp = sb(stb, "btmp", [128, 16, 16], F32)
                kb.dma('sp', br.t[:], b_re[l].rearrange("(m two) p j -> (two p) m j", two=2), (), [br])
                kb.dma('sp', bi.t[:], b_im[l].rearrange("(m two) p j -> (two p) m j", two=2), (), [bi])
                TT(V, bsr.t[:], br.t[:], bc3(fr.t[:], 16), ALU.mult, [br, fr], [bsr])
                TT(V, btmp.t[:], bi.t[:], bc3(fi.t[:], 16), ALU.mult, [bi, fi], [btmp])
                TT(V, bsr.t[:], bsr.t[:], btmp.t[:], ALU.subtract, [bsr, btmp], [bsr])
                TT(V, bsi.t[:], bi.t[:], bc3(fr.t[:], 16), ALU.mult, [bi, fr], [bsi])
                TT(V, btmp.t[:], br.t[:], bc3(fi.t[:], 16), ALU.mult, [br, fi], [btmp])
                TT(V, bsi.t[:], bsi.t[:], btmp.t[:], ALU.add, [bsi, btmp], [bsi])
                Pcat = sb(stb, "Pcat", [128, 2, 4, 4, 2, 16], F32)
                MEMSET(V, Pcat.t[:], 0.0, [Pcat])
                for ri, bs in enumerate((bsr, bsi)):
                    CP(V, Pcat.t[0:64, ri, :, :, 0, :], bs.t[0:64, :, :].rearrange("p (f m) j -> p f m j", f=4), [bs], [Pcat])
                    CP(V, Pcat.t[64:128, ri, :, :, 1, :], bs.t[64:128, :, :].rearrange("p (f m) j -> p f m j", f=4), [bs], [Pcat])
                for ri in range(2):
                    for ft in range(4):
                        TR(pY.t[:, ft, 0:128], Pcat.t[:, ri, ft].rearrange("p m t j -> p (m t j)"), identf.t[:], [Pcat, identf], [pY])
                    evac(BT.t[:, ri, :, :], pY.t[:, :, 0:128], [pY], [BT])
                MEMSET(V, BT3.t[:], 0.0, [BT3])
                CP(V, BT3.t[96:128, :, :, :], BT.t[96:128, :, :, :], [BT], [BT3])
                Cin = sb(stb, "Cin", [128, 4, 2, 128], F32)
                for ri, cc in enumerate((c_re, c_im)):
                    src = cc[l].rearrange("(f g) k p -> (g k) f p", f=4)
                    kb.dma('sp', Cin.t[:, :, ri, 0:64], src, (), [Cin], acc=True)
                    kb.dma('sp', Cin.t[:, :, ri, 64:128], src, (), [Cin], acc=True)
                TS(V, Cin.t[:, :, 1, :], Cin.t[:, :, 1, :], -1.0, None, ALU.mult, None, [Cin], [Cin])
                CTc = sb(stb, "CTc", [128, 4, 2, 128], F32)
                for ri in range(2):
                    for ft in range(4):
                        TR(pY.t[:, ft, 0:128], Cin.t[:, ft, ri, :], identf.t[:], [Cin, identf], [pY])
                    evac(CTc.t[:, :, ri, :], pY.t[:, :, 0:128], [pY], [CTc])
                MEMSET(V, CTp.t[:], 0.0, [CTp])
                for m in range(16):
                    ft, mm = m // 4, m % 4
                    e = (V, G_)[m % 2]
                    CP(e, CTp.t[0:64, m, :, 32 * mm:32 * mm + 16], CTc.t[0:64, ft, :, 32 * mm:32 * mm + 16], [CTc], [CTp])
                    CP(e, CTp.t[64:128, m, :, 32 * mm + 16:32 * mm + 32], CTc.t[64:128, ft, :, 32 * mm + 16:32 * mm + 32], [CTc], [CTp])
                kb.dma('sp', dsk.t[:], d_skip[l].rearrange("(f p) -> p f", p=128), (), [dsk])
                kb.barrier()
                stb.close()
                initR = sb(st, "initR", [128, 16], F32); initI = sb(st, "initI", [128, 16], F32)
                hlR = sb(st, "hlR", [128, 16], F32); hlI = sb(st, "hlI", [128, 16], F32)
                tcA = sb(st, "tcA", [128, 16], F32)
                G5 = sb(st, "G5", [128, 2, 16], F32)
                ysb = sb(st, "ysb", [128, 4, 512], F32)
                st = contextlib.ExitStack()
                MEMSET(V, initR.t[:], 0.0, [initR]); MEMSET(V, initI.t[:], 0.0, [initI])
                uT = [sb(st, f"suT{i}", [128, 4, 512], BF) for i in range(2)]
                w1 = [sb(st, f"w1{i}", [128, 512], F32) for i in range(2)]
                w2 = [sb(st, f"w2{i}", [128, 512], F32) for i in range(2)]
                w3 = [sb(st, f"w3{i}", [128, 512], F32) for i in range(2)]
                w4 = [sb(st, f"w4{i}", [128, 512], F32) for i in range(2)]
                xr_ = [sb(st, f"xr{i}", [128, 512], F32) for i in range(2)]
                xi_ = [sb(st, f"xi{i}", [128, 512], F32) for i in range(2)]
                gr = [sb(st, f"gr{i}", [128, 512], F32) for i in range(2)]
                gi_ = [sb(st, f"gi{i}", [128, 512], F32) for i in range(2)]
                hr = [sb(st, f"hr{i}", [128, 512], BF) for i in range(2)]
                hi = [sb(st, f"hi{i}", [128, 512], BF) for i in range(2)]
                zT = [sb(st, f"zT{i}", [128, 4, 512], BF) for i in range(2)]
                it = 0
                pend = [None]
                ks5 = os.environ.get("KS5", "main,smp").split(",")
                for ti in range(NTL if "main" in ks5 else 0):
                    ua = uT[ti % 2]
                    t0 = ti * 512
                    kb.dma('sp', ua.t[:], Us.rearrange("(f p) t -> p f t", p=128)[:, :, t0:t0 + 512], (), [ua])
                    for m in range(16):
                        a = it % 2
                        it += 1
                        ft, mm = m // 4, m % 4
                        px = pX[a]
                        for ri in range(2):
                            if mm < 3:
                                MM(px.t[:, ri, :], BT.t[32 * mm:32 * mm + 32, ri, ft, :], ua.t[32 * mm:32 * mm + 32, ft, :], True, True, [BT, ua], [px])
                            else:
                                MM(px.t[:, ri, :], BT3.t[64:128, ri, ft, :], ua.t[64:128, ft, :], True, True, [BT3, ua], [px])
                        if pend[0] is not None:
                            pend[0]()
                            pend[0] = None
                        c_, s_ = cosT.t[:, m, :], sinT.t[:, m, :]
                        TT(V, w1[a].t[:], px.t[:, 0, :], c_, ALU.mult, [px, cosT], [w1[a]])
                        TT(V, w2[a].t[:], px.t[:, 1, :], s_, ALU.mult, [px, sinT], [w2[a]])
                        TT(V, xr_[a].t[:], w1[a].t[:], w2[a].t[:], ALU.add, [w1[a], w2[a]], [xr_[a]])
                        TT(V, w3[a].t[:], px.t[:, 1, :], c_, ALU.mult, [px, cosT], [w3[a]])
                        TT(V, w4[a].t[:], px.t[:, 0, :], s_, ALU.mult, [px, sinT], [w4[a]])
                        TT(V, xi_[a].t[:], w3[a].t[:], w4[a].t[:], ALU.subtract, [w3[a], w4[a]], [xi_[a]])
                        magb = mag.t[:, m:m + 1].to_broadcast([128, 512])
                        kb.op(V, lambda: nc.vector.tensor_tensor_scan(out=gr[a].t[:], data0=magb, data1=xr_[a].t[:], initial=initR.t[:, m:m + 1], op0=ALU.mult, op1=ALU.add), [mag, xr_[a], initR], [gr[a]])
                        kb.op(V, lambda: nc.vector.tensor_tensor_scan(out=gi_[a].t[:], data0=magb, data1=xi_[a].t[:], initial=initI.t[:, m:m + 1], op0=ALU.mult, op1=ALU.add), [mag, xi_[a], initI], [gi_[a]])
                        TT(G_, w1[a].t[:], gr[a].t[:], c_, ALU.mult, [gr[a], cosT], [w1[a]])
                        TT(G_, w2[a].t[:], gi_[a].t[:], s_, ALU.mult, [gi_[a], sinT], [w2[a]])
                        TT(G_, hr[a].t[:], w1[a].t[:], w2[a].t[:], ALU.subtract, [w1[a], w2[a]], [hr[a]])
                        TT(G_, w3[a].t[:], gr[a].t[:], s_, ALU.mult, [gr[a], sinT], [w3[a]])
                        TT(G_, w4[a].t[:], gi_[a].t[:], c_, ALU.mult, [gi_[a], cosT], [w4[a]])
                        TT(G_, hi[a].t[:], w3[a].t[:], w4[a].t[:], ALU.add, [w3[a], w4[a]], [hi[a]])
                        CP(S_, G5.t[:, 0, m:m + 1], gr[a].t[:, 511:512], [gr[a]], [G5])
                        CP(S_, G5.t[:, 1, m:m + 1], gi_[a].t[:, 511:512], [gi_[a]], [G5])
                        def ymm(a=a, m=m, ft=ft, mm=mm):
                            MM(pY.t[:, ft, :], CTp.t[:, m, 0, :], hr[a].t[:], mm == 0, False, [CTp, hr[a]], [pY])
                            MM(pY.t[:, ft, :], CTp.t[:, m, 1, :], hi[a].t[:], False, mm == 3, [CTp, hi[a]], [pY])
                        pend[0] = ymm
                    pend[0]()
                    pend[0] = None
                    c5, s5 = cosT.t[:, :, 511], sinT.t[:, :, 511]
                    c1, s1 = cosT.t[:, :, 1], sinT.t[:, :, 1]
                    TT(V, tcA.t[:], G5.t[:, 1, :], s5, ALU.mult, [G5, sinT], [tcA])
                    TT(V, hlR.t[:], G5.t[:, 0, :], c5, ALU.mult, [G5, cosT], [hlR])
                    TT(V, hlR.t[:], hlR.t[:], tcA.t[:], ALU.subtract, [hlR, tcA], [hlR])
                    TT(V, tcA.t[:], G5.t[:, 1, :], c5, ALU.mult, [G5, cosT], [tcA])
                    TT(V, hlI.t[:], G5.t[:, 0, :], s5, ALU.mult, [G5, sinT], [hlI])
                    TT(V, hlI.t[:], hlI.t[:], tcA.t[:], ALU.add, [hlI, tcA], [hlI])
                    TT(V, tcA.t[:], hlI.t[:], s1, ALU.mult, [hlI, sinT], [tcA])
                    TT(V, initR.t[:], hlR.t[:], c1, ALU.mult, [hlR, cosT], [initR])
                    TT(V, initR.t[:], initR.t[:], tcA.t[:], ALU.subtract, [initR, tcA], [initR])
                    TT(V, tcA.t[:], hlI.t[:], c1, ALU.mult, [hlI, cosT], [tcA])
                    TT(V, initI.t[:], hlR.t[:], s1, ALU.mult, [hlR, sinT], [initI])
                    TT(V, initI.t[:], initI.t[:], tcA.t[:], ALU.add, [initI, tcA], [initI])
                    za = zT[ti % 2]
                    for ft in range(4):
                        STT(ysb.t[:, ft, :], ua.t[:, ft, :], dsk.t[:, ft:ft + 1], pY.t[:, ft, :], ALU.mult, ALU.add, [ua, dsk, pY], [ysb])
                    ACTF(za.t[:], ysb.t[:], AF.Gelu, [ysb], [za])
                    kb.dma('sp', Zs.rearrange("(f p) t -> p f t", p=128)[:, :, t0:t0 + 512], za.t[:], [za], ())
                kb.dma('sp', o_pre[l].rearrange("(m r) -> r m", r=128), hlR.t[:], [hlR], ())
                kb.dma('sp', o_pim[l].rearrange("(m r) -> r m", r=128), hlI.t[:], [hlI], ())
                kb.barrier()
                st.close()
                st = contextlib.ExitStack()
                if "smp" not in ks5:
                    st.close()
                    kb.barrier()
                    return
                stok = sb(st, "stok", [NS, 2, 2048], F32)
                kb.dma('sp', stok.t[:, 0, :], c_sre[l], (), [stok], acc=True)
                kb.dma('sp', stok.t[:, 1, :], c_sim[l], (), [stok], acc=True)
                px = pX[0]
                for ri in range(2):
                    for m in range(16):
                        TR(px.t[:, ri, m * 16:(m + 1) * 16], stok.t[:, ri, m * 128:(m + 1) * 128], identf.t[0:NS, 0:NS], [stok, identf], [px])
                prv = sb(st, "prv", [128, 2, 256], F32)
                evac(prv.t[:], px.t[:, :, 0:256], [px], [prv])
                for ri in range(2):
                    for m in range(16):
                        ft, mm = m // 4, m % 4
                        oc_ = (ri * 4 + ft) * 16
                        if mm < 3:
                            MM(pY.t[:, mm, oc_:oc_ + 16], BT.t[32 * mm:32 * mm + 32, ri, ft, :], uT_s.t[32 * mm:32 * mm + 32, ft, :], True, True, [BT, uT_s], [pY])
                        else:
                            MM(pY.t[:, mm, oc_:oc_ + 16], BT3.t[64:128, ri, ft, :], uT_s.t[64:128, ft, :], True, True, [BT3, uT_s], [pY])
                vA = lambda ap: ap.rearrange("p (f m b) -> p m f b", f=4, m=4, b=16)
                vB = lambda ri: pY.t[:, :, ri * 64:(ri + 1) * 64].rearrange("p m (f b) -> p m f b", f=4)
                hs = sb(st, "hs", [128, 2, 256], F32)
                hsb = sb(st, "hsb", [128, 2, 256], BF)
                q1 = sb(st, "q1", [128, 256], F32); q2 = sb(st, "q2", [128, 256], F32)
                arB, aiB = bc3(ar.t[:], 16), bc3(ai.t[:], 16)
                v3 = lambda ap: ap.rearrange("p (m b) -> p m b", b=16)
                TT(V, v3(q1.t[:]), v3(prv.t[:, 0, :]), arB, ALU.mult, [prv, ar], [q1])
                TT(V, v3(q2.t[:]), v3(prv.t[:, 1, :]), aiB, ALU.mult, [prv, ai], [q2])
                TT(V, q1.t[:], q1.t[:], q2.t[:], ALU.subtract, [q1, q2], [q1])
                TT(V, vA(hs.t[:, 0, :]), vA(q1.t[:]), vB(0), ALU.add, [q1, pY], [hs])
                TT(V, v3(q1.t[:]), v3(prv.t[:, 1, :]), arB, ALU.mult, [prv, ar], [q1])
                TT(V, v3(q2.t[:]), v3(prv.t[:, 0, :]), aiB, ALU.mult, [prv, ai], [q2])
                TT(V, q1.t[:], q1.t[:], q2.t[:], ALU.add, [q1, q2], [q1])
                TT(V, vA(hs.t[:, 1, :]), vA(q1.t[:]), vB(1), ALU.add, [q1, pY], [hs])
                CP(V, hsb.t[:], hs.t[:], [hs], [hsb])
                for ft in range(4):
                    for mm in range(4):
                        m = ft * 4 + mm
                        for ri in range(2):
                            MM(pY.t[:, ft, 0:16], CTp.t[:, m, ri, :], hsb.t[:, ri, m * 16:(m + 1) * 16], mm == 0 and ri == 0, mm == 3 and ri == 1, [CTp, hsb], [pY])
                zs = sb(st, "zs", [128, 4, NS], BF)
                for ft in range(4):
                    STT(ysb.t[:, ft, 0:16], uT_s.t[:, ft, :], dsk.t[:, ft:ft + 1], pY.t[:, ft, 0:16], ALU.mult, ALU.add, [uT_s, dsk, pY], [ysb])
                ACTF(zs.t[:], ysb.t[:, :, 0:16], AF.Gelu, [ysb], [zs])
                kb.dma('sp', Zs.rearrange("(f p) t -> p f t", p=128)[:, :, T:T + NS], zs.t[:], [zs], ())
                sout = sb(st, "sout", [NS, 2, 2048], F32)
                for ri in range(2):
                    for m in range(16):
                        TR(pY.t[0:NS, (m // 4), (m % 4) * 128:(m % 4) * 128 + 128], hs.t[:, ri, m * 16:(m + 1) * 16], identf.t[:], [hs, identf], [pY])
                    evac(sout.t[:, ri, :], pY.t[0:NS, :, :].rearrange("p f t -> p (f t)"), [pY], [sout])
                kb.dma('sp', o_sre[l], sout.t[:, 0, :], [sout], ())
                kb.dma('sp', o_sim[l], sout.t[:, 1, :], [sout], ())
                kb.barrier()
                st.close()
            kb.barrier()

        def stage_matt(l):
            with contextlib.ExitStack() as st:
                tls = [attn_tiles(st, "m0"), attn_tiles(st, "m1")]
                tl = tls[0]
                ucnt = [0]
                qt = [sb(st, f"mq{i}", [128, 4, 512], BF) for i in range(2)]
                oc = [sb(st, f"moc{i}", [128, 4, 512], BF) for i in range(2)]
                sc = 128.0 ** -0.5
                units = []
                for ti in range(NTL):
                    a = ti % 2
                    t0 = ti * 512

                    def pre(a=a, t0=t0):
                        kb.dma('sp', qt[a].t[:], XQs.rearrange("(f p) t -> p f t", p=128)[:, :, t0:t0 + 512], (), [qt[a]])

                    def post(a=a, t0=t0):
                        kb.dma('sp', OCs.rearrange("(f p) t -> p f t", p=128)[:, :, t0:t0 + 512], oc[a].t[:], [oc[a]], ())
                    for j in range(4):
                        ucnt[0] += 1
                        g = attn_unit_g(
                            tls[ucnt[0] % 2], 128, 4, 256,
                            lambda h, a=a, j=j: (qt[a].t[:, h, j * 128:(j + 1) * 128], [qt[a]]),
                            lambda h: (mkT.t[:, h, :], [mkT]),
                            lambda h, kt, kn: (mvT.t[:, kt, 128 * h:128 * h + 128], [mvT]),
                            128, sc, None, None,
                            lambda pO, a=a, j=j: CP(S_, oc[a].t[:, :, j * 128:(j + 1) * 128], pO.t[:, :, :], [pO], [oc[a]]))
                        units.append((pre if j == 0 else None, g, post if j == 3 else None))
                run_pipelined(units)
                mk = [sb(st, f"smk{i}", [128, 2, 512], F32) for i in range(2)]
                mv = [sb(st, f"smv{i}", [128, 2, 512], F32) for i in range(2)]
                mvb = [sb(st, f"smvb{i}", [128, 2, 512], BF) for i in range(2)]
                mkTs = [sb(st, f"smkT{i}", [128, 4, 256], BF) for i in range(2)]
                ptr = tls[1][0]
                ocs = sb(st, "ocs", [128, 4, NS], BF)
                for b in range(NS):
                    a = b % 2
                    kb.dma('sp', mk[a].t[:], c_mk[l, b].rearrange("(j p) c -> p j c", p=128), (), [mk[a]])
                    kb.dma('sp', mv[a].t[:], c_mv[l, b].rearrange("(j p) c -> p j c", p=128), (), [mv[a]])
                    for h in range(4):
                        for j in range(2):
                            TR(ptr.t[:, h, j * 128:(j + 1) * 128], mk[a].t[:, j, h * 128:(h + 1) * 128], identf.t[:], [mk[a], identf], [ptr])
                    evac(mkTs[a].t[:], ptr.t[:], [ptr], [mkTs[a]])
                    CP(G_, mvb[a].t[:], mv[a].t[:], [mv[a]], [mvb[a]])
                    attn_unit(
                        tl, 1, 4, 256,
                        lambda h: (xqT_s.t[:, h, b:b + 1], [xqT_s]),
                        lambda h: (mkTs[a].t[:, h, :], [mkTs[a]]),
                        lambda h, kt, kn: (mvb[a].t[:, kt, 128 * h:128 * h + 128], [mvb[a]]),
                        128, sc, None, None,
                        lambda pO: evac(ocs.t[:, :, b:b + 1], pO.t[:, :, 0:1], [pO], [ocs]))
                kb.dma('sp', OCs.rearrange("(f p) t -> p f t", p=128)[:, :, T:T + NS], ocs.t[:], [ocs], ())
            kb.barrier()

        def stage_mrg(l):
            with contextlib.ExitStack() as st:
                wa = sb(st, "wa", [64, 8, D], BF)
                for h in range(8):
                    kb.dma('pool', wa.t[:, h, :], w_bra[l, h * 64:(h + 1) * 64, :], (), [wa], acc=True)
                wga = sb(st, "wga", [128, 4, D], BF); wgb = sb(st, "wgb", [128, 4, D], BF)
                wc = sb(st, "wc", [128, 4, D], BF); wo = sb(st, "wo", [128, 8, D], BF)
                load_w_begin(st)
                load_w(wga, lambda kt: w_ga[l, kt * 128:(kt + 1) * 128, :], 4, D)
                load_w(wgb, lambda kt: w_gb[l, kt * 128:(kt + 1) * 128, :], 4, D)
                load_w(wc, lambda kt: w_brm[l, kt * 128:(kt + 1) * 128, :], 4, D)
                load_w(wo, lambda kt: w_out[l, kt * 128:(kt + 1) * 128, :], 8, D)
                load_w_end()
                oa = [sb(st, f"goa{i}", [64, 8, 512], BF) for i in range(2)]
                zt = [sb(st, f"gz{i}", [128, 4, 512], BF) for i in range(2)]
                oc = [sb(st, f"goc{i}", [128, 4, 512], BF) for i in range(2)]
                sg = [sb(st, "gsg0", [128, 24, 512], BF)] * 2
                xt = [sb(st, "gx0", [128, 8, 512], F32)] * 2
                mg = sb(st, "mg", [128, 8, 512], BF)
                m1s = [sb(st, f"m1{i}", [128, 512], F32) for i in range(2)]
                m2s = [sb(st, f"m2{i}", [128, 512], F32) for i in range(2)]
                m3s = [sb(st, f"m3{i}", [128, 512], F32) for i in range(2)]
                pa = [psb(st, f"gpa{i}", [128, 512], F32) for i in range(8)]
                XTv = XT.rearrange("(k p) t -> p k t", p=128)
                pi = 0
                for ti, (t0, n) in enumerate(tiles):
                    a = ti % 2
                    kb.dma('sp', oa[a].t[:, :, 0:n], OAs.rearrange("(h p) t -> p h t", p=64)[:, :, t0:t0 + n], (), [oa[a]])
                    kb.dma('sp', zt[a].t[:, :, 0:n], Zs.rearrange("(f p) t -> p f t", p=128)[:, :, t0:t0 + n], (), [zt[a]])
                    kb.dma('sp', oc[a].t[:, :, 0:n], OCs.rearrange("(f p) t -> p f t", p=128)[:, :, t0:t0 + n], (), [oc[a]])
                    kb.dma('sp', sg[a].t[:, :, 0:n], SGs.rearrange("(f p) t -> p f t", p=128)[:, :, t0:t0 + n], (), [sg[a]])
                    kb.dma('sp', xt[a].t[:, :, 0:n], XTv[:, :, t0:t0 + n], (), [xt[a]])
                    for dm in range(8):
                        cs_ = slice(dm * 128, (dm + 1) * 128)
                        m1, m2, m3 = m1s[dm % 2], m2s[dm % 2], m3s[dm % 2]
                        pA, pGa, pGb, pC = pa[pi % 8], pa[(pi + 1) % 8], pa[(pi + 2) % 8], pa[(pi + 3) % 8]
                        pi += 4
                        for h in range(8):
                            MM(pA.t[:, 0:n], wa.t[:, h, cs_], oa[a].t[:, h, 0:n], h == 0, h == 7, [wa, oa[a]], [pA])
                        for k in range(4):
                            MM(pGa.t[:, 0:n], wga.t[:, k, cs_], zt[a].t[:, k, 0:n], k == 0, k == 3, [wga, zt[a]], [pGa])
                        for k in range(4):
                            MM(pGb.t[:, 0:n], wgb.t[:, k, cs_], zt[a].t[:, k, 0:n], k == 0, k == 3, [wgb, zt[a]], [pGb])
                        for k in range(4):
                            MM(pC.t[:, 0:n], wc.t[:, k, cs_], oc[a].t[:, k, 0:n], k == 0, k == 3, [wc, oc[a]], [pC])
                        TT(V, m1.t[:, 0:n], pA.t[:, 0:n], sg[a].t[:, dm, 0:n], ALU.mult, [pA, sg[a]], [m1])
                        ACTF(m2.t[:, 0:n], pGb.t[:, 0:n], AF.Sigmoid, [pGb], [m2])
                        TT(V, m2.t[:, 0:n], pGa.t[:, 0:n], m2.t[:, 0:n], ALU.mult, [pGa, m2], [m2])
                        TT(G_, m2.t[:, 0:n], m2.t[:, 0:n], sg[a].t[:, 8 + dm, 0:n], ALU.mult, [m2, sg[a]], [m2])
                        TT(V, m3.t[:, 0:n], pC.t[:, 0:n], sg[a].t[:, 16 + dm, 0:n], ALU.mult, [pC, sg[a]], [m3])
                        TT(G_, m1.t[:, 0:n], m1.t[:, 0:n], m2.t[:, 0:n], ALU.add, [m1, m2], [m1])
                        TT(G_, mg.t[:, dm, 0:n], m1.t[:, 0:n], m3.t[:, 0:n], ALU.add, [m1, m3], [mg])
                    for dm in range(8):
                        pO = pa[pi % 8]
                        pi += 1
                        for k in range(8):
                            MM(pO.t[:, 0:n], wo.t[:, k, dm * 128:(dm + 1) * 128], mg.t[:, k, 0:n], k == 0, k == 7, [wo, mg], [pO])
                        TT(V, xt[a].t[:, dm, 0:n], xt[a].t[:, dm, 0:n], pO.t[:, 0:n], ALU.add, [xt[a], pO], [xt[a]])
                    kb.dma('sp', XTv[:, :, t0:t0 + n], xt[a].t[:, :, 0:n], [xt[a]], ())
            kb.barrier()

        def stage_f1(l, half):
            HF = 11
            F0 = half * HF
            C0 = F0 * 128
            CW = HF * 128
            with contextlib.ExitStack() as st:
                wg = sb(st, "wg", [128, 8, CW], BF); wu = sb(st, "wu", [128, 8, CW], BF)
                load_w_begin(st)
                load_w(wg, lambda kt: w_fg[l, kt * 128:(kt + 1) * 128, C0:C0 + CW], 8, CW)
                load_w(wu, lambda kt: w_fu[l, kt * 128:(kt + 1) * 128, C0:C0 + CW], 8, CW)
                load_w_end()
                gf = sb(st, "gffn", [128, 8], F32)
                kb.dma('sp', gf.t[:], g_ffn[l].rearrange("(k p) -> p k", p=128), (), [gf])
                cw = sb(st, "cw", [128, 3, HF], F32); cb = sb(st, "cb", [128, HF], F32)
                for j3 in range(3):
                    kb.dma('sp', cw.t[:, j3, :], conv_w[l, j3, C0:C0 + CW].rearrange("(f p) -> p f", p=128), (), [cw], acc=True)
                kb.dma('sp', cb.t[:], conv_b[l, C0:C0 + CW].rearrange("(f p) -> p f", p=128), (), [cb])
                xt = [sb(st, "fx0", [128, 8, 512], F32)] * 2
                hT = [sb(st, f"fh{i}", [128, 8, 512], BF) for i in range(2)]
                sq = sb(st, "fsq", [128, 8, 512], BF); rt = sb(st, "frt", [128, 512], F32)
                pss = psb(st, "fpss", [128, 512], F32)
                pg = [psb(st, f"fpg{i}", [128, 512], F32) for i in range(3)]
                pu = [psb(st, f"fpu{i}", [128, 512], F32) for i in range(3)]
                apad = sb(st, "apad", [128, HF, 514], F32)
                MEMSET(V, apad.t[:, :, 0:2], 0.0, [apad])
                apb = [Buf(apad.t) for _ in range(HF)]
                for b_ in apb:
                    b_.w = dict(apad.w)
                cvs = [sb(st, f"cv{i}", [128, 512], F32) for i in range(2)]
                cgs = [sb(st, f"cg{i}", [128, 512], F32) for i in range(2)]
                gout = [sb(st, f"gout{i}", [128, HF, 512], BF) for i in range(2)]
                sprev = sb(st, "sprev", [128, 2, HF, NS], F32)
                asmp = sb(st, "asmp", [128, HF, NS], F32)
                stc = contextlib.ExitStack()
                cvt = sb(stc, "cvt", [NS, 2, CW], F32)
                kb.dma('sp', cvt.t[:], c_cv[l, :, :, C0:C0 + CW], (), [cvt])
                if half == 0:
                    kb.dma('sp', o_scv[l, :, 0, :], c_cv[l, :, 1, :], (), ())
                for j in range(2):
                    for f in range(HF):
                        TR(pg[j].t[:, f * 16:(f + 1) * 16], cvt.t[:, j, f * 128:(f + 1) * 128], identf.t[0:NS, 0:NS], [cvt, identf], [pg[j]])
                    evac(sprev.t[:, j, :, :], pg[j].t[:, 0:HF * 16].rearrange("p (f b) -> p f b", b=NS), [pg[j]], [sprev])
                kb.barrier()
                stc.close()
                XTv = XT.rearrange("(k p) t -> p k t", p=128)
                pi = 0
                def prol(ti):
                    t0_, n_ = tiles[ti]
                    kb.dma('sp', xt[ti % 2].t[:, :, 0:n_], XTv[:, :, t0_:t0_ + n_], (), [xt[ti % 2]])
                    rmsnorm_tile(xt[ti % 2], n_, gf, hT[ti % 2], sq, rt, pss)
                prol(0)
                for ti, (t0, n) in enumerate(tiles):
                    a = ti % 2
                    smp = n == NS
                    if ti + 1 < len(tiles):
                        prol(ti + 1)
                    go = gout[a]
                    for f in range(HF):
                        p1, p2 = pg[pi % 3], pu[pi % 3]
                        pi += 1
                        for k in range(8):
                            MM(p1.t[:, 0:n], wg.t[:, k, f * 128:(f + 1) * 128], hT[a].t[:, k, 0:n], k == 0, k == 7, [wg, hT[a]], [p1])
                        for k in range(8):
                            MM(p2.t[:, 0:n], wu.t[:, k, f * 128:(f + 1) * 128], hT[a].t[:, k, 0:n], k == 0, k == 7, [wu, hT[a]], [p2])
                        w0, w1_, w2_ = cw.t[:, 0, f:f + 1], cw.t[:, 1, f:f + 1], cw.t[:, 2, f:f + 1]
                        cv, cg = cvs[f % 2], cgs[f % 2]
                        ab = apb[f]
                        if not smp:
                            CP(S_, apad.t[:, f, 2:514], p1.t[:, :], [p1], [ab])
                            TS(V, cv.t[:], apad.t[:, f, 0:512], w0, cb.t[:, f:f + 1], ALU.mult, ALU.add, [ab, cw, cb], [cv])
                            STT(cv.t[:], apad.t[:, f, 1:513], w1_, cv.t[:], ALU.mult, ALU.add, [ab, cw, cv], [cv])
                            STT(cv.t[:], apad.t[:, f, 2:514], w2_, cv.t[:], ALU.mult, ALU.add, [ab, cw, cv], [cv])
                            ACTF(cg.t[:], cv.t[:], AF.Gelu, [cv], [cg])
                            TT(V, go.t[:, f, :], cg.t[:], p2.t[:, :], ALU.mult, [cg, p2], [go])
                            CP(G_, apad.t[:, f, 0:2], apad.t[:, f, 512:514], [ab], [ab])
                        else:
                            CP(S_, asmp.t[:, f, :], p1.t[:, 0:n], [p1], [asmp])
                            TS(V, cv.t[:, 0:n], sprev.t[:, 0, f, :], w0, cb.t[:, f:f + 1], ALU.mult, ALU.add, [sprev, cw, cb], [cv])
                            STT(cv.t[:, 0:n], sprev.t[:, 1, f, :], w1_, cv.t[:, 0:n], ALU.mult, ALU.add, [sprev, cw, cv], [cv])
                            STT(cv.t[:, 0:n], asmp.t[:, f, :], w2_, cv.t[:, 0:n], ALU.mult, ALU.add, [asmp, cw, cv], [cv])
                            ACTF(cg.t[:, 0:n], cv.t[:, 0:n], AF.Gelu, [cv], [cg])
                            TT(V, go.t[:, f, 0:n], cg.t[:, 0:n], p2.t[:, 0:n], ALU.mult, [cg, p2], [go])
                    kb.dma('sp', Gs.rearrange("(f p) t -> p f t", p=128)[:, F0:F0 + HF, t0:t0 + n], go.t[:, :, 0:n], [go], ())
                pbs = (pg[0], pg[1], pg[2])
                pcs = sb(st, "pcs", [2, CW], F32)
                for f in range(HF):
                    TR(pbs[f // 4].t[0:2, (f % 4) * 128:(f % 4) * 128 + 128], apad.t[:, f, 0:2], identf.t[:], [apb[f], identf], list(pbs))
                for gi, pb in enumerate(pbs):
                    w_ = 512 if gi < 2 else CW - 1024
                    evac(pcs.t[:, gi * 512:gi * 512 + w_], pb.t[0:2, 0:w_], [pb], [pcs])
                kb.dma('sp', o_pcv[l, :, C0:C0 + CW], pcs.t[:], [pcs], ())
                scs = sb(st, "scs", [NS, CW], F32)
                pbu = (pu[0], pu[1], pu[2])
                for f in range(HF):
                    TR(pbu[f // 4].t[0:NS, (f % 4) * 128:(f % 4) * 128 + 128], asmp.t[:, f, :], identf.t[:], [asmp, identf], list(pbu))
                for gi, pb in enumerate(pbu):
                    w_ = 512 if gi < 2 else CW - 1024
                    evac(scs.t[:, gi * 512:gi * 512 + w_], pb.t[0:NS, 0:w_], [pb], [scs])
                kb.dma('sp', o_scv[l, :, 1, C0:C0 + CW], scs.t[:], [scs], ())
            kb.barrier()

        def stage_f2(l):
            with contextlib.ExitStack() as st:
                wd = sb(st, "wd", [128, NFT, D], BF)
                load_w_begin(st)
                load_w(wd, lambda kt: w_fd[l, kt * 128:(kt + 1) * 128, :], NFT, D)
                load_w_end()
                gt = [sb(st, "dg0", [128, NFT, 512], BF)] * 2
                xt = [sb(st, "dx0", [128, 8, 512], F32)] * 2
                pd = [psb(st, f"dpd{i}", [128, 512], F32) for i in range(4)]
                XTv = XT.rearrange("(k p) t -> p k t", p=128)
                pi = 0
                for ti, (t0, n) in enumerate(tiles):
                    a = ti % 2
                    kb.dma('sp', gt[a].t[:, :, 0:n], Gs.rearrange("(f p) t -> p f t", p=128)[:, :, t0:t0 + n], (), [gt[a]])
                    kb.dma('sp', xt[a].t[:, :, 0:n], XTv[:, :, t0:t0 + n], (), [xt[a]])
                    for dm in range(8):
                        p = pd[pi % 4]
                        pi += 1
                        for k in range(NFT):
                            MM(p.t[:, 0:n], wd.t[:, k, dm * 128:(dm + 1) * 128], gt[a].t[:, k, 0:n], k == 0, k == NFT - 1, [wd, gt[a]], [p])
                        TT(V, xt[a].t[:, dm, 0:n], xt[a].t[:, dm, 0:n], p.t[:, 0:n], ALU.add, [xt[a], p], [xt[a]])
                    kb.dma('sp', XTv[:, :, t0:t0 + n], xt[a].t[:, :, 0:n], [xt[a]], ())
            kb.barrier()

        def stage_fin():
            with contextlib.ExitStack() as st:
                gfn = sb(st, "gfin", [128, 8], F32)
                kb.dma('sp', gfn.t[:], g_fin[0].rearrange("(k p) -> p k", p=128), (), [gfn])
                xt = [sb(st, f"nx{i}", [128, 8, 512], F32) for i in range(2)]
                yo = [sb(st, f"ny{i}", [128, 8, 512], F32) for i in range(2)]
                sq = sb(st, "nsq", [128, 8, 512], BF); rt = sb(st, "nrt", [128, 512], F32)
                pss = psb(st, "npss", [128, 512], F32)
                pt = [psb(st, f"npt{i}", [128, 1024], F32) for i in range(2)]
                yt = [sb(st, f"nyt{i}", [128, D], F32) for i in range(2)]
                XTv = XT.rearrange("(k p) t -> p k t", p=128)
                bi = 0
                for ti, (t0, n) in enumerate(tiles):
                    a = ti % 2
                    kb.dma('sp', xt[a].t[:, :, 0:n], XTv[:, :, t0:t0 + n], (), [xt[a]])
                    rmsnorm_tile(xt[a], n, gfn, yo[a], sq, rt, pss)
                    for j in range((n + 127) // 128):
                        m = min(128, n - j * 128)
                        b2 = bi % 2
                        bi += 1
                        for k in range(8):
                            TR(pt[b2].t[0:m, k * 128:(k + 1) * 128], yo[a].t[:, k, j * 128:j * 128 + m], identf.t[:], [yo[a], identf], [pt[b2]])
                        evac(yt[b2].t[0:m, :], pt[b2].t[0:m, :], [pt[b2]], [yt[b2]])
                        if n == NS:
                            kb.dma('sp', o_ys[:, :], yt[b2].t[0:m, :], [yt[b2]], ())
                        else:
                            kb.dma('sp', o_yp[t0 + j * 128:t0 + j * 128 + 128, :], yt[b2].t[:], [yt[b2]], ())
            kb.barrier()

        import os
        dbg = os.environ.get("KSTAGES", "mem,in,att,s5,matt,mrg,f1,f2").split(",")
        nl = int(os.environ.get("KLAYERS", str(DEPTH)))
        stage_p0()
        for l in range(nl):
            if "mem" in dbg: stage_mem(l)
            if "in" in dbg: stage_in(l)
            if "att" in dbg: stage_att(l)
            if "s5" in dbg: stage_s5(l)
            if "matt" in dbg: stage_matt(l)
            if "mrg" in dbg: stage_mrg(l)
            if "f1" in dbg:
                stage_f1(l, 0)
                stage_f1(l, 1)
            if "f2" in dbg: stage_f2(l)
        stage_fin()
        kb.barrier()
    return nc


T_FULL = 4096
_cache = {}


def _in_maps(inp, T):
    maps = []
    for c in range(8):
        b = c % 4
        sl = slice(NS * c, NS * (c + 1))
        m = {
            "xp": np.ascontiguousarray(inp["x_prompt"][b, :T]),
            "xs": np.ascontiguousarray(inp["x_sample"][sl, 0]),
            "mem": np.ascontiguousarray(inp["mem_prompt"][b]),
            "c_swk": np.ascontiguousarray(inp["cache_swa_k"][:, sl].reshape(DEPTH, NS, 128, 128)),
            "c_swv": np.ascontiguousarray(inp["cache_swa_v"][:, sl].reshape(DEPTH, NS, 128, 128)),
            "c_sre": np.ascontiguousarray(inp["state_ssm_re"][:, sl].reshape(DEPTH, NS, 2048)),
            "c_sim": np.ascontiguousarray(inp["state_ssm_im"][:, sl].reshape(DEPTH, NS, 2048)),
            "c_cv": np.ascontiguousarray(inp["cache_ffn_conv"][:, sl]),
            "c_mk": np.ascontiguousarray(inp["cache_mem_k"][:, sl].reshape(DEPTH, NS, 256, 512)),
            "c_mv": np.ascontiguousarray(inp["cache_mem_v"][:, sl].reshape(DEPTH, NS, 256, 512)),
            "norm_final_g": np.ascontiguousarray(inp["norm_final_g"].reshape(1, D)),
        }
        for k in ("norm_mix_g", "norm_ffn_g", "norm_mem_g", "w_in", "sinks", "w_br_attn", "lam_re", "lam_im", "log_dt",
                  "b_re", "b_im", "c_re", "c_im", "d_skip", "w_glu_a", "w_glu_b", "w_mem_kv", "w_br_mem", "w_out",
                  "w_ffn_gate", "w_ffn_up", "conv_w", "conv_b", "w_ffn_down"):
            m[k] = np.ascontiguousarray(inp[k])
        maps.append(m)
    return maps


def run(inp, T):
    if T not in _cache:
        _cache[T] = build(T)
    nc = _cache[T]
    inp = {k: np.asarray(v, dtype=np.float32) for k, v in inp.items()}
    import os
    ncores = int(os.environ.get("KCORES", "8"))
    res = run_bass_kernel_spmd(nc, _in_maps(inp, T)[:ncores], core_ids=list(range(ncores))).results
    res = [res[c % ncores] for c in range(8)]
    P = lambda name: np.stack([res[b][name] for b in range(4)], axis=1)
    Sm = lambda name: np.concatenate([res[c][name] for c in range(8)], axis=1)
    y_p = np.stack([res[b]["o_yp"] for b in range(4)], axis=0)
    y_s = np.concatenate([res[c]["o_ys"] for c in range(8)], axis=0)[:, None, :]
    return (y_p, y_s,
            P("o_pk").reshape(DEPTH, 4, 128, 2, 64), P("o_pv").reshape(DEPTH, 4, 128, 2, 64),
            P("o_pre").reshape(DEPTH, 4, 32, 64), P("o_pim").reshape(DEPTH, 4, 32, 64),
            P("o_pcv"), P("o_pmk").reshape(DEPTH, 4, 256, 4, 128), P("o_pmv").reshape(DEPTH, 4, 256, 4, 128),
            Sm("o_sk").reshape(DEPTH, 128, 128, 2, 64), Sm("o_sv").reshape(DEPTH, 128, 128, 2, 64),
            Sm("o_sre").reshape(DEPTH, 128, 32, 64), Sm("o_sim").reshape(DEPTH, 128, 32, 64), Sm("o_scv"))


def kernel(**inputs):
    return run(inputs, T_FULL)
```

```python
import contextlib
import math
import os
import numpy as np
import concourse.bass as bass
import concourse.mybir as mybir
from concourse.bass_utils import run_bass_kernel_spmd

F32 = mybir.dt.float32
BF = mybir.dt.bfloat16
I32 = mybir.dt.int32
AF = mybir.ActivationFunctionType
ALU = mybir.AluOpType
AX = mybir.AxisListType

D = 1024
DEPTH = 4
NS = 16
DFF = 2816
NFT = 22
INC = 4864
EPS = 1e-6
TWO_PI = 2.0 * math.pi


class Buf:
    def __init__(self, t, psum=False):
        self.t = t
        self.w = {}
        self.r = {}
        self.psum = psum


class KB:
    def __init__(self, nc, es):
        self.nc = nc
        self.es = es
        self.E = {'pe': nc.tensor, 'act': nc.scalar, 'dve': nc.vector, 'pool': nc.gpsimd, 'sp': nc.sync}
        self.cur = {}
        self.cnt = {}
        self.nsem = 0
        self.waited = {e: {} for e in self.E}
        for e in self.E:
            self._newesem(e)
        self.dq = {q: [[self._sem(), 0] for _ in range(8)] for q in ('sp', 'pool', 'act')}
        self.dqi = {q: 0 for q in self.dq}

    def _sem(self):
        self.nsem += 1
        return self.es.enter_context(self.nc.semaphore(f"sm{self.nsem}"))

    def _newesem(self, e):
        self.cur[e] = self._sem()
        self.cnt[e] = 0

    def wait(self, e, sem, val):
        k = id(sem)
        if self.waited[e].get(k, 0) >= val:
            return
        self.E[e].wait_ge(sem, val)
        self.waited[e][k] = val

    def deps(self, e, r, w, acc=False):
        own = self.cur[e]
        for b in r:
            for sem, val in b.w.values():
                if e == 'pe' and sem is own:
                    continue
                self.wait(e, sem, val)
            if b.psum:
                for sem, val in b.r.values():
                    if sem is not own:
                        self.wait(e, sem, val)
        if acc:
            return
        for b in w:
            for sem, val in list(b.w.values()) + list(b.r.values()):
                if e == 'pe' and sem is own:
                    continue
                self.wait(e, sem, val)

    def mark(self, sem, val, r, w, acc=False):
        for b in r:
            b.r[id(sem)] = (sem, val)
        for b in w:
            if acc:
                b.w[id(sem)] = (sem, val)
            else:
                b.w = {id(sem): (sem, val)}
                b.r = {}

    def op(self, e, fn, r=(), w=(), acc=False):
        self.deps(e, r, w, acc)
        ins = fn()
        self.cnt[e] += 1
        ins.then_inc(self.cur[e], 1)
        self.mark(self.cur[e], self.cnt[e], r, w, acc)

    def dma(self, q, out, in_, r=(), w=(), acc=False):
        slot = self.dq[q][self.dqi[q] % 8]
        self.dqi[q] += 1
        self.wait(q, slot[0], slot[1])
        self.deps(q, r, w, acc)
        self.E[q].dma_start(out=out, in_=in_).then_inc(slot[0], 16)
        slot[1] += 16
        self.mark(slot[0], slot[1], r, w, acc)

    def barrier(self):
        for e in self.E:
            for e2 in self.E:
                if e2 != e and self.cnt[e2] > 0:
                    self.wait(e, self.cur[e2], self.cnt[e2])
            for q in self.dq:
                for sem, val in self.dq[q]:
                    if val > 0:
                        self.wait(e, sem, val)
        for e in self.E:
            if self.cnt[e] > 20000:
                self._newesem(e)


def build(T):
    assert T % 512 == 0
    NTL = T // 512
    TA = T + NS
    nc = bass.Bass("TRN2", target_bir_lowering=False)

    def din(name, shape, dt=F32):
        return nc.dram_tensor(name, list(shape), dt, kind="ExternalInput").ap()

    def dout(name, shape):
        return nc.dram_tensor(name, list(shape), F32, kind="ExternalOutput").ap()

    def dscr(name, shape, dt):
        return nc.dram_tensor(name, list(shape), dt, kind="Internal").ap()

    xp = din("xp", [T, D]); xs = din("xs", [NS, D]); mem = din("mem", [256, D])
    c_swk = din("c_swk", [DEPTH, NS, 128, 128]); c_swv = din("c_swv", [DEPTH, NS, 128, 128])
    c_sre = din("c_sre", [DEPTH, NS, 2048]); c_sim = din("c_sim", [DEPTH, NS, 2048])
    c_cv = din("c_cv", [DEPTH, NS, 2, DFF])
    c_mk = din("c_mk", [DEPTH, NS, 256, 512]); c_mv = din("c_mv", [DEPTH, NS, 256, 512])
    g_mix = din("norm_mix_g", [DEPTH, D]); g_ffn = din("norm_ffn_g", [DEPTH, D])
    g_mem = din("norm_mem_g", [DEPTH, D]); g_fin = din("norm_final_g", [1, D])
    w_in = din("w_in", [DEPTH, D, INC]); sinks = din("sinks", [DEPTH, 8])
    w_bra = din("w_br_attn", [DEPTH, 512, D])
    lam_re = din("lam_re", [DEPTH, 32, 64]); lam_im = din("lam_im", [DEPTH, 32, 64])
    log_dt = din("log_dt", [DEPTH, 32])
    b_re = din("b_re", [DEPTH, 32, 64, 16]); b_im = din("b_im", [DEPTH, 32, 64, 16])
    c_re = din("c_re", [DEPTH, 32, 16, 64]); c_im = din("c_im", [DEPTH, 32, 16, 64])
    d_skip = din("d_skip", [DEPTH, 512])
    w_ga = din("w_glu_a", [DEPTH, 512, D]); w_gb = din("w_glu_b", [DEPTH, 512, D])
    w_mkv = din("w_mem_kv", [DEPTH, D, 1024]); w_brm = din("w_br_mem", [DEPTH, 512, D])
    w_out = din("w_out", [DEPTH, D, D])
    w_fg = din("w_ffn_gate", [DEPTH, D, DFF]); w_fu = din("w_ffn_up", [DEPTH, D, DFF])
    conv_w = din("conv_w", [DEPTH, 3, DFF]); conv_b = din("conv_b", [DEPTH, DFF])
    w_fd = din("w_ffn_down", [DEPTH, DFF, D])
    o_yp = dout("o_yp", [T, D]); o_ys = dout("o_ys", [NS, D])
    o_pk = dout("o_pk", [DEPTH, 128, 128]); o_pv = dout("o_pv", [DEPTH, 128, 128])
    o_pre = dout("o_pre", [DEPTH, 2048]); o_pim = dout("o_pim", [DEPTH, 2048])
    o_pcv = dout("o_pcv", [DEPTH, 2, DFF])
    o_pmk = dout("o_pmk", [DEPTH, 256, 512]); o_pmv = dout("o_pmv", [DEPTH, 256, 512])
    o_sk = dout("o_sk", [DEPTH, NS, 128, 128]); o_sv = dout("o_sv", [DEPTH, NS, 128, 128])
    o_sre = dout("o_sre", [DEPTH, NS, 2048]); o_sim = dout("o_sim", [DEPTH, NS, 2048])
    o_scv = dout("o_scv", [DEPTH, NS, 2, DFF])
    XT = dscr("XT", [D, TA], F32)
    Qs = dscr("Qs", [8, 64, TA], BF)
    Ks = dscr("Ks", [2, 64, 128 + TA], BF)
    Vs = dscr("Vs", [128 + T, 128], BF)
    Us = dscr("Us", [512, TA], BF)
    XQs = dscr("XQs", [512, TA], BF)
    SGs = dscr("SGs", [3072, TA], BF)
    OAs = dscr("OAs", [512, TA], BF)
    Zs = dscr("Zs", [512, TA], BF)
    OCs = dscr("OCs", [512, TA], BF)
    Gs = dscr("Gs", [DFF, TA], BF)
    VNs = dscr("VNs", [NS, 128], F32)

    tiles = [(i * 512, 512) for i in range(NTL)] + [(T, NS)]

    with contextlib.ExitStack() as es:
        es.enter_context(nc.allow_non_contiguous_dma(reason="small parameter tables"))
        kb = KB(nc, es)

        uid = [0]

        def sb(st, name, shape, dt):
            uid[0] += 1
            return Buf(st.enter_context(nc.sbuf_tensor(f"{name}_{uid[0]}", list(shape), dt)))

        def sbn(st, name, shape, dt, n):
            t = st.enter_context(nc.sbuf_tensor(name, list(shape), dt))
            return t, [Buf(t) for _ in range(n)]

        def psb(st, name, shape, dt):
            uid[0] += 1
            nb = int(np.prod(shape[1:])) * (2 if dt == BF else 4)
            assert nb % 2048 == 0, (name, shape)
            return Buf(st.enter_context(nc.psum_tensor(f"{name}_{uid[0]}", list(shape), dt)), psum=True)

        V, S_, G_, PE = 'dve', 'act', 'pool', 'pe'

        def TT(e, out, in0, in1, op, r, w):
            kb.op(e, lambda: kb.E[e].tensor_tensor(out=out, in0=in0, in1=in1, op=op), r, w)

        def TS(e, out, in0, s1, s2, op0, op1, r, w):
            if op1 is None:
                kb.op(e, lambda: kb.E[e].tensor_scalar(out=out, in0=in0, scalar1=s1, scalar2=None, op0=op0), r, w)
            else:
                kb.op(e, lambda: kb.E[e].tensor_scalar(out=out, in0=in0, scalar1=s1, scalar2=s2, op0=op0, op1=op1), r, w)

        def STT(out, in0, sc, in1, op0, op1, r, w):
            kb.op(V, lambda: nc.vector.scalar_tensor_tensor(out=out, in0=in0, scalar=sc, in1=in1, op0=op0, op1=op1), r, w)

        def ACTF(out, in_, func, r, w, bias=None, scale=None, accum=None):
            kw = {}
            if bias is not None:
                kw['bias'] = bias
            if scale is not None:
                kw['scale'] = scale
            if accum is not None:
                kw['accum_out'] = accum
            kb.op(S_, lambda: nc.scalar.activation(out=out, in_=in_, func=func, **kw), r, w)

        def CP(e, out, in_, r, w, acc=False):
            if e == S_:
                kb.op(e, lambda: nc.scalar.copy(out=out, in_=in_), r, w, acc)
            else:
                kb.op(e, lambda: kb.E[e].tensor_copy(out=out, in_=in_), r, w, acc)

        def MM(out, lhsT, rhs, start, stop, r, w):
            kb.op(PE, lambda: nc.tensor.matmul(out, lhsT, rhs, start=start, stop=stop), r, w)

        def TR(out, in_, ident, r, w):
            kb.op(PE, lambda: nc.tensor.transpose(out, in_, ident), r, w)

        def MEMSET(e, ap, val, w):
            kb.op(e, lambda: kb.E[e].memset(ap, val), (), w)

        wst = {"bufs": None, "i": 0}

        def load_w_begin(st):
            wst["bufs"] = [sb(st, f"wstg{i}", [128, 2048], F32) for i in range(2)]

        def load_w(dst, dram_rows_fn, nkt, ncols, q='pool'):
            for kt in range(nkt):
                src_ = dram_rows_fn(kt)
                for c0 in range(0, ncols, 2048):
                    c1 = min(ncols, c0 + 2048)
                    wst["n"] = wst.get("n", 0) + 1
                    if wst["bufs"] is None or wst["n"] % 3 != 0:
                        kb.dma('pool', dst.t[:, kt, c0:c1], src_[:, c0:c1], (), [dst], acc=True)
                        continue
                    sg_ = wst["bufs"][wst["i"] % 2]
                    e = (S_, V)[wst["i"] % 2]
                    wst["i"] += 1
                    kb.dma('sp', sg_.t[:, 0:c1 - c0], src_[:, c0:c1], (), [sg_])
                    CP(e, dst.t[:, kt, c0:c1], sg_.t[:, 0:c1 - c0], [sg_], [dst], acc=True)

        def load_w_end():
            wst["bufs"] = None

        evac_rr = [0]

        def evac(out, in_, r, w):
            e = (S_, V)[evac_rr[0] % 2]
            evac_rr[0] += 1
            CP(e, out, in_, r, w)

        cst = es
        ident_i = sb(cst, "ident_i", [128, 128], I32)
        identf = sb(cst, "identf", [128, 128], F32)
        identb = sb(cst, "identb", [128, 128], BF)
        onesf = sb(cst, "onesf", [128, 128], F32)
        kb.op(G_, lambda: nc.gpsimd.iota(ident_i.t[:], pattern=[[1, 128]], base=0, channel_multiplier=-1), (), [ident_i])
        TS(V, identf.t[:], ident_i.t[:], 0, None, ALU.is_equal, None, [ident_i], [identf])
        CP(V, identb.t[:], identf.t[:], [identf], [identb])
        MEMSET(V, onesf.t[:], 1.0, [onesf])
        onesb = sb(cst, "onesb", [128, 128], BF)
        MEMSET(V, onesb.t[:], 1.0, [onesb])
        dist_i = sb(cst, "dist_i", [128, 256], I32)
        distf = sb(cst, "distf", [128, 256], F32)
        mskf = sb(cst, "mskf", [128, 256], F32)
        msk2 = sb(cst, "msk2", [128, 256], F32)
        biasA = sb(cst, "biasA", [128, 8, 256], F32)
        biasA0 = sb(cst, "biasA0", [128, 8, 256], F32)
        kb.op(G_, lambda: nc.gpsimd.iota(dist_i.t[:], pattern=[[-1, 256]], base=128, channel_multiplier=1), (), [dist_i])
        CP(V, distf.t[:], dist_i.t[:], [dist_i], [distf])
        TS(V, mskf.t[:], distf.t[:], 0.0, None, ALU.is_ge, None, [distf], [mskf])
        TS(V, msk2.t[:], distf.t[:], 128.0, None, ALU.is_le, None, [distf], [msk2])
        TT(V, mskf.t[:], mskf.t[:], msk2.t[:], ALU.mult, [mskf, msk2], [mskf])
        TS(V, msk2.t[:], mskf.t[:], -1.0, 30000.0, ALU.add, ALU.mult, [mskf], [msk2])
        TT(V, distf.t[:], distf.t[:], mskf.t[:], ALU.mult, [distf, mskf], [distf])
        for h in range(8):
            slope = 2.0 ** (-(h + 1))
            STT(biasA.t[:, h, :], distf.t[:], -slope, msk2.t[:], ALU.mult, ALU.add, [distf, msk2], [biasA])
        CP(V, biasA0.t[:], biasA.t[:], [biasA], [biasA0])
        MEMSET(V, biasA0.t[:, :, 0:128], -30000.0, [biasA0])
        bs_i = sb(cst, "bs_i", [4, 129], I32)
        bs_f = sb(cst, "bs_f", [4, 129], F32)
        biasS = sb(cst, "biasS", [4, 2, 129], F32)
        slp = sb(cst, "slp", [4, 2], F32)
        slp_i = sb(cst, "slp_i", [4, 2], I32)
        kb.op(G_, lambda: nc.gpsimd.iota(bs_i.t[:], pattern=[[-1, 129]], base=128, channel_multiplier=0), (), [bs_i])
        CP(V, bs_f.t[:], bs_i.t[:], [bs_i], [bs_f])
        kb.op(G_, lambda: nc.gpsimd.iota(slp_i.t[:], pattern=[[4, 2]], base=1, channel_multiplier=1), (), [slp_i])
        CP(V, slp.t[:], slp_i.t[:], [slp_i], [slp])
        ACTF(slp.t[:], slp.t[:], AF.Exp, [slp], [slp], scale=-math.log(2.0))
        for k2 in range(2):
            TS(V, biasS.t[:, k2, :], bs_f.t[:], slp.t[:, k2:k2 + 1], -1.0, ALU.mult, ALU.mult, [bs_f, slp], [biasS])
        jidx_i = sb(cst, "jidx_i", [128, 512], I32)
        jidx = sb(cst, "jidx", [128, 512], F32)
        kb.op(G_, lambda: nc.gpsimd.iota(jidx_i.t[:], pattern=[[1, 512]], base=0, channel_multiplier=0), (), [jidx_i])
        CP(V, jidx.t[:], jidx_i.t[:], [jidx_i], [jidx])
        xqT_s = sb(cst, "xqT_s", [128, 4, NS], BF)
        qT_s = sb(cst, "qT_s", [64, 8, NS], BF)
        kT_s = sb(cst, "kT_s", [64, 2, NS], BF)
        uT_s = sb(cst, "uT_s", [128, 4, NS], BF)
        mkT = sb(cst, "mkT", [128, 4, 256], BF)
        mvT = sb(cst, "mvT", [128, 2, 512], BF)
        kb.barrier()

        def stage_p0():
            with contextlib.ExitStack() as st:
                xin = [sb(st, f"xin{i}", [128, D], F32) for i in range(2)]
                xo = [sb(st, f"xo{i}", [128, 8, 128], F32) for i in range(2)]
                pt = [psb(st, f"p0t{i}", [128, 1024], F32) for i in range(2)]
                blocks = [(j * 128, 128, xp[j * 128:(j + 1) * 128, :]) for j in range(T // 128)] + [(T, NS, xs[:, :])]
                for bi, (t0, n, src) in enumerate(blocks):
                    a = bi % 2
                    kb.dma('sp', xin[a].t[0:n, :], src, (), [xin[a]])
                    for k in range(8):
                        TR(pt[a].t[:, k * 128:k * 128 + n], xin[a].t[0:n, k * 128:(k + 1) * 128], identf.t[0:n, 0:n], [xin[a], identf], [pt[a]])
                    evac(xo[a].t[:, :, 0:n], pt[a].t[:].rearrange("p (k t) -> p k t", k=8)[:, :, 0:n], [pt[a]], [xo[a]])
                    kb.dma('sp', XT.rearrange("(k p) t -> p k t", p=128)[:, :, t0:t0 + n], xo[a].t[:, :, 0:n], [xo[a]], ())
            kb.barrier()

        def rmsnorm_tile(xt, n, gcol, hout, sq, rt, pss, r_extra=()):
            ACTF(sq.t[:, :, 0:n], xt.t[:, :, 0:n], AF.Square, [xt], [sq])
            for k in range(8):
                MM(pss.t[:, 0:n], onesb.t[:], sq.t[:, k, 0:n], k == 0, k == 7, [onesb, sq], [pss])
            ACTF(rt.t[:, 0:n], pss.t[:, 0:n], AF.Sqrt, [pss], [rt], bias=EPS, scale=1.0 / D)
            kb.op(V, lambda: nc.vector.reciprocal(out=rt.t[:, 0:n], in_=rt.t[:, 0:n]), [rt], [rt])
            for k in range(8):
                STT(hout.t[:, k, 0:n], xt.t[:, k, 0:n], gcol.t[:, k:k + 1], rt.t[:, 0:n], ALU.mult, ALU.mult, [xt, gcol, rt], [hout])

        def stage_mem(l):
            with contextlib.ExitStack() as st:
                wm = sb(st, "wm", [128, 8, 1024], BF)
                load_w_begin(st)
                load_w(wm, lambda kt: w_mkv[l, kt * 128:(kt + 1) * 128, :], 8, 1024)
                load_w_end()
                gm = sb(st, "gm", [128, 8], F32)
                kb.dma('sp', gm.t[:], g_mem[l].rearrange("(k p) -> p k", p=128), (), [gm])
                min_ = sb(st, "min_", [128, 2, D], F32)
                kb.dma('sp', min_.t[:], mem.rearrange("(j p) d -> p j d", p=128), (), [min_])
                mx = sb(st, "mx", [128, 8, 512], F32)
                mh = sb(st, "mh", [128, 8, 512], BF)
                sq = sb(st, "msq", [128, 8, 512], BF)
                rt = sb(st, "mrt", [128, 512], F32)
                pss = psb(st, "mpss", [128, 512], F32)
                pt = psb(st, "mpt", [128, 1024], F32)
                po = [psb(st, f"mpo{i}", [128, 512], F32) for i in range(2)]
                mo = [sb(st, f"mo{i}", [128, 512], F32) for i in range(2)]
                for j in range(2):
                    for k in range(8):
                        TR(pt.t[:, k * 128:(k + 1) * 128], min_.t[:, j, k * 128:(k + 1) * 128], identf.t[:], [min_, identf], [pt])
                    evac(mx.t[:, :, j * 128:(j + 1) * 128], pt.t[:].rearrange("p (k t) -> p k t", k=8), [pt], [mx])
                rmsnorm_tile(mx, 256, gm, mh, sq, rt, pss)
                cnt = 0
                for j in range(2):
                    for half in range(2):
                        a = cnt % 2
                        cnt += 1
                        for k in range(8):
                            MM(po[a].t[:, :], mh.t[:, k, j * 128:(j + 1) * 128], wm.t[:, k, half * 512:(half + 1) * 512], k == 0, k == 7, [mh, wm], [po[a]])
                        evac(mo[a].t[:], po[a].t[:], [po[a]], [mo[a]])
                        dst = (o_pmk, o_pmv)[half]
                        kb.dma('sp', dst[l, j * 128:(j + 1) * 128, :], mo[a].t[:], [mo[a]], ())
                        if half == 1:
                            CP(V, mvT.t[:, j, :], mo[a].t[:], [mo[a]], [mvT])
                for h in range(4):
                    a = h % 2
                    for k in range(8):
                        MM(po[a].t[:, 0:256], wm.t[:, k, h * 128:(h + 1) * 128], mh.t[:, k, 0:256], k == 0, k == 7, [mh, wm], [po[a]])
                    evac(mkT.t[:, h, :], po[a].t[:, 0:256], [po[a]], [mkT])
            kb.barrier()

        def stage_in(l):
            with contextlib.ExitStack() as st:
                win = sb(st, "win", [128, 8, INC], BF)
                load_w_begin(st)
                load_w(win, lambda kt: w_in[l, kt * 128:(kt + 1) * 128, :], 8, INC)
                load_w_end()
                gm = sb(st, "gmix", [128, 8], F32)
                kb.dma('sp', gm.t[:], g_mix[l].rearrange("(k p) -> p k", p=128), (), [gm])
                xt = [sb(st, "xt0", [128, 8, 512], F32)] * 2
                hT = [sb(st, f"hT{i}", [128, 8, 512], BF) for i in range(2)]
                sq = sb(st, "sq", [128, 8, 512], BF)
                rt = sb(st, "rt", [128, 512], F32)
                pss = psb(st, "pss", [128, 512], F32)
                pp = [psb(st, f"pp{i}", [128, 512], F32) for i in range(5)]
                stq = sb(st, "stq", [64, 8, 512], BF)
                stk = sb(st, "stk", [64, 2, 512], BF)
                stv = sb(st, "stv", [128, 4, 128], BF)
                stvf = sb(st, "stvf", [128, 128], F32)
                stkf = sb(st, "stkf", [128, 128], F32)
                stu = sb(st, "stu", [128, 4, 512], BF)
                stx = sb(st, "stx", [128, 4, 512], BF)
                sg = [sb(st, f"sg{i}", [128, 8, 512], BF) for i in range(2)]
                ppi = [0]

                def nextp():
                    p = pp[ppi[0] % 5]
                    ppi[0] += 1
                    return p
                XTv = XT.rearrange("(k p) t -> p k t", p=128)
                def prol(ti):
                    t0_, n_ = tiles[ti]
                    kb.dma('sp', xt[ti % 2].t[:, :, 0:n_], XTv[:, :, t0_:t0_ + n_], (), [xt[ti % 2]])
                    rmsnorm_tile(xt[ti % 2], n_, gm, hT[ti % 2], sq, rt, pss)
                prol(0)
                for ti, (t0, n) in enumerate(tiles):
                    a = ti % 2
                    smp = (n == NS)
                    if ti + 1 < len(tiles):
                        prol(ti + 1)
                    h_ = hT[a]

                    def proj(c0, M, outap, r_w, sig=False):
                        p = nextp()
                        for k in range(8):
                            MM(p.t[0:M, 0:n], win.t[:, k, c0:c0 + M], h_.t[:, k, 0:n], k == 0, k == 7, [win, h_], [p])
                        if sig:
                            ACTF(outap, p.t[0:M, 0:n], AF.Sigmoid, [p], r_w)
                        else:
                            evac(outap, p.t[0:M, 0:n], [p], r_w)
                    kin = os.environ.get("KIN", "q,k,v,u,x,g").split(",")
                    for h in range(8 if "q" in kin else 0):
                        proj(64 * h, 64, (qT_s.t[:, h, :] if smp else stq.t[:, h, 0:n]), [qT_s if smp else stq])
                    for h in range(2 if "k" in kin else 0):
                        proj(512 + 64 * h, 64, (kT_s.t[:, h, :] if smp else stk.t[:, h, 0:n]), [kT_s if smp else stk])
                    if not smp and "q" in kin and "k" in kin:
                        kb.dma('sp', Qs.rearrange("h p t -> p h t")[:, :, t0:t0 + n], stq.t[:, :, 0:n], [stq], ())
                        kb.dma('sp', Ks.rearrange("h p t -> p h t")[:, :, 128 + t0:128 + t0 + n], stk.t[:, :, 0:n], [stk], ())
                    nb = 1 if smp else 4
                    if "v" not in kin:
                        nb = 0
                    kvs = os.environ.get("KVS", "p,s").split(",")
                    if (smp and "s" not in kvs) or ((not smp) and "p" not in kvs):
                        nb = 0
                    kv = os.environ.get("KV", "vn,osv,opv,ktok,vs").split(",")
                    for j in range(nb):
                        m = NS if smp else 128
                        p = nextp()
                        kvx = os.environ.get("KVX", "")
                        for k in range(8):
                            lh = win.t[:, k, 0:m] if kvx == "lw" else h_.t[:, k, j * 128:j * 128 + m]
                            rh = h_.t[:, k, 0:128] if kvx == "rh" else win.t[:, k, 640:768]
                            MM(p.t[0:m, 0:128], lh, rh, k == 0, k == 7, [win, h_], [p])
                        last = (not smp) and (t0 + 512 == T) and j == 3
                        if smp:
                            CP(V, stvf.t[0:m, :], p.t[0:m, 0:128], [p], [stvf])
                            if "vn" in kv:
                                kb.dma('sp', VNs[:, :], stvf.t[0:m, :], [stvf], ())
                            if "osv" in kv:
                                kb.dma('sp', o_sv[l, :, 127, :], stvf.t[0:m, :], [stvf], ())
                        else:
                            kve = os.environ.get("KVE", "")
                            if kve != "noevac":
                                evac(stv.t[:, j, :], p.t[:, 0:128], [p], [stv])
                            if last and kve != "nocp":
                                CP(V, stvf.t[:], p.t[:, 0:128], [p], [stvf])
                                if "opv" in kv:
                                    kb.dma('sp', o_pv[l, :, :], stvf.t[:], [stvf], ())
                        if (smp or last) and "ktok" in kv:
                            p = nextp()
                            for k in range(8):
                                MM(p.t[0:m, 0:128], h_.t[:, k, j * 128:j * 128 + m], win.t[:, k, 512:640], k == 0, k == 7, [win, h_], [p])
                            CP(V, stkf.t[0:m, :], p.t[0:m, 0:128], [p], [stkf])
                            if smp:
                                kb.dma('sp', o_sk[l, :, 127, :], stkf.t[0:m, :], [stkf], ())
                            else:
                                kb.dma('sp', o_pk[l, :, :], stkf.t[:], [stkf], ())
                    if not smp and "v" in kin and "vs" in kv:
                        kb.dma('sp', Vs[128 + t0:128 + t0 + 512, :].rearrange("(j p) c -> p j c", p=128), stv.t[:], [stv], ())
                    for f in range(4 if "u" in kin else 0):
                        proj(768 + 128 * f, 128, (uT_s.t[:, f, :] if smp else stu.t[:, f, 0:n]), [uT_s if smp else stu])
                    for f in range(4 if "x" in kin else 0):
                        proj(1280 + 128 * f, 128, (xqT_s.t[:, f, :] if smp else stx.t[:, f, 0:n]), [xqT_s if smp else stx])
                    if "u" in kin:
                        kb.dma('sp', Us.rearrange("(f p) t -> p f t", p=128)[:, :, t0:t0 + n],
                               (uT_s.t[:] if smp else stu.t[:, :, 0:n]), [uT_s if smp else stu], ())
                    if not smp and "x" in kin:
                        kb.dma('sp', XQs.rearrange("(f p) t -> p f t", p=128)[:, :, t0:t0 + n], stx.t[:, :, 0:n], [stx], ())
                    for gi in range(3 if "g" in kin else 0):
                        s_ = sg[gi % 2]
                        for f in range(8):
                            proj(1792 + gi * 1024 + f * 128, 128, s_.t[:, f, 0:n], [s_], sig=True)
                        kb.dma('sp', SGs.rearrange("(f p) t -> p f t", p=128)[:, gi * 8:(gi + 1) * 8, t0:t0 + n], s_.t[:, :, 0:n], [s_], ())
            kb.barrier()

        def attn_unit(*args):
            for _ in attn_unit_g(*args):
                pass

        def attn_unit_g(st_t, nq, heads, nk, qfn, kfn, vfn, hd, scale, bias_ap, sink_ap, out_fn):
            pS, sS, pS_b, pT, sT, pO, mx, rs, es_, dn = st_t
            nkt = (nk + 127) // 128
            for h in range(heads):
                qa, qb = qfn(h)
                ka, kbuf = kfn(h)
                MM(pS.t[0:nq, h, 0:nk], qa, ka, True, True, qb + kbuf, [pS])
            yield
            if bias_ap is not None:
                STT(sS.t[0:nq, 0:heads, 0:nk], pS.t[0:nq, 0:heads, 0:nk], scale, bias_ap[0], ALU.mult, ALU.add, [pS] + bias_ap[1], [sS])
            else:
                TS(V, sS.t[0:nq, 0:heads, 0:nk], pS.t[0:nq, 0:heads, 0:nk], scale, None, ALU.mult, None, [pS], [sS])
            kb.op(V, lambda: nc.vector.tensor_reduce(out=mx.t[0:nq, 0:heads], in_=sS.t[0:nq, 0:heads, 0:nk], axis=AX.X, op=ALU.max), [sS], [mx])
            if sink_ap is not None:
                TT(V, mx.t[0:nq, 0:heads], mx.t[0:nq, 0:heads], sink_ap[0], ALU.max, [mx] + sink_ap[1], [mx])
            TS(V, dn.t[0:nq, 0:heads], mx.t[0:nq, 0:heads], -1.0, None, ALU.mult, None, [mx], [dn])
            for h in range(heads):
                ACTF(sS.t[0:nq, h, 0:nk], sS.t[0:nq, h, 0:nk], AF.Exp, [sS, dn], [sS, rs],
                     bias=dn.t[0:nq, h:h + 1], accum=rs.t[0:nq, h:h + 1])
            yield
            if sink_ap is not None:
                TT(V, es_.t[0:nq, 0:heads], sink_ap[0], mx.t[0:nq, 0:heads], ALU.subtract, [mx] + sink_ap[1], [es_])
                ACTF(es_.t[0:nq, 0:heads], es_.t[0:nq, 0:heads], AF.Exp, [es_], [es_])
                TT(V, rs.t[0:nq, 0:heads], rs.t[0:nq, 0:heads], es_.t[0:nq, 0:heads], ALU.add, [rs, es_], [rs])
            kb.op(V, lambda: nc.vector.reciprocal(out=dn.t[0:nq, 0:heads], in_=rs.t[0:nq, 0:heads]), [rs], [dn])
            TT(V, pS_b.t[0:nq, 0:heads, 0:nk], sS.t[0:nq, 0:heads, 0:nk],
               dn.t[0:nq, 0:heads].rearrange("p (h o) -> p h o", o=1).to_broadcast([nq, heads, nk]), ALU.mult, [sS, dn], [pS_b])
            for h in range(heads):
                for kt in range(nkt):
                    kn = min(128, nk - kt * 128)
                    TR(pT.t[0:kn, h, kt, 0:nq], pS_b.t[0:nq, h, kt * 128:kt * 128 + kn], identb.t[0:nq, 0:nq], [pS_b, identb], [pT])
            for kt in range(nkt):
                kn = min(128, nk - kt * 128)
                CP(S_, sT.t[0:kn, 0:heads, kt, 0:nq], pT.t[0:kn, 0:heads, kt, 0:nq], [pT], [sT])
            for h in range(heads):
                for kt in range(nkt):
                    kn = min(128, nk - kt * 128)
                    va, vb = vfn(h, kt, kn)
                    MM(pO.t[0:hd, h, 0:nq], va, sT.t[0:kn, h, kt, 0:nq], kt == 0, kt == nkt - 1, vb + [sT], [pO])
            out_fn(pO)

        def run_pipelined(units):
            n = len(units)
            for s in range(n + 2):
                if s < n:
                    if units[s][0] is not None:
                        units[s][0]()
                    next(units[s][1])
                if 0 <= s - 1 < n:
                    next(units[s - 1][1])
                if 0 <= s - 2 < n:
                    for _ in units[s - 2][1]:
                        pass
                    if units[s - 2][2] is not None:
                        units[s - 2][2]()

        def attn_tiles(st, pfx, psum_from=None):
            if psum_from is not None:
                pS, pT, pO = psum_from[0], psum_from[3], psum_from[5]
                sS = sb(st, pfx + "sS", [128, 4, 256], F32)
                pS_b = sb(st, pfx + "pSb", [128, 4, 256], BF)
                sT = sb(st, pfx + "sT", [128, 4, 2, 128], BF)
                mx = sb(st, pfx + "mx", [128, 4], F32)
                rs = sb(st, pfx + "rs", [128, 4], F32)
                es_ = sb(st, pfx + "es", [128, 4], F32)
                dn = sb(st, pfx + "dn", [128, 4], F32)
                return (pS, sS, pS_b, pT, sT, pO, mx, rs, es_, dn)
            pS = psb(st, pfx + "pS", [128, 4, 256], F32)
            sS = sb(st, pfx + "sS", [128, 4, 256], F32)
            pS_b = sb(st, pfx + "pSb", [128, 4, 256], BF)
            pT = psb(st, pfx + "pT", [128, 4, 2, 128], BF)
            sT = sb(st, pfx + "sT", [128, 4, 2, 128], BF)
            pO = psb(st, pfx + "pO", [128, 4, 128], F32)
            mx = sb(st, pfx + "mx", [128, 4], F32)
            rs = sb(st, pfx + "rs", [128, 4], F32)
            es_ = sb(st, pfx + "es", [128, 4], F32)
            dn = sb(st, pfx + "dn", [128, 4], F32)
            return (pS, sS, pS_b, pT, sT, pO, mx, rs, es_, dn)

        def stage_att(l):
            with contextlib.ExitStack() as st:
                tls = [attn_tiles(st, "a0"), attn_tiles(st, "a1")]
                tl = tls[0]
                ucnt = [0]
                snk = sb(st, "snk", [128, 8], F32)
                kb.dma('sp', snk.t[:], sinks[l:l + 1, :].to_broadcast([128, 8]), (), [snk])
                snkS = sb(st, "snkS", [4, 2], F32)
                kb.dma('sp', snkS.t[:], sinks[l].rearrange("(k g) -> g k", g=4), (), [snkS])
                qt = [sb(st, f"aq{i}", [64, 8, 512], BF) for i in range(2)]
                kt_ = [sb(st, f"ak{i}", [64, 2, 640], BF) for i in range(2)]
                vt = [sb(st, f"av{i}", [128, 5, 128], BF) for i in range(2)]
                oa = [sb(st, f"ao{i}", [64, 8, 512], BF) for i in range(2)]
                zk = sb(st, "zk", [64, 2, 128], BF)
                zv = sb(st, "zv", [128, 128], BF)
                MEMSET(V, zk.t[:], 0.0, [zk])
                MEMSET(V, zv.t[:], 0.0, [zv])
                kb.dma('sp', Ks.rearrange("h p t -> p h t")[:, :, 0:128], zk.t[:], [zk], ())
                kb.dma('sp', Vs[0:128, :], zv.t[:], [zv], ())
                kb.barrier()
                units = []
                for ti in range(NTL):
                    a = ti % 2
                    t0 = ti * 512

                    def pre(a=a, t0=t0):
                        kb.dma('sp', qt[a].t[:], Qs.rearrange("h p t -> p h t")[:, :, t0:t0 + 512], (), [qt[a]])
                        kb.dma('sp', kt_[a].t[:], Ks.rearrange("h p t -> p h t")[:, :, t0:t0 + 640], (), [kt_[a]])
                        kb.dma('sp', vt[a].t[:], Vs[t0:t0 + 640, :].rearrange("(j p) c -> p j c", p=128), (), [vt[a]])

                    def post(a=a, t0=t0):
                        kb.dma('sp', OAs.rearrange("(h p) t -> p h t", p=64)[:, :, t0:t0 + 512], oa[a].t[:], [oa[a]], ())
                    for j in range(4):
                        for k2 in range(2):
                            bias = (biasA0 if (ti == 0 and j == 0) else biasA)
                            ucnt[0] += 1
                            g = attn_unit_g(
                                tls[ucnt[0] % 2], 128, 4, 256,
                                lambda h, a=a, j=j, k2=k2: (qt[a].t[:, 4 * k2 + h, j * 128:(j + 1) * 128], [qt[a]]),
                                lambda h, a=a, j=j, k2=k2: (kt_[a].t[:, k2, j * 128:j * 128 + 256], [kt_[a]]),
                                lambda h, kt, kn, a=a, j=j, k2=k2: (vt[a].t[:, j + kt, 64 * k2:64 * k2 + 64], [vt[a]]),
                                64, 0.125, (bias.t[:, 4 * k2:4 * k2 + 4, :], [bias]), (snk.t[:, 4 * k2:4 * k2 + 4], [snk]),
                                lambda pO, a=a, j=j, k2=k2: CP(S_, oa[a].t[:, 4 * k2:4 * k2 + 4, j * 128:(j + 1) * 128], pO.t[0:64, :, :], [pO], [oa[a]]))
                            first = (j == 0 and k2 == 0)
                            lastu = (j == 3 and k2 == 1)
                            units.append((pre if first else None, g, post if lastu else None))
                run_pipelined(units)
                kc = [sb(st, f"skc{i}", [128, 128], F32) for i in range(2)]
                vc = [sb(st, f"svc{i}", [128, 128], F32) for i in range(2)]
                vcb = [sb(st, f"svcb{i}", [128, 128], BF) for i in range(2)]
                vn = [sb(st, f"svn{i}", [1, 128], F32) for i in range(2)]
                vnb = [sb(st, f"svnb{i}", [1, 128], BF) for i in range(2)]
                ktx = [sb(st, f"sktx{i}", [64, 2, 129], BF) for i in range(2)]
                pk = tls[1][5]
                oas = sb(st, "oas", [64, 8, NS], BF)
                for b in range(NS):
                    a = b % 2
                    kb.dma('sp', kc[a].t[:], c_swk[l, b], (), [kc[a]])
                    kb.dma('sp', vc[a].t[:], c_swv[l, b], (), [vc[a]])
                    kb.dma('sp', vn[a].t[:], VNs[b:b + 1, :], (), [vn[a]])
                    kb.dma('sp', o_sk[l, b, 0:127, :], c_swk[l, b, 1:128, :], (), ())
                    kb.dma('sp', o_sv[l, b, 0:127, :], c_swv[l, b, 1:128, :], (), ())
                    for k2 in range(2):
                        TR(pk.t[0:64, k2, 0:128], kc[a].t[:, 64 * k2:64 * k2 + 64], identf.t[:], [kc[a], identf], [pk])
                    evac(ktx[a].t[:, :, 0:128], pk.t[0:64, 0:2, 0:128], [pk], [ktx[a]])
                    CP(V, ktx[a].t[:, :, 128:129], kT_s.t[:, :, b:b + 1], [kT_s], [ktx[a]])
                    CP(V, vcb[a].t[:], vc[a].t[:], [vc[a]], [vcb[a]])
                    CP(V, vnb[a].t[:], vn[a].t[:], [vn[a]], [vnb[a]])
                    for k2 in range(2):
                        attn_unit(
                            tl, 4, 1, 129,
                            lambda h: (qT_s.t[:, 4 * k2:4 * k2 + 4, b], [qT_s]),
                            lambda h: (ktx[a].t[:, k2, :], [ktx[a]]),
                            lambda h, kt, kn: ((vcb[a].t[:, 64 * k2:64 * k2 + 64], [vcb[a]]) if kt == 0 else (vnb[a].t[0:1, 64 * k2:64 * k2 + 64], [vnb[a]])),
                            64, 0.125, (biasS.t[:, k2:k2 + 1, :], [biasS]), (snkS.t[:, k2:k2 + 1], [snkS]),
                            lambda pO: evac(oas.t[:, 4 * k2:4 * k2 + 4, b], pO.t[0:64, 0, 0:4], [pO], [oas]))
                kb.dma('sp', OAs.rearrange("(h p) t -> p h t", p=64)[:, :, T:T + NS], oas.t[:], [oas], ())
            kb.barrier()


        def sin_of(out_ap, ang_ap, shape, tmpf, tmpi, tmpm, bufs_in, buf_out, phase=0.0):
            tf, ti_, tm = tmpf, tmpi, tmpm
            TS(V, tf[0], ang_ap, 1.0, phase, ALU.mult, ALU.add, bufs_in, [tf[1]])
            TS(V, tm[0], tf[0], 1.0 / TWO_PI, None, ALU.mult, None, [tf[1]], [tm[1]])
            CP(V, ti_[0], tm[0], [tm[1]], [ti_[1]])
            CP(V, tm[0], ti_[0], [ti_[1]], [tm[1]])
            STT(tf[0], tm[0], -TWO_PI, tf[0], ALU.mult, ALU.add, [tm[1], tf[1]], [tf[1]])
            TS(V, tm[0], tf[0], math.pi, None, ALU.is_gt, None, [tf[1]], [tm[1]])
            STT(tf[0], tm[0], -TWO_PI, tf[0], ALU.mult, ALU.add, [tm[1], tf[1]], [tf[1]])
            TS(V, tm[0], tf[0], -math.pi, None, ALU.is_lt, None, [tf[1]], [tm[1]])
            STT(tf[0], tm[0], TWO_PI, tf[0], ALU.mult, ALU.add, [tm[1], tf[1]], [tf[1]])
            TS(V, tf[0], tf[0], math.pi, -math.pi, ALU.min, ALU.max, [tf[1]], [tf[1]])
            ACTF(out_ap, tf[0], AF.Sin, [tf[1]], [buf_out])

        def bc3(ap2, n):
            P_, M_ = ap2.shape[0], ap2.shape[1]
            return ap2.rearrange("p (m o) -> p m o", o=1).to_broadcast([P_, M_, n])

        def stage_s5(l):
            with contextlib.ExitStack() as st:
                def t2(name, shape=(128, 16), dt=F32):
                    return sb(st, name, list(shape), dt)
                lr, li, dtl, dtt, th, mag, ar, ai = [t2(n) for n in ("lr", "li", "dtl", "dtt", "th", "mag", "ar", "ai")]
                cs, sn, den, fr, fi, tA, tB = [t2(n) for n in ("cs", "sn", "den", "fr", "fi", "tA", "tB")]
                tf = t2("rtf"); ti_ = t2("rti", dt=I32); tm = t2("rtm")
                kb.dma('sp', lr.t[:], lam_re[l].rearrange("(m two) p -> (two p) m", two=2), (), [lr])
                kb.dma('sp', li.t[:], lam_im[l].rearrange("(m two) p -> (two p) m", two=2), (), [li])
                ldv = log_dt[l].rearrange("(m two) -> two m", two=2)
                kb.dma('sp', dtl.t[0:64, :], ldv[0:1, :].to_broadcast([64, 16]), (), [dtl], acc=True)
                kb.dma('sp', dtl.t[64:128, :], ldv[1:2, :].to_broadcast([64, 16]), (), [dtl], acc=True)
                ACTF(dtt.t[:], dtl.t[:], AF.Exp, [dtl], [dtt])
                TT(V, th.t[:], li.t[:], dtt.t[:], ALU.mult, [li, dtt], [th])
                TT(V, mag.t[:], lr.t[:], dtt.t[:], ALU.mult, [lr, dtt], [mag])
                ACTF(mag.t[:], mag.t[:], AF.Exp, [mag], [mag])
                sin_of(sn.t[:], th.t[:], None, (tf.t[:], tf), (ti_.t[:], ti_), (tm.t[:], tm), [th], sn)
                sin_of(cs.t[:], th.t[:], None, (tf.t[:], tf), (ti_.t[:], ti_), (tm.t[:], tm), [th], cs, phase=math.pi / 2)
                TT(V, ar.t[:], mag.t[:], cs.t[:], ALU.mult, [mag, cs], [ar])
                TT(V, ai.t[:], mag.t[:], sn.t[:], ALU.mult, [mag, sn], [ai])
                TT(V, den.t[:], lr.t[:], lr.t[:], ALU.mult, [lr], [den])
                TT(V, tA.t[:], li.t[:], li.t[:], ALU.mult, [li], [tA])
                TT(V, den.t[:], den.t[:], tA.t[:], ALU.add, [den, tA], [den])
                kb.op(V, lambda: nc.vector.reciprocal(out=den.t[:], in_=den.t[:]), [den], [den])
                TS(V, tA.t[:], ar.t[:], -1.0, None, ALU.add, None, [ar], [tA])
                TT(V, fr.t[:], tA.t[:], lr.t[:], ALU.mult, [tA, lr], [fr])
                TT(V, tB.t[:], ai.t[:], li.t[:], ALU.mult, [ai, li], [tB])
                TT(V, fr.t[:], fr.t[:], tB.t[:], ALU.add, [fr, tB], [fr])
                TT(V, fr.t[:], fr.t[:], den.t[:], ALU.mult, [fr, den], [fr])
                TT(V, fi.t[:], ai.t[:], lr.t[:], ALU.mult, [ai, lr], [fi])
                TT(V, tB.t[:], tA.t[:], li.t[:], ALU.mult, [tA, li], [tB])
                TT(V, fi.t[:], fi.t[:], tB.t[:], ALU.subtract, [fi, tB], [fi])
                TT(V, fi.t[:], fi.t[:], den.t[:], ALU.mult, [fi, den], [fi])
                cosT = sb(st, "cosT", [128, 16, 512], F32)
                sinT = sb(st, "sinT", [128, 16, 512], F32)
                with contextlib.ExitStack() as st2:
                    ang = sb(st2, "ang", [128, 4, 512], F32)
                    rf = sb(st2, "rf", [128, 4, 512], F32); ri2 = sb(st2, "ri2", [128, 4, 512], I32); rm = sb(st2, "rm", [128, 4, 512], F32)
                    for c in range(4):
                        TT(V, ang.t[:], bc3(th.t[:, 4 * c:4 * c + 4], 512),
                           jidx.t[:].rearrange("p (o j) -> p o j", o=1).to_broadcast([128, 4, 512]), ALU.mult, [th, jidx], [ang])
                        sin_of(sinT.t[:, 4 * c:4 * c + 4, :], ang.t[:], None, (rf.t[:], rf), (ri2.t[:], ri2), (rm.t[:], rm), [ang], sinT)
                        sin_of(cosT.t[:, 4 * c:4 * c + 4, :], ang.t[:], None, (rf.t[:], rf), (ri2.t[:], ri2), (rm.t[:], rm), [ang], cosT, phase=math.pi / 2)
                    kb.barrier()
                pX = [psb(st, f"pX{i}", [128, 2, 512], F32) for i in range(2)]
                pY = psb(st, "pY", [128, 4, 512], F32)
                BT = sb(st, "BT", [128, 2, 4, 128], BF)
                BT3 = sb(st, "BT3", [128, 2, 4, 128], BF)
                CTp = sb(st, "CTp", [128, 16, 2, 128], BF)
                dsk = sb(st, "dsk", [128, 4], F32)
                stb = contextlib.ExitStack()
                br = sb(stb, "br", [128, 16, 16], F32); bi = sb(stb, "bi", [128, 16, 16], F32)
                bsr = sb(stb, "bsr", [128, 16, 16], F32); bsi = sb(stb, "bsi", [128, 16, 16], F32); btmp = sb(stb, "btmp", [128, 16, 16], F32)
                kb.dma('sp', br.t[:], b_re[l].rearrange("(m two) p j -> (two p) m j", two=2), (), [br])
                kb.dma('sp', bi.t[:], b_im[l].rearrange("(m two) p j -> (two p) m j", two=2), (), [bi])
                TT(V, bsr.t[:], br.t[:], bc3(fr.t[:], 16), ALU.mult, [br, fr], [bsr])
                TT(V, btmp.t[:], bi.t[:], bc3(fi.t[:], 16), ALU.mult, [bi, fi], [btmp])
                TT(V, bsr.t[:], bsr.t[:], btmp.t[:], ALU.subtract, [bsr, btmp], [bsr])
                TT(V, bsi.t[:], bi.t[:], bc3(fr.t[:], 16), ALU.mult, [bi, fr], [bsi])
                TT(V, btmp.t[:], br.t[:], bc3(fi.t[:], 16), ALU.mult, [br, fi], [btmp])
                TT(V, bsi.t[:], bsi.t[:], btmp.t[:], ALU.add, [bsi, btmp], [bsi])
                Pcat = sb(stb, "Pcat", [128, 2, 4, 4, 2, 16], F32)
                MEMSET(V, Pcat.t[:], 0.0, [Pcat])
                for ri, bs in enumerate((bsr, bsi)):
                    CP(V, Pcat.t[0:64, ri, :, :, 0, :], bs.t[0:64, :, :].rearrange("p (f m) j -> p f m j", f=4), [bs], [Pcat])
                    CP(V, Pcat.t[64:128, ri, :, :, 1, :], bs.t[64:128, :, :].rearrange("p (f m) j -> p f m j", f=4), [bs], [Pcat])
                for ri in range(2):
                    for ft in range(4):
                        TR(pY.t[:, ft, 0:128], Pcat.t[:, ri, ft].rearrange("p m t j -> p (m t j)"), identf.t[:], [Pcat, identf], [pY])
                    evac(BT.t[:, ri, :, :], pY.t[:, :, 0:128], [pY], [BT])
                MEMSET(V, BT3.t[:], 0.0, [BT3])
                CP(V, BT3.t[96:128, :, :, :], BT.t[96:128, :, :, :], [BT], [BT3])
                Cin = sb(stb, "Cin", [128, 4, 2, 128], F32)
                for ri, cc in enumerate((c_re, c_im)):
                    src = cc[l].rearrange("(f g) k p -> (g k) f p", f=4)
                    kb.dma('sp', Cin.t[:, :, ri, 0:64], src, (), [Cin], acc=True)
                    kb.dma('sp', Cin.t[:, :, ri, 64:128], src, (), [Cin], acc=True)
                TS(V, Cin.t[:, :, 1, :], Cin.t[:, :, 1, :], -1.0, None, ALU.mult, None, [Cin], [Cin])
                CTc = sb(stb, "CTc", [128, 4, 2, 128], F32)
                for ri in range(2):
                    for ft in range(4):
                        TR(pY.t[:, ft, 0:128], Cin.t[:, ft, ri, :], identf.t[:], [Cin, identf], [pY])
                    evac(CTc.t[:, :, ri, :], pY.t[:, :, 0:128], [pY], [CTc])
                MEMSET(V, CTp.t[:], 0.0, [CTp])
                for m in range(16):
                    ft, mm = m // 4, m % 4
                    e = (V, G_)[m % 2]
                    CP(e, CTp.t[0:64, m, :, 32 * mm:32 * mm + 16], CTc.t[0:64, ft, :, 32 * mm:32 * mm + 16], [CTc], [CTp])
                    CP(e, CTp.t[64:128, m, :, 32 * mm + 16:32 * mm + 32], CTc.t[64:128, ft, :, 32 * mm + 16:32 * mm + 32], [CTc], [CTp])
                kb.dma('sp', dsk.t[:], d_skip[l].rearrange("(f p) -> p f", p=128), (), [dsk])
                kb.barrier()
                stb.close()
                initR = sb(st, "initR", [128, 16], F32); initI = sb(st, "initI", [128, 16], F32)
                hlR = sb(st, "hlR", [128, 16], F32); hlI = sb(st, "hlI", [128, 16], F32)
                tcA = sb(st, "tcA", [128, 16], F32)
                G5 = sb(st, "G5", [128, 2, 16], F32)
                ysb = sb(st, "ysb", [128, 4, 512], F32)
                st = contextlib.ExitStack()
                MEMSET(V, initR.t[:], 0.0, [initR]); MEMSET(V, initI.t[:], 0.0, [initI])
                uT = [sb(st, f"suT{i}", [128, 4, 512], BF) for i in range(2)]
                w1 = [sb(st, f"w1{i}", [128, 512], F32) for i in range(2)]
                w2 = [sb(st, f"w2{i}", [128, 512], F32) for i in range(2)]
                w3 = [sb(st, f"w3{i}", [128, 512], F32) for i in range(2)]
                w4 = [sb(st, f"w4{i}", [128, 512], F32) for i in range(2)]
                xr_ = [sb(st, f"xr{i}", [128, 512], F32) for i in range(2)]
                xi_ = [sb(st, f"xi{i}", [128, 512], F32) for i in range(2)]
                gr = [sb(st, f"gr{i}", [128, 512], F32) for i in range(2)]
                gi_ = [sb(st, f"gi{i}", [128, 512], F32) for i in range(2)]
                hr = [sb(st, f"hr{i}", [128, 512], BF) for i in range(2)]
                hi = [sb(st, f"hi{i}", [128, 512], BF) for i in range(2)]
                zT = [sb(st, f"zT{i}", [128, 4, 512], BF) for i in range(2)]
                it = 0
                pend = [None]
                ks5 = os.environ.get("KS5", "main,smp").split(",")
                for ti in range(NTL if "main" in ks5 else 0):
                    ua = uT[ti % 2]
                    t0 = ti * 512
                    kb.dma('sp', ua.t[:], Us.rearrange("(f p) t -> p f t", p=128)[:, :, t0:t0 + 512], (), [ua])
                    for m in range(16):
                        a = it % 2
                        it += 1
                        ft, mm = m // 4, m % 4
                        px = pX[a]
                        for ri in range(2):
                            if mm < 3:
                                MM(px.t[:, ri, :], BT.t[32 * mm:32 * mm + 32, ri, ft, :], ua.t[32 * mm:32 * mm + 32, ft, :], True, True, [BT, ua], [px])
                            else:
                                MM(px.t[:, ri, :], BT3.t[64:128, ri, ft, :], ua.t[64:128, ft, :], True, True, [BT3, ua], [px])
                        if pend[0] is not None:
                            pend[0]()
                            pend[0] = None
                        c_, s_ = cosT.t[:, m, :], sinT.t[:, m, :]
                        TT(V, w1[a].t[:], px.t[:, 0, :], c_, ALU.mult, [px, cosT], [w1[a]])
                        TT(V, w2[a].t[:], px.t[:, 1, :], s_, ALU.mult, [px, sinT], [w2[a]])
                        TT(V, xr_[a].t[:], w1[a].t[:], w2[a].t[:], ALU.add, [w1[a], w2[a]], [xr_[a]])
                        TT(V, w3[a].t[:], px.t[:, 1, :], c_, ALU.mult, [px, cosT], [w3[a]])
                        TT(V, w4[a].t[:], px.t[:, 0, :], s_, ALU.mult, [px, sinT], [w4[a]])
                        TT(V, xi_[a].t[:], w3[a].t[:], w4[a].t[:], ALU.subtract, [w3[a], w4[a]], [xi_[a]])
                        magb = mag.t[:, m:m + 1].to_broadcast([128, 512])
                        kb.op(V, lambda: nc.vector.tensor_tensor_scan(out=gr[a].t[:], data0=magb, data1=xr_[a].t[:], initial=initR.t[:, m:m + 1], op0=ALU.mult, op1=ALU.add), [mag, xr_[a], initR], [gr[a]])
                        kb.op(V, lambda: nc.vector.tensor_tensor_scan(out=gi_[a].t[:], data0=magb, data1=xi_[a].t[:], initial=initI.t[:, m:m + 1], op0=ALU.mult, op1=ALU.add), [mag, xi_[a], initI], [gi_[a]])
                        TT(G_, w1[a].t[:], gr[a].t[:], c_, ALU.mult, [gr[a], cosT], [w1[a]])
                        TT(G_, w2[a].t[:], gi_[a].t[:], s_, ALU.mult, [gi_[a], sinT], [w2[a]])
                        TT(G_, hr[a].t[:], w1[a].t[:], w2[a].t[:], ALU.subtract, [w1[a], w2[a]], [hr[a]])
                        TT(G_, w3[a].t[:], gr[a].t[:], s_, ALU.mult, [gr[a], sinT], [w3[a]])
                        TT(G_, w4[a].t[:], gi_[a].t[:], c_, ALU.mult, [gi_[a], cosT], [w4[a]])
                        TT(G_, hi[a].t[:], w3[a].t[:], w4[a].t[:], ALU.add, [w3[a], w4[a]], [hi[a]])
                        CP(S_, G5.t[:, 0, m:m + 1], gr[a].t[:, 511:512], [gr[a]], [G5])
                        CP(S_, G5.t[:, 1, m:m + 1], gi_[a].t[:, 511:512], [gi_[a]], [G5])
                        def ymm(a=a, m=m, ft=ft, mm=mm):
                            MM(pY.t[:, ft, :], CTp.t[:, m, 0, :], hr[a].t[:], mm == 0, False, [CTp, hr[a]], [pY])
                            MM(pY.t[:, ft, :], CTp.t[:, m, 1, :], hi[a].t[:], False, mm == 3, [CTp, hi[a]], [pY])
                        pend[0] = ymm
                    pend[0]()
                    pend[0] = None
                    c5, s5 = cosT.t[:, :, 511], sinT.t[:, :, 511]
                    c1, s1 = cosT.t[:, :, 1], sinT.t[:, :, 1]
                    TT(V, tcA.t[:], G5.t[:, 1, :], s5, ALU.mult, [G5, sinT], [tcA])
                    TT(V, hlR.t[:], G5.t[:, 0, :], c5, ALU.mult, [G5, cosT], [hlR])
                    TT(V, hlR.t[:], hlR.t[:], tcA.t[:], ALU.subtract, [hlR, tcA], [hlR])
                    TT(V, tcA.t[:], G5.t[:, 1, :], c5, ALU.mult, [G5, cosT], [tcA])
                    TT(V, hlI.t[:], G5.t[:, 0, :], s5, ALU.mult, [G5, sinT], [hlI])
                    TT(V, hlI.t[:], hlI.t[:], tcA.t[:], ALU.add, [hlI, tcA], [hlI])
                    TT(V, tcA.t[:], hlI.t[:], s1, ALU.mult, [hlI, sinT], [tcA])
                    TT(V, initR.t[:], hlR.t[:], c1, ALU.mult, [hlR, cosT], [initR])
                    TT(V, initR.t[:], initR.t[:], tcA.t[:], ALU.subtract, [initR, tcA], [initR])
                    TT(V, tcA.t[:], hlI.t[:], c1, ALU.mult, [hlI, cosT], [tcA])
                    TT(V, initI.t[:], hlR.t[:], s1, ALU.mult, [hlR, sinT], [initI])
                    TT(V, initI.t[:], initI.t[:], tcA.t[:], ALU.add, [initI, tcA], [initI])
                    za = zT[ti % 2]
                    for ft in range(4):
                        STT(ysb.t[:, ft, :], ua.t[:, ft, :], dsk.t[:, ft:ft + 1], pY.t[:, ft, :], ALU.mult, ALU.add, [ua, dsk, pY], [ysb])
                    ACTF(za.t[:], ysb.t[:], AF.Gelu, [ysb], [za])
                    kb.dma('sp', Zs.rearrange("(f p) t -> p f t", p=128)[:, :, t0:t0 + 512], za.t[:], [za], ())
                kb.dma('sp', o_pre[l].rearrange("(m r) -> r m", r=128), hlR.t[:], [hlR], ())
                kb.dma('sp', o_pim[l].rearrange("(m r) -> r m", r=128), hlI.t[:], [hlI], ())
                kb.barrier()
                st.close()
                st = contextlib.ExitStack()
                if "smp" not in ks5:
                    st.close()
                    kb.barrier()
                    return
                stok = sb(st, "stok", [NS, 2, 2048], F32)
                kb.dma('sp', stok.t[:, 0, :], c_sre[l], (), [stok], acc=True)
                kb.dma('sp', stok.t[:, 1, :], c_sim[l], (), [stok], acc=True)
                px = pX[0]
                for ri in range(2):
                    for m in range(16):
                        TR(px.t[:, ri, m * 16:(m + 1) * 16], stok.t[:, ri, m * 128:(m + 1) * 128], identf.t[0:NS, 0:NS], [stok, identf], [px])
                prv = sb(st, "prv", [128, 2, 256], F32)
                evac(prv.t[:], px.t[:, :, 0:256], [px], [prv])
                for ri in range(2):
                    for m in range(16):
                        ft, mm = m // 4, m % 4
                        oc_ = (ri * 4 + ft) * 16
                        if mm < 3:
                            MM(pY.t[:, mm, oc_:oc_ + 16], BT.t[32 * mm:32 * mm + 32, ri, ft, :], uT_s.t[32 * mm:32 * mm + 32, ft, :], True, True, [BT, uT_s], [pY])
                        else:
                            MM(pY.t[:, mm, oc_:oc_ + 16], BT3.t[64:128, ri, ft, :], uT_s.t[64:128, ft, :], True, True, [BT3, uT_s], [pY])
                vA = lambda ap: ap.rearrange("p (f m b) -> p m f b", f=4, m=4, b=16)
                vB = lambda ri: pY.t[:, :, ri * 64:(ri + 1) * 64].rearrange("p m (f b) -> p m f b", f=4)
                hs = sb(st, "hs", [128, 2, 256], F32)
                hsb = sb(st, "hsb", [128, 2, 256], BF)
                q1 = sb(st, "q1", [128, 256], F32); q2 = sb(st, "q2", [128, 256], F32)
                arB, aiB = bc3(ar.t[:], 16), bc3(ai.t[:], 16)
                v3 = lambda ap: ap.rearrange("p (m b) -> p m b", b=16)
                TT(V, v3(q1.t[:]), v3(prv.t[:, 0, :]), arB, ALU.mult, [prv, ar], [q1])
                TT(V, v3(q2.t[:]), v3(prv.t[:, 1, :]), aiB, ALU.mult, [prv, ai], [q2])
                TT(V, q1.t[:], q1.t[:], q2.t[:], ALU.subtract, [q1, q2], [q1])
                TT(V, vA(hs.t[:, 0, :]), vA(q1.t[:]), vB(0), ALU.add, [q1, pY], [hs])
                TT(V, v3(q1.t[:]), v3(prv.t[:, 1, :]), arB, ALU.mult, [prv, ar], [q1])
                TT(V, v3(q2.t[:]), v3(prv.t[:, 0, :]), aiB, ALU.mult, [prv, ai], [q2])
                TT(V, q1.t[:], q1.t[:], q2.t[:], ALU.add, [q1, q2], [q1])
                TT(V, vA(hs.t[:, 1, :]), vA(q1.t[:]), vB(1), ALU.add, [q1, pY], [hs])
                CP(V, hsb.t[:], hs.t[:], [hs], [hsb])
                for ft in range(4):
                    for mm in range(4):
                        m = ft * 4 + mm
                        for ri in range(2):
                            MM(pY.t[:, ft, 0:16], CTp.t[:, m, ri, :], hsb.t[:, ri, m * 16:(m + 1) * 16], mm == 0 and ri == 0, mm == 3 and ri == 1, [CTp, hsb], [pY])
                zs = sb(st, "zs", [128, 4, NS], BF)
                for ft in range(4):
                    STT(ysb.t[:, ft, 0:16], uT_s.t[:, ft, :], dsk.t[:, ft:ft + 1], pY.t[:, ft, 0:16], ALU.mult, ALU.add, [uT_s, dsk, pY], [ysb])
                ACTF(zs.t[:], ysb.t[:, :, 0:16], AF.Gelu, [ysb], [zs])
                kb.dma('sp', Zs.rearrange("(f p) t -> p f t", p=128)[:, :, T:T + NS], zs.t[:], [zs], ())
                sout = sb(st, "sout", [NS, 2, 2048], F32)
                for ri in range(2):
                    for m in range(16):
                        TR(pY.t[0:NS, (m // 4), (m % 4) * 128:(m % 4) * 128 + 128], hs.t[:, ri, m * 16:(m + 1) * 16], identf.t[:], [hs, identf], [pY])
                    evac(sout.t[:, ri, :], pY.t[0:NS, :, :].rearrange("p f t -> p (f t)"), [pY], [sout])
                kb.dma('sp', o_sre[l], sout.t[:, 0, :], [sout], ())
                kb.dma('sp', o_sim[l], sout.t[:, 1, :], [sout], ())
                kb.barrier()
                st.close()
            kb.barrier()

        def stage_matt(l):
            with contextlib.ExitStack() as st:
                tls = [attn_tiles(st, "m0"), attn_tiles(st, "m1")]
                tl = tls[0]
                ucnt = [0]
                qt = [sb(st, f"mq{i}", [128, 4, 512], BF) for i in range(2)]
                oc = [sb(st, f"moc{i}", [128, 4, 512], BF) for i in range(2)]
                sc = 128.0 ** -0.5
                units = []
                for ti in range(NTL):
                    a = ti % 2
                    t0 = ti * 512

                    def pre(a=a, t0=t0):
                        kb.dma('sp', qt[a].t[:], XQs.rearrange("(f p) t -> p f t", p=128)[:, :, t0:t0 + 512], (), [qt[a]])

                    def post(a=a, t0=t0):
                        kb.dma('sp', OCs.rearrange("(f p) t -> p f t", p=128)[:, :, t0:t0 + 512], oc[a].t[:], [oc[a]], ())
                    for j in range(4):
                        ucnt[0] += 1
                        g = attn_unit_g(
                            tls[ucnt[0] % 2], 128, 4, 256,
                            lambda h, a=a, j=j: (qt[a].t[:, h, j * 128:(j + 1) * 128], [qt[a]]),
                            lambda h: (mkT.t[:, h, :], [mkT]),
                            lambda h, kt, kn: (mvT.t[:, kt, 128 * h:128 * h + 128], [mvT]),
                            128, sc, None, None,
                            lambda pO, a=a, j=j: CP(S_, oc[a].t[:, :, j * 128:(j + 1) * 128], pO.t[:, :, :], [pO], [oc[a]]))
                        units.append((pre if j == 0 else None, g, post if j == 3 else None))
                run_pipelined(units)
                mk = [sb(st, f"smk{i}", [128, 2, 512], F32) for i in range(2)]
                mv = [sb(st, f"smv{i}", [128, 2, 512], F32) for i in range(2)]
                mvb = [sb(st, f"smvb{i}", [128, 2, 512], BF) for i in range(2)]
                mkTs = [sb(st, f"smkT{i}", [128, 4, 256], BF) for i in range(2)]
                ptr = tls[1][0]
                ocs = sb(st, "ocs", [128, 4, NS], BF)
                for b in range(NS):
                    a = b % 2
                    kb.dma('sp', mk[a].t[:], c_mk[l, b].rearrange("(j p) c -> p j c", p=128), (), [mk[a]])
                    kb.dma('sp', mv[a].t[:], c_mv[l, b].rearrange("(j p) c -> p j c", p=128), (), [mv[a]])
                    for h in range(4):
                        for j in range(2):
                            TR(ptr.t[:, h, j * 128:(j + 1) * 128], mk[a].t[:, j, h * 128:(h + 1) * 128], identf.t[:], [mk[a], identf], [ptr])
                    evac(mkTs[a].t[:], ptr.t[:], [ptr], [mkTs[a]])
                    CP(G_, mvb[a].t[:], mv[a].t[:], [mv[a]], [mvb[a]])
                    attn_unit(
                        tl, 1, 4, 256,
                        lambda h: (xqT_s.t[:, h, b:b + 1], [xqT_s]),
                        lambda h: (mkTs[a].t[:, h, :], [mkTs[a]]),
                        lambda h, kt, kn: (mvb[a].t[:, kt, 128 * h:128 * h + 128], [mvb[a]]),
                        128, sc, None, None,
                        lambda pO: evac(ocs.t[:, :, b:b + 1], pO.t[:, :, 0:1], [pO], [ocs]))
                kb.dma('sp', OCs.rearrange("(f p) t -> p f t", p=128)[:, :, T:T + NS], ocs.t[:], [ocs], ())
            kb.barrier()

        def stage_mrg(l):
            with contextlib.ExitStack() as st:
                wa = sb(st, "wa", [64, 8, D], BF)
                for h in range(8):
                    kb.dma('pool', wa.t[:, h, :], w_bra[l, h * 64:(h + 1) * 64, :], (), [wa], acc=True)
                wga = sb(st, "wga", [128, 4, D], BF); wgb = sb(st, "wgb", [128, 4, D], BF)
                wc = sb(st, "wc", [128, 4, D], BF); wo = sb(st, "wo", [128, 8, D], BF)
                load_w_begin(st)
                load_w(wga, lambda kt: w_ga[l, kt * 128:(kt + 1) * 128, :], 4, D)
                load_w(wgb, lambda kt: w_gb[l, kt * 128:(kt + 1) * 128, :], 4, D)
                load_w(wc, lambda kt: w_brm[l, kt * 128:(kt + 1) * 128, :], 4, D)
                load_w(wo, lambda kt: w_out[l, kt * 128:(kt + 1) * 128, :], 8, D)
                load_w_end()
                oa = [sb(st, f"goa{i}", [64, 8, 512], BF) for i in range(2)]
                zt = [sb(st, f"gz{i}", [128, 4, 512], BF) for i in range(2)]
                oc = [sb(st, f"goc{i}", [128, 4, 512], BF) for i in range(2)]
                sg = [sb(st, "gsg0", [128, 24, 512], BF)] * 2
                xt = [sb(st, "gx0", [128, 8, 512], F32)] * 2
                mg = sb(st, "mg", [128, 8, 512], BF)
                m1s = [sb(st, f"m1{i}", [128, 512], F32) for i in range(2)]
                m2s = [sb(st, f"m2{i}", [128, 512], F32) for i in range(2)]
                m3s = [sb(st, f"m3{i}", [128, 512], F32) for i in range(2)]
                pa = [psb(st, f"gpa{i}", [128, 512], F32) for i in range(8)]
                XTv = XT.rearrange("(k p) t -> p k t", p=128)
                pi = 0
                for ti, (t0, n) in enumerate(tiles):
                    a = ti % 2
                    kb.dma('sp', oa[a].t[:, :, 0:n], OAs.rearrange("(h p) t -> p h t", p=64)[:, :, t0:t0 + n], (), [oa[a]])
                    kb.dma('sp', zt[a].t[:, :, 0:n], Zs.rearrange("(f p) t -> p f t", p=128)[:, :, t0:t0 + n], (), [zt[a]])
                    kb.dma('sp', oc[a].t[:, :, 0:n], OCs.rearrange("(f p) t -> p f t", p=128)[:, :, t0:t0 + n], (), [oc[a]])
                    kb.dma('sp', sg[a].t[:, :, 0:n], SGs.rearrange("(f p) t -> p f t", p=128)[:, :, t0:t0 + n], (), [sg[a]])
                    kb.dma('sp', xt[a].t[:, :, 0:n], XTv[:, :, t0:t0 + n], (), [xt[a]])
                    for dm in range(8):
                        cs_ = slice(dm * 128, (dm + 1) * 128)
                        m1, m2, m3 = m1s[dm % 2], m2s[dm % 2], m3s[dm % 2]
                        pA, pGa, pGb, pC = pa[pi % 8], pa[(pi + 1) % 8], pa[(pi + 2) % 8], pa[(pi + 3) % 8]
                        pi += 4
                        for h in range(8):
                            MM(pA.t[:, 0:n], wa.t[:, h, cs_], oa[a].t[:, h, 0:n], h == 0, h == 7, [wa, oa[a]], [pA])
                        for k in range(4):
                            MM(pGa.t[:, 0:n], wga.t[:, k, cs_], zt[a].t[:, k, 0:n], k == 0, k == 3, [wga, zt[a]], [pGa])
                        for k in range(4):
                            MM(pGb.t[:, 0:n], wgb.t[:, k, cs_], zt[a].t[:, k, 0:n], k == 0, k == 3, [wgb, zt[a]], [pGb])
                        for k in range(4):
                            MM(pC.t[:, 0:n], wc.t[:, k, cs_], oc[a].t[:, k, 0:n], k == 0, k == 3, [wc, oc[a]], [pC])
                        TT(V, m1.t[:, 0:n], pA.t[:, 0:n], sg[a].t[:, dm, 0:n], ALU.mult, [pA, sg[a]], [m1])
                        ACTF(m2.t[:, 0:n], pGb.t[:, 0:n], AF.Sigmoid, [pGb], [m2])
                        TT(V, m2.t[:, 0:n], pGa.t[:, 0:n], m2.t[:, 0:n], ALU.mult, [pGa, m2], [m2])
                        TT(G_, m2.t[:, 0:n], m2.t[:, 0:n], sg[a].t[:, 8 + dm, 0:n], ALU.mult, [m2, sg[a]], [m2])
                        TT(V, m3.t[:, 0:n], pC.t[:, 0:n], sg[a].t[:, 16 + dm, 0:n], ALU.mult, [pC, sg[a]], [m3])
                        TT(G_, m1.t[:, 0:n], m1.t[:, 0:n], m2.t[:, 0:n], ALU.add, [m1, m2], [m1])
                        TT(G_, mg.t[:, dm, 0:n], m1.t[:, 0:n], m3.t[:, 0:n], ALU.add, [m1, m3], [mg])
                    for dm in range(8):
                        pO = pa[pi % 8]
                        pi += 1
                        for k in range(8):
                            MM(pO.t[:, 0:n], wo.t[:, k, dm * 128:(dm + 1) * 128], mg.t[:, k, 0:n], k == 0, k == 7, [wo, mg], [pO])
                        TT(V, xt[a].t[:, dm, 0:n], xt[a].t[:, dm, 0:n], pO.t[:, 0:n], ALU.add, [xt[a], pO], [xt[a]])
                    kb.dma('sp', XTv[:, :, t0:t0 + n], xt[a].t[:, :, 0:n], [xt[a]], ())
            kb.barrier()

        def stage_f1(l, half):
            HF = 11
            F0 = half * HF
            C0 = F0 * 128
            CW = HF * 128
            with contextlib.ExitStack() as st:
                wg = sb(st, "wg", [128, 8, CW], BF); wu = sb(st, "wu", [128, 8, CW], BF)
                load_w_begin(st)
                load_w(wg, lambda kt: w_fg[l, kt * 128:(kt + 1) * 128, C0:C0 + CW], 8, CW)
                load_w(wu, lambda kt: w_fu[l, kt * 128:(kt + 1) * 128, C0:C0 + CW], 8, CW)
                load_w_end()
                gf = sb(st, "gffn", [128, 8], F32)
                kb.dma('sp', gf.t[:], g_ffn[l].rearrange("(k p) -> p k", p=128), (), [gf])
                cw = sb(st, "cw", [128, 3, HF], F32); cb = sb(st, "cb", [128, HF], F32)
                for j3 in range(3):
                    kb.dma('sp', cw.t[:, j3, :], conv_w[l, j3, C0:C0 + CW].rearrange("(f p) -> p f", p=128), (), [cw], acc=True)
                kb.dma('sp', cb.t[:], conv_b[l, C0:C0 + CW].rearrange("(f p) -> p f", p=128), (), [cb])
                xt = [sb(st, "fx0", [128, 8, 512], F32)] * 2
                hT = [sb(st, f"fh{i}", [128, 8, 512], BF) for i in range(2)]
                sq = sb(st, "fsq", [128, 8, 512], BF); rt = sb(st, "frt", [128, 512], F32)
                pss = psb(st, "fpss", [128, 512], F32)
                pg = [psb(st, f"fpg{i}", [128, 512], F32) for i in range(3)]
                pu = [psb(st, f"fpu{i}", [128, 512], F32) for i in range(3)]
                apad = sb(st, "apad", [128, HF, 514], F32)
                MEMSET(V, apad.t[:, :, 0:2], 0.0, [apad])
                apb = [Buf(apad.t) for _ in range(HF)]
                for b_ in apb:
                    b_.w = dict(apad.w)
                cvs = [sb(st, f"cv{i}", [128, 512], F32) for i in range(2)]
                cgs = [sb(st, f"cg{i}", [128, 512], F32) for i in range(2)]
                gout = [sb(st, f"gout{i}", [128, HF, 512], BF) for i in range(2)]
                sprev = sb(st, "sprev", [128, 2, HF, NS], F32)
                asmp = sb(st, "asmp", [128, HF, NS], F32)
                stc = contextlib.ExitStack()
                cvt = sb(stc, "cvt", [NS, 2, CW], F32)
                kb.dma('sp', cvt.t[:], c_cv[l, :, :, C0:C0 + CW], (), [cvt])
                if half == 0:
                    kb.dma('sp', o_scv[l, :, 0, :], c_cv[l, :, 1, :], (), ())
                for j in range(2):
                    for f in range(HF):
                        TR(pg[j].t[:, f * 16:(f + 1) * 16], cvt.t[:, j, f * 128:(f + 1) * 128], identf.t[0:NS, 0:NS], [cvt, identf], [pg[j]])
                    evac(sprev.t[:, j, :, :], pg[j].t[:, 0:HF * 16].rearrange("p (f b) -> p f b", b=NS), [pg[j]], [sprev])
                kb.barrier()
                stc.close()
                XTv = XT.rearrange("(k p) t -> p k t", p=128)
                pi = 0
                def prol(ti):
                    t0_, n_ = tiles[ti]
                    kb.dma('sp', xt[ti % 2].t[:, :, 0:n_], XTv[:, :, t0_:t0_ + n_], (), [xt[ti % 2]])
                    rmsnorm_tile(xt[ti % 2], n_, gf, hT[ti % 2], sq, rt, pss)
                prol(0)
                for ti, (t0, n) in enumerate(tiles):
                    a = ti % 2
                    smp = n == NS
                    if ti + 1 < len(tiles):
                        prol(ti + 1)
                    go = gout[a]
                    for f in range(HF):
                        p1, p2 = pg[pi % 3], pu[pi % 3]
                        pi += 1
                        for k in range(8):
                            MM(p1.t[:, 0:n], wg.t[:, k, f * 128:(f + 1) * 128], hT[a].t[:, k, 0:n], k == 0, k == 7, [wg, hT[a]], [p1])
                        for k in range(8):
                            MM(p2.t[:, 0:n], wu.t[:, k, f * 128:(f + 1) * 128], hT[a].t[:, k, 0:n], k == 0, k == 7, [wu, hT[a]], [p2])
                        w0, w1_, w2_ = cw.t[:, 0, f:f + 1], cw.t[:, 1, f:f + 1], cw.t[:, 2, f:f + 1]
                        cv, cg = cvs[f % 2], cgs[f % 2]
                        ab = apb[f]
                        if not smp:
                            CP(S_, apad.t[:, f, 2:514], p1.t[:, :], [p1], [ab])
                            TS(V, cv.t[:], apad.t[:, f, 0:512], w0, cb.t[:, f:f + 1], ALU.mult, ALU.add, [ab, cw, cb], [cv])
                            STT(cv.t[:], apad.t[:, f, 1:513], w1_, cv.t[:], ALU.mult, ALU.add, [ab, cw, cv], [cv])
                            STT(cv.t[:], apad.t[:, f, 2:514], w2_, cv.t[:], ALU.mult, ALU.add, [ab, cw, cv], [cv])
                            ACTF(cg.t[:], cv.t[:], AF.Gelu, [cv], [cg])
                            TT(V, go.t[:, f, :], cg.t[:], p2.t[:, :], ALU.mult, [cg, p2], [go])
                            CP(G_, apad.t[:, f, 0:2], apad.t[:, f, 512:514], [ab], [ab])
                        else:
                            CP(S_, asmp.t[:, f, :], p1.t[:, 0:n], [p1], [asmp])
                            TS(V, cv.t[:, 0:n], sprev.t[:, 0, f, :], w0, cb.t[:, f:f + 1], ALU.mult, ALU.add, [sprev, cw, cb], [cv])
                            STT(cv.t[:, 0:n], sprev.t[:, 1, f, :], w1_, cv.t[:, 0:n], ALU.mult, ALU.add, [sprev, cw, cv], [cv])
                            STT(cv.t[:, 0:n], asmp.t[:, f, :], w2_, cv.t[:, 0:n], ALU.mult, ALU.add, [asmp, cw, cv], [cv])
                            ACTF(cg.t[:, 0:n], cv.t[:, 0:n], AF.Gelu, [cv], [cg])
                            TT(V, go.t[:, f, 0:n], cg.t[:, 0:n], p2.t[:, 0:n], ALU.mult, [cg, p2], [go])
                    kb.dma('sp', Gs.rearrange("(f p) t -> p f t", p=128)[:, F0:F0 + HF, t0:t0 + n], go.t[:, :, 0:n], [go], ())
                pbs = (pg[0], pg[1], pg[2])
                pcs = sb(st, "pcs", [2, CW], F32)
                for f in range(HF):
                    TR(pbs[f // 4].t[0:2, (f % 4) * 128:(f % 4) * 128 + 128], apad.t[:, f, 0:2], identf.t[:], [apb[f], identf], list(pbs))
                for gi, pb in enumerate(pbs):
                    w_ = 512 if gi < 2 else CW - 1024
                    evac(pcs.t[:, gi * 512:gi * 512 + w_], pb.t[0:2, 0:w_], [pb], [pcs])
                kb.dma('sp', o_pcv[l, :, C0:C0 + CW], pcs.t[:], [pcs], ())
                scs = sb(st, "scs", [NS, CW], F32)
                pbu = (pu[0], pu[1], pu[2])
                for f in range(HF):
                    TR(pbu[f // 4].t[0:NS, (f % 4) * 128:(f % 4) * 128 + 128], asmp.t[:, f, :], identf.t[:], [asmp, identf], list(pbu))
                for gi, pb in enumerate(pbu):
                    w_ = 512 if gi < 2 else CW - 1024
                    evac(scs.t[:, gi * 512:gi * 512 + w_], pb.t[0:NS, 0:w_], [pb], [scs])
                kb.dma('sp', o_scv[l, :, 1, C0:C0 + CW], scs.t[:], [scs], ())
            kb.barrier()

        def stage_f2(l):
            with contextlib.ExitStack() as st:
                wd = sb(st, "wd", [128, NFT, D], BF)
                load_w_begin(st)
                load_w(wd, lambda kt: w_fd[l, kt * 128:(kt + 1) * 128, :], NFT, D)
                load_w_end()
                gt = [sb(st, f"dg{i}", [128, NFT, 512], BF) for i in range(2)]
                xt = [sb(st, f"dx{i}", [128, 8, 512], F32) for i in range(2)]
                pd = [psb(st, f"dpd{i}", [128, 512], F32) for i in range(4)]
                XTv = XT.rearrange("(k p) t -> p k t", p=128)
                pi = 0
                def lod(ti):
                    t0_, n_ = tiles[ti]
                    kb.dma('sp', gt[ti % 2].t[:, :, 0:n_], Gs.rearrange("(f p) t -> p f t", p=128)[:, :, t0_:t0_ + n_], (), [gt[ti % 2]])
                    kb.dma('sp', xt[ti % 2].t[:, :, 0:n_], XTv[:, :, t0_:t0_ + n_], (), [xt[ti % 2]])
                lod(0)
                for ti, (t0, n) in enumerate(tiles):
                    a = ti % 2
                    if ti + 1 < len(tiles):
                        lod(ti + 1)
                    for dm in range(8):
                        p = pd[pi % 4]
                        pi += 1
                        for k in range(NFT):
                            MM(p.t[:, 0:n], wd.t[:, k, dm * 128:(dm + 1) * 128], gt[a].t[:, k, 0:n], k == 0, k == NFT - 1, [wd, gt[a]], [p])
                        TT(V, xt[a].t[:, dm, 0:n], xt[a].t[:, dm, 0:n], p.t[:, 0:n], ALU.add, [xt[a], p], [xt[a]])
                    kb.dma('sp', XTv[:, :, t0:t0 + n], xt[a].t[:, :, 0:n], [xt[a]], ())
            kb.barrier()

        def stage_fin():
            with contextlib.ExitStack() as st:
                gfn = sb(st, "gfin", [128, 8], F32)
                kb.dma('sp', gfn.t[:], g_fin[0].rearrange("(k p) -> p k", p=128), (), [gfn])
                xt = [sb(st, f"nx{i}", [128, 8, 512], F32) for i in range(2)]
                yo = [sb(st, f"ny{i}", [128, 8, 512], F32) for i in range(2)]
                sq = sb(st, "nsq", [128, 8, 512], BF); rt = sb(st, "nrt", [128, 512], F32)
                pss = psb(st, "npss", [128, 512], F32)
                pt = [psb(st, f"npt{i}", [128, 1024], F32) for i in range(2)]
                yt = [sb(st, f"nyt{i}", [128, D], F32) for i in range(2)]
                XTv = XT.rearrange("(k p) t -> p k t", p=128)
                bi = 0
                for ti, (t0, n) in enumerate(tiles):
                    a = ti % 2
                    kb.dma('sp', xt[a].t[:, :, 0:n], XTv[:, :, t0:t0 + n], (), [xt[a]])
                    rmsnorm_tile(xt[a], n, gfn, yo[a], sq, rt, pss)
                    for j in range((n + 127) // 128):
                        m = min(128, n - j * 128)
                        b2 = bi % 2
                        bi += 1
                        for k in range(8):
                            TR(pt[b2].t[0:m, k * 128:(k + 1) * 128], yo[a].t[:, k, j * 128:j * 128 + m], identf.t[:], [yo[a], identf], [pt[b2]])
                        evac(yt[b2].t[0:m, :], pt[b2].t[0:m, :], [pt[b2]], [yt[b2]])
                        if n == NS:
                            kb.dma('sp', o_ys[:, :], yt[b2].t[0:m, :], [yt[b2]], ())
                        else:
                            kb.dma('sp', o_yp[t0 + j * 128:t0 + j * 128 + 128, :], yt[b2].t[:], [yt[b2]], ())
            kb.barrier()

        import os
        dbg = os.environ.get("KSTAGES", "mem,in,att,s5,matt,mrg,f1,f2").split(",")
        nl = int(os.environ.get("KLAYERS", str(DEPTH)))
        stage_p0()
        for l in range(nl):
            if "mem" in dbg: stage_mem(l)
            if "in" in dbg: stage_in(l)
            if "att" in dbg: stage_att(l)
            if "s5" in dbg: stage_s5(l)
            if "matt" in dbg: stage_matt(l)
            if "mrg" in dbg: stage_mrg(l)
            if "f1" in dbg:
                stage_f1(l, 0)
                stage_f1(l, 1)
            if "f2" in dbg: stage_f2(l)
        stage_fin()
        kb.barrier()
    return nc


T_FULL = 4096
_cache = {}


def _in_maps(inp, T):
    maps = []
    for c in range(8):
        b = c % 4
        sl = slice(NS * c, NS * (c + 1))
        m = {
            "xp": np.ascontiguousarray(inp["x_prompt"][b, :T]),
            "xs": np.ascontiguousarray(inp["x_sample"][sl, 0]),
            "mem": np.ascontiguousarray(inp["mem_prompt"][b]),
            "c_swk": np.ascontiguousarray(inp["cache_swa_k"][:, sl].reshape(DEPTH, NS, 128, 128)),
            "c_swv": np.ascontiguousarray(inp["cache_swa_v"][:, sl].reshape(DEPTH, NS, 128, 128)),
            "c_sre": np.ascontiguousarray(inp["state_ssm_re"][:, sl].reshape(DEPTH, NS, 2048)),
            "c_sim": np.ascontiguousarray(inp["state_ssm_im"][:, sl].reshape(DEPTH, NS, 2048)),
            "c_cv": np.ascontiguousarray(inp["cache_ffn_conv"][:, sl]),
            "c_mk": np.ascontiguousarray(inp["cache_mem_k"][:, sl].reshape(DEPTH, NS, 256, 512)),
            "c_mv": np.ascontiguousarray(inp["cache_mem_v"][:, sl].reshape(DEPTH, NS, 256, 512)),
            "norm_final_g": np.ascontiguousarray(inp["norm_final_g"].reshape(1, D)),
        }
        for k in ("norm_mix_g", "norm_ffn_g", "norm_mem_g", "w_in", "sinks", "w_br_attn", "lam_re", "lam_im", "log_dt",
                  "b_re", "b_im", "c_re", "c_im", "d_skip", "w_glu_a", "w_glu_b", "w_mem_kv", "w_br_mem", "w_out",
                  "w_ffn_gate", "w_ffn_up", "conv_w", "conv_b", "w_ffn_down"):
            m[k] = np.ascontiguousarray(inp[k])
        maps.append(m)
    return maps


def run(inp, T):
    if T not in _cache:
        _cache[T] = build(T)
    nc = _cache[T]
    inp = {k: np.asarray(v, dtype=np.float32) for k, v in inp.items()}
    import os
    ncores = int(os.environ.get("KCORES", "8"))
    res = run_bass_kernel_spmd(nc, _in_maps(inp, T)[:ncores], core_ids=list(range(ncores))).results
    res = [res[c % ncores] for c in range(8)]
    P = lambda name: np.stack([res[b][name] for b in range(4)], axis=1)
    Sm = lambda name: np.concatenate([res[c][name] for c in range(8)], axis=1)
    y_p = np.stack([res[b]["o_yp"] for b in range(4)], axis=0)
    y_s = np.concatenate([res[c]["o_ys"] for c in range(8)], axis=0)[:, None, :]
    return (y_p, y_s,
            P("o_pk").reshape(DEPTH, 4, 128, 2, 64), P("o_pv").reshape(DEPTH, 4, 128, 2, 64),
            P("o_pre").reshape(DEPTH, 4, 32, 64), P("o_pim").reshape(DEPTH, 4, 32, 64),
            P("o_pcv"), P("o_pmk").reshape(DEPTH, 4, 256, 4, 128), P("o_pmv").reshape(DEPTH, 4, 256, 4, 128),
            Sm("o_sk").reshape(DEPTH, 128, 128, 2, 64), Sm("o_sv").reshape(DEPTH, 128, 128, 2, 64),
            Sm("o_sre").reshape(DEPTH, 128, 32, 64), Sm("o_sim").reshape(DEPTH, 128, 32, 64), Sm("o_scv"))


def kernel(**inputs):
    return run(inputs, T_FULL)
```
